# Optimizing a Trainium2 kernel written in Bass

```python
import math
import jax, jax.numpy as jnp
from jax import lax
import numpy as np

D_MODEL = 2048
BATCH = 1
SEQ = 16384
DEPTH = 1

N_HEADS = D_MODEL // 128
N_KV_HEADS = 4
HEAD_DIM = 64
GROUP = N_HEADS // N_KV_HEADS
WINDOW = 128
BLOCK = 128
ROPE_THETA = 10000.0
CONV_CH = D_MODEL // 4
CONV_WIDTH = 31
MEM_TOKENS = 256
MEM_HEADS = 4
MEM_HEAD_DIM = D_MODEL // (4 * MEM_HEADS)
ATTN_Q_W = N_HEADS * HEAD_DIM
KV_W = N_KV_HEADS * HEAD_DIM
MEM_W = MEM_HEADS * MEM_HEAD_DIM
N_BRANCH = 3
IN_W = ATTN_Q_W + 2 * KV_W + 2 * CONV_CH + MEM_W + N_BRANCH * D_MODEL
PEER_HEADS = 8
N_KEYS = 128
N_EXPERTS = N_KEYS * N_KEYS
PEER_TOPK = 16
PEER_QDIM = 256
PEER_HALF = PEER_QDIM // 2
PEER_CHUNK = 128
EPS = 1e-6
NEG = -1e30

kernel_name = "hybrid_swa_conformer_mem_peer_block"


def rms_norm(x, g):
    xf = x.astype(jnp.float32)
    y = xf * lax.rsqrt(jnp.mean(xf * xf, axis=-1, keepdims=True) + EPS)
    return (y * g.astype(jnp.float32)).astype(x.dtype)


def layer_norm(x, g, b):
    xf = x.astype(jnp.float32)
    mu = jnp.mean(xf, axis=-1, keepdims=True)
    xc = xf - mu
    y = xc * lax.rsqrt(jnp.mean(xc * xc, axis=-1, keepdims=True) + EPS)
    return (y * g.astype(jnp.float32) + b.astype(jnp.float32)).astype(x.dtype)


def rotary(x, cos, sin):
    xf = x.astype(jnp.float32)
    x1, x2 = jnp.split(xf, 2, axis=-1)
    return jnp.concatenate([x1 * cos - x2 * sin, x2 * cos + x1 * sin], axis=-1).astype(x.dtype)


def sliding_window_attention(q, k, v, sinks):
    B, S = q.shape[0], q.shape[1]
    nb = S // BLOCK
    q = q.reshape(B, nb, BLOCK, N_KV_HEADS, GROUP, HEAD_DIM)
    k = k.reshape(B, nb, BLOCK, N_KV_HEADS, HEAD_DIM)
    v = v.reshape(B, nb, BLOCK, N_KV_HEADS, HEAD_DIM)

    def with_prev(t):
        prev = jnp.concatenate([jnp.zeros_like(t[:, :1]), t[:, :-1]], axis=1)
        return jnp.concatenate([prev, t], axis=2)

    kb, vb = with_prev(k), with_prev(v)
    s = jnp.einsum('bnqhgd,bnkhd->bnhgqk', q, kb).astype(jnp.float32) * (HEAD_DIM ** -0.5)
    qi = jnp.arange(BLOCK)[:, None]
    kj = jnp.arange(2 * BLOCK)[None, :]
    diff = qi + BLOCK - kj
    band = (diff >= 0) & (diff < WINDOW)
    blk = jnp.arange(nb)[:, None, None]
    mask = band[None] & ((blk > 0) | (kj >= BLOCK)[None])
    s = jnp.where(mask[None, :, None, None], s, NEG)
    sink = jnp.broadcast_to(sinks.astype(jnp.float32).reshape(N_KV_HEADS, GROUP)[None, None, :, :, None, None],
                            s.shape[:-1] + (1,))
    p = jax.nn.softmax(jnp.concatenate([s, sink], axis=-1), axis=-1)[..., :-1].astype(v.dtype)
    o = jnp.einsum('bnhgqk,bnkhd->bnqhgd', p, vb)
    return o.reshape(B, S, N_HEADS * HEAD_DIM)


def conformer_conv(u, dw_w, dw_b, ln_g, ln_b, w_o):
    a, b = jnp.split(u, 2, axis=-1)
    glu = a * jax.nn.sigmoid(b)
    c = lax.conv_general_dilated(glu, dw_w, window_strides=(1,), padding=((CONV_WIDTH - 1, 0),),
                                 dimension_numbers=('NWC', 'WIO', 'NWC'),
                                 feature_group_count=CONV_CH) + dw_b
    c = layer_norm(c, ln_g, ln_b)
    return jax.nn.silu(c) @ w_o


def memory_attention(mq, mem, g_mem, w_mem_kv, mq_g, mk_g, w_o):
    B, S = mq.shape[0], mq.shape[1]
    mq = rms_norm(mq.reshape(B, S, MEM_HEADS, MEM_HEAD_DIM), mq_g)
    kv = rms_norm(mem, g_mem) @ w_mem_kv
    mk, mv = jnp.split(kv, 2, axis=-1)
    mk = rms_norm(mk.reshape(B, -1, MEM_HEADS, MEM_HEAD_DIM), mk_g)
    mv = mv.reshape(B, -1, MEM_HEADS, MEM_HEAD_DIM)
    s = jnp.einsum('bshd,bmhd->bhsm', mq, mk).astype(jnp.float32) * (MEM_HEAD_DIM ** -0.5)
    p = jax.nn.softmax(s, axis=-1).astype(mv.dtype)
    o = jnp.einsum('bhsm,bmhd->bshd', p, mv).reshape(B, S, MEM_W)
    return o @ w_o


def peer(h, w_query, sub_keys, expert_u, expert_v):
    B, S, D = h.shape
    q = (h @ w_query).reshape(B, S, PEER_HEADS, 2, PEER_HALF)
    sc = jnp.einsum('bshpd,hpnd->bshpn', q, sub_keys).astype(jnp.float32)
    vals, idx = lax.top_k(sc, PEER_TOPK)
    cand_s = (vals[..., 0, :, None] + vals[..., 1, None, :]).reshape(B, S, PEER_HEADS, PEER_TOPK * PEER_TOPK)
    cand_i = (idx[..., 0, :, None] * N_KEYS + idx[..., 1, None, :]).reshape(B, S, PEER_HEADS, PEER_TOPK * PEER_TOPK)
    top_s, pos = lax.top_k(cand_s, PEER_TOPK)
    experts = jnp.take_along_axis(cand_i, pos, axis=-1)
    gates = jax.nn.softmax(top_s, axis=-1).astype(h.dtype)
    n_chunks = (B * S) // PEER_CHUNK
    hf = h.reshape(n_chunks, PEER_CHUNK, D)
    ef = experts.reshape(n_chunks, PEER_CHUNK, PEER_HEADS * PEER_TOPK)
    gf = gates.reshape(n_chunks, PEER_CHUNK, PEER_HEADS * PEER_TOPK)

    def block(args):
        hc, ec, gc = args
        a = jnp.einsum('cd,ckd->ck', hc, expert_u[ec])
        act = jax.nn.gelu(a, approximate=False) * gc
        return jnp.einsum('ck,ckd->cd', act, expert_v[ec])

    return lax.map(block, (hf, ef, gf)).reshape(B, S, D)


def setup_inputs(seed: int = 0) -> dict:
    key = jax.random.key(seed)
    ks = jax.random.split(key, 32)
    L = DEPTH

    def nrm(k, shape, scale):
        return jax.random.normal(k, shape, jnp.float32) * scale

    return {
        "x": nrm(ks[0], (BATCH, SEQ, D_MODEL), 1.0),
        "mem": nrm(ks[1], (BATCH, MEM_TOKENS, D_MODEL), 1.0),
        "positions": jnp.tile(jnp.arange(SEQ, dtype=jnp.int32)[None], (BATCH, 1)),
        "g_mix": 1.0 + nrm(ks[2], (L, D_MODEL), 0.02),
        "w_in": nrm(ks[3], (L, D_MODEL, IN_W), D_MODEL ** -0.5),
        "q_norm_g": 1.0 + nrm(ks[4], (L, HEAD_DIM), 0.02),
        "k_norm_g": 1.0 + nrm(ks[5], (L, HEAD_DIM), 0.02),
        "attn_sinks": nrm(ks[6], (L, N_HEADS), 0.5),
        "w_attn_o": nrm(ks[7], (L, ATTN_Q_W, D_MODEL), ATTN_Q_W ** -0.5),
        "conv_dw_w": nrm(ks[8], (L, CONV_WIDTH, 1, CONV_CH), CONV_WIDTH ** -0.5),
        "conv_dw_b": nrm(ks[9], (L, CONV_CH), 0.02),
        "conv_ln_g": 1.0 + nrm(ks[10], (L, CONV_CH), 0.02),
        "conv_ln_b": nrm(ks[11], (L, CONV_CH), 0.02),
        "w_conv_o": nrm(ks[12], (L, CONV_CH, D_MODEL), CONV_CH ** -0.5),
        "g_mem": 1.0 + nrm(ks[13], (L, D_MODEL), 0.02),
        "w_mem_kv": nrm(ks[14], (L, D_MODEL, 2 * MEM_W), D_MODEL ** -0.5),
        "mq_norm_g": 1.0 + nrm(ks[15], (L, MEM_HEAD_DIM), 0.02),
        "mk_norm_g": 1.0 + nrm(ks[16], (L, MEM_HEAD_DIM), 0.02),
        "w_mem_o": nrm(ks[17], (L, MEM_W, D_MODEL), MEM_W ** -0.5),
        "w_out": nrm(ks[18], (L, D_MODEL, D_MODEL), D_MODEL ** -0.5),
        "g_ffn": 1.0 + nrm(ks[19], (L, D_MODEL), 0.02),
        "w_query": nrm(ks[20], (L, D_MODEL, PEER_HEADS * PEER_QDIM), D_MODEL ** -0.5),
        "sub_keys": nrm(ks[21], (L, PEER_HEADS, 2, N_KEYS, PEER_HALF), PEER_HALF ** -0.5),
        "expert_u": nrm(ks[22], (L, N_EXPERTS, D_MODEL), D_MODEL ** -0.5),
        "expert_v": nrm(ks[23], (L, N_EXPERTS, D_MODEL), (PEER_HEADS * PEER_TOPK) ** -0.5),
    }


def reference(x, mem, positions, g_mix, w_in, q_norm_g, k_norm_g, attn_sinks, w_attn_o,
              conv_dw_w, conv_dw_b, conv_ln_g, conv_ln_b, w_conv_o, g_mem, w_mem_kv,
              mq_norm_g, mk_norm_g, w_mem_o, w_out, g_ffn, w_query, sub_keys, expert_u, expert_v):
    B, S, _ = x.shape
    inv_freq = ROPE_THETA ** (-jnp.arange(0, HEAD_DIM, 2, dtype=jnp.float32) / HEAD_DIM)
    ang = positions.astype(jnp.float32)[..., None] * inv_freq
    cos, sin = jnp.cos(ang)[:, :, None, :], jnp.sin(ang)[:, :, None, :]
    splits = np.cumsum([ATTN_Q_W, KV_W, KV_W, 2 * CONV_CH, MEM_W, D_MODEL, D_MODEL]).tolist()

    for l in range(DEPTH):
        h = rms_norm(x, g_mix[l])
        proj = h @ w_in[l]
        q, k, v, conv_in, mq, ga, gc, gm = jnp.split(proj, splits, axis=-1)
        q = rotary(rms_norm(q.reshape(B, S, N_HEADS, HEAD_DIM), q_norm_g[l]), cos, sin)
        k = rotary(rms_norm(k.reshape(B, S, N_KV_HEADS, HEAD_DIM), k_norm_g[l]), cos, sin)
        v = v.reshape(B, S, N_KV_HEADS, HEAD_DIM)
        attn = sliding_window_attention(q, k, v, attn_sinks[l]) @ w_attn_o[l]
        conv = conformer_conv(conv_in, conv_dw_w[l], conv_dw_b[l], conv_ln_g[l], conv_ln_b[l], w_conv_o[l])
        memo = memory_attention(mq, mem, g_mem[l], w_mem_kv[l], mq_norm_g[l], mk_norm_g[l], w_mem_o[l])
        merged = jax.nn.sigmoid(ga) * attn + jax.nn.sigmoid(gc) * conv + jax.nn.sigmoid(gm) * memo
        x = x + merged @ w_out[l]
        x = x + peer(rms_norm(x, g_ffn[l]), w_query[l], sub_keys[l], expert_u[l], expert_v[l])
    return x
```

```python
import numpy as np
from contextlib import ExitStack
import concourse.bass as bass
import concourse.mybir as mybir
from concourse.bass_utils import run_bass_kernel_spmd

F32 = mybir.dt.float32
BF16 = mybir.dt.bfloat16
I32 = mybir.dt.int32
U32 = mybir.dt.uint32
AF = mybir.ActivationFunctionType
ALU = mybir.AluOpType
AX = mybir.AxisListType

NCORES = 8
D = 2048
KC = 16
EPS = 1e-6
TWO_PI = 2.0 * np.pi

CV_GMIX, CV_GFFN, CV_GMEM = 0, 16, 32
CV_GQ, CV_GK, CV_MQG, CV_MKG = 48, 50, 52, 53
CV_SINK = 54
CV_DW = 70
CV_DWB, CV_LNG, CV_LNB = 194, 198, 202
CV_INVF, CV_SGN = 206, 207
NCV = 208


class B:
    def __init__(s, nc, es):
        s.nc = nc
        s.es = es
        s.engs = {'pe': nc.tensor, 'act': nc.scalar, 'dve': nc.vector, 'pool': nc.gpsimd, 'sp': nc.sync}
        s.sems = {}
        s.cnt = {}
        s.seen = {e: {} for e in s.engs}
        s.lastw = {}
        s.readers = {}
        for e in ['pe', 'act', 'dve', 'pool']:
            s.newsem(e)
        s.same_sync = {'pe': False, 'act': True, 'dve': True, 'pool': True, 'sp': True}

    def newsem(s, name):
        if name not in s.sems:
            s.sems[name] = s.es.enter_context(s.nc.semaphore(name))
            s.cnt[name] = 0
        return name

    def op(s, e, fn, r=(), w=(), dsem=None):
        eng = s.engs[e]
        need = {}

        def add(ev):
            if ev is not None:
                need[ev[0]] = max(need.get(ev[0], 0), ev[1])

        for t in r:
            add(s.lastw.get(t))
        for t in w:
            add(s.lastw.get(t))
            for sm, v in s.readers.get(t, {}).items():
                add((sm, v))
        for sm, v in need.items():
            if s.seen[e].get(sm, 0) < v:
                eng.wait_ge(s.sems[sm], v)
                s.seen[e][sm] = v
        inst = fn(eng)
        if dsem is not None:
            sm, inc = dsem, 16
        else:
            sm, inc = e, 1
        s.cnt[sm] += inc
        inst.then_inc(s.sems[sm], inc)
        ev = (sm, s.cnt[sm])
        if dsem is None and not s.same_sync[e]:
            s.seen[e][sm] = s.cnt[sm]
        for t in w:
            s.lastw[t] = ev
            s.readers[t] = {}
        for t in r:
            d = s.readers.setdefault(t, {})
            d[sm] = max(d.get(sm, 0), ev[1])
        return ev

    def barrier(s):
        for e, eng in s.engs.items():
            for sm, c in s.cnt.items():
                if c > 0 and s.seen[e].get(sm, 0) < c:
                    eng.wait_ge(s.sems[sm], c)
                    s.seen[e][sm] = c


import os
class _Stop(Exception):
    pass


def ckpt(k):
    if int(os.environ.get("KSTOP", "99")) == k:
        raise _Stop()


def blocks_of(total, bs=512):
    out = []
    o = 0
    while o < total:
        out.append((o, min(bs, total - o)))
        o += bs
    return out


def build_program(NPASS, NT, do_peer=True, first_core_flag=None):
    TS = NT * 128
    TH = TS + 128
    TTOT = NPASS * TS
    nc = bass.Bass("TRN2", target_bir_lowering=False)
    dr = {}

    def din(name, shape, dt=F32):
        dr[name] = nc.dram_tensor(name, list(shape), dt, kind="ExternalInput").ap()
        return dr[name]

    xh = din("xh", [TTOT + 128, D])
    posb = din("posb", [1, TTOT + 128], I32)
    w_in = din("w_in", [D, 3072])
    wqk_sw = din("wqk_sw", [D, 1536])
    wkdup = din("wkdup", [D, 512])
    wmerge = din("wmerge", [16, 128, 8192])
    w_out = din("w_out", [D, D])
    w_mem_kv = din("w_mem_kv", [D, 1024])
    memx = din("mem", [256, D])
    cvec_d = din("cvec", [128, NCV])
    cmat_d = din("cmat", [128, 384])
    masks_d = din("masks", [128, 1536])
    if do_peer:
        w_query = din("w_query", [D, D])
        skT_d = din("skT", [128, 16 * 128])
        uT_d = din("uT", [D, 16384])
        ev_d = din("ev", [16384, D])
        iota_d = din("iota", [128, 128])
    out_d = nc.dram_tensor("out", [TTOT, D], F32, kind="ExternalOutput").ap()
    if do_peer:
        dr["gsc"] = nc.dram_tensor("gsc", [128, 128, TS], BF16, kind="Internal").ap()

    es = ExitStack()
    with es:
        b = B(nc, es)

        def sb(name, shape, dt=F32):
            return es.enter_context(nc.sbuf_tensor(name, list(shape), dt))

        xacc = sb("xacc", [128, NT, D])
        cvec = sb("cvec_s", [128, NCV])
        cmat = sb("cmat_s", [128, 384], BF16)
        masks = sb("masks_s", [128, 1536], BF16)
        esink = sb("esink", [128, 16])
        mkT = sb("mkT", [128, 4, 256], BF16)
        mvd = sb("mvd", [128, 2, 512], BF16)
        wbuf = [sb(f"wbuf{i}", [128, 16, 512], BF16) for i in range(2)]
        for i in range(2):
            b.newsem(f"w{i}")
        ps = [es.enter_context(nc.psum_tensor(f"ps{i}", [128, 512], F32)) for i in range(8)]
        ident = cmat[:, 0:128]
        ones = cmat[:, 128:256]
        bones = cmat[:, 256:384]
        b.newsem("cst")
        b.newsem("mxl")
        b.newsem("pl")
        for i in range(NT + 1):
            b.newsem(f"xl{i}")
        for i in range(NT):
            b.newsem(f"st{i}")
        for nm in ["gsp", "gl0", "gl1", "wx0", "wx1"]:
            b.newsem(nm)

        b.newsem("cst0")
        b.op('sp', lambda e: e.dma_start(out=cvec[:], in_=cvec_d), w=['cvec'], dsem='cst0')
        b.op('pool', lambda e: e.dma_start(out=cmat[:], in_=cmat_d), w=['cmat'], dsem='cst')
        b.op('pool', lambda e: e.dma_start(out=masks[:], in_=masks_d), w=['masks'], dsem='cst')
        fin = ('cst', b.cnt['cst'])
        for t in ['cmat', 'masks']:
            b.lastw[t] = fin
        b.op('act', lambda e: e.activation(out=esink[:], in_=cvec[:, CV_SINK:CV_SINK + 16], func=AF.Exp),
             r=['cvec'], w=['esink'])

        wstate = {'i': 0}

        def load_w(pieces):
            i = wstate['i'] % 2
            wstate['i'] += 1
            for dst_fn, src in pieces:
                b.op('pool', lambda e, dst_fn=dst_fn, src=src: e.dma_start(out=dst_fn(wbuf[i]), in_=src),
                     w=[f'wbuf{i}'], dsem=f'w{i}')
            return i

        def wsrc(w_ap, c0, ncols, k0=0, kcn=KC):
            return w_ap[k0 * 128:(k0 + kcn) * 128, c0:c0 + ncols].rearrange("(c p) n -> p c n", p=128)

        def mm_group(out_ap, pairs, rtoks, wtok):
            def fn(e):
                inst = None
                n = len(pairs)
                for j, (l, r_) in enumerate(pairs):
                    inst = e.matmul(out_ap, l, r_, start=(j == 0), stop=(j == n - 1))
                return inst
            return b.op('pe', fn, r=rtoks, w=[wtok])

        def norm_transpose(pfx, src_ap, src_tok, dstT, dst_col0, gcol, dst_tok, tmp_sq, tmp_hb, stat, pbanks):
            ssq = stat[:, 0:1]
            rs = stat[:, 1:2]
            b.op('act', lambda e: e.activation(out=tmp_sq, in_=src_ap, func=AF.Square, accum_out=ssq),
                 r=[src_tok], w=[pfx + 'sq', pfx + 'stat'])
            b.op('act', lambda e: e.activation(out=rs, in_=ssq, func=AF.Sqrt, bias=cvec_eps, scale=1.0 / D),
                 r=[pfx + 'stat', 'eps'], w=[pfx + 'stat2'])
            b.op('dve', lambda e: e.reciprocal(out=rs, in_=rs), r=[pfx + 'stat2'], w=[pfx + 'stat2'])
            b.op('act', lambda e: e.activation(out=tmp_hb, in_=src_ap, func=AF.Copy, scale=rs),
                 r=[src_tok, pfx + 'stat2'], w=[pfx + 'hb'])
            for half in range(2):
                pb = pbanks[half]
                pview = ps[pb][:].bitcast(BF16)

                def fn(e, half=half, pview=pview):
                    inst = None
                    for j in range(8):
                        kc = half * 8 + j
                        inst = e.transpose(pview[:, j * 128:(j + 1) * 128], tmp_hb[:, kc * 128:(kc + 1) * 128], ident)
                    return inst
                b.op('pe', fn, r=[pfx + 'hb', 'cmat'], w=[f'ps{pb}'])
                b.op('dve', lambda e, half=half, pview=pview: e.tensor_tensor(
                    out=dstT[:, half * 8:(half + 1) * 8, dst_col0:dst_col0 + 128],
                    in0=pview.rearrange("p (c t) -> p c t", c=8),
                    in1=cvec[:, gcol + half * 8:gcol + half * 8 + 8].unsqueeze(2).to_broadcast([128, 8, 128]),
                    op=ALU.mult), r=[f'ps{pb}', 'cvec'], w=[dst_tok])

        eps_t = sb("eps_t", [128, 1])
        b.op('dve', lambda e: e.memset(eps_t[:], EPS), w=['eps'])
        cvec_eps = eps_t[:, 0:1]

        def fm_pair_gemm(XT, xtok, ncols_tok, items, epilogue, banks):
            pend = None
            it = 0
            for idx, (slot, la, lb) in enumerate(items):
                for (c0, cw) in blocks_of(ncols_tok):
                    ba, bb = banks[it % len(banks)]
                    it += 1
                    mm_group(ps[ba][:, 0:cw], [(l, XT[:, kc, c0:c0 + cw]) for kc, l in enumerate(la)],
                             [xtok, f'wbuf{slot}'], f'ps{ba}')
                    if lb is not None:
                        mm_group(ps[bb][:, 0:cw], [(l, XT[:, kc, c0:c0 + cw]) for kc, l in enumerate(lb)],
                                 [xtok, f'wbuf{slot}'], f'ps{bb}')
                    if pend is not None:
                        epilogue(*pend)
                    pend = (idx, c0, cw, ba, bb)
            if pend is not None:
                epilogue(*pend)

        with ExitStack() as ms:
            def msb(name, shape, dt=F32):
                return ms.enter_context(nc.sbuf_tensor(name, list(shape), dt))
            memT = msb("memT", [128, 16, 256], BF16)
            mx = msb("mx", [128, D])
            msq = msb("msq", [128, D])
            mhb = msb("mhb", [128, D], BF16)
            mstat = msb("mstat", [128, 2])
            t_sq = msb("m_t_sq", [128, 256], BF16)
            t_rs = msb("m_t_rs", [128, 256])
            for ti in range(2):
                b.op('sp', lambda e, ti=ti: e.dma_start(out=mx[:], in_=memx[ti * 128:(ti + 1) * 128, :]),
                     w=['mx'], dsem='mxl')
                norm_transpose('m', mx[:], 'mx', memT, ti * 128, CV_GMEM, 'memT', msq[:], mhb[:], mstat, (0, 1))
            slot = load_w([(lambda wb: wb[:, :, 0:512], wsrc(w_mem_kv, 0, 512))])
            for h in range(4):
                mm_group(ps[2][:, 0:256], [(wbuf[slot][:, kc, h * 128:(h + 1) * 128], memT[:, kc, :]) for kc in range(KC)],
                         ['memT', f'wbuf{slot}'], 'ps2')
                b.op('act', lambda e: e.activation(out=t_sq[:], in_=ps[2][:, 0:256], func=AF.Square), r=['ps2'], w=['m_sq'])
                mm_group(ps[3][:, 0:256], [(ones, t_sq[:])], ['m_sq', 'cmat'], 'ps3')
                b.op('act', lambda e: e.activation(out=t_rs[:], in_=ps[3][:, 0:256], func=AF.Sqrt, bias=cvec_eps, scale=1.0 / 128),
                     r=['ps3', 'eps'], w=['m_rs'])
                b.op('dve', lambda e: e.reciprocal(out=t_rs[:], in_=t_rs[:]), r=['m_rs'], w=['m_rs'])
                b.op('dve', lambda e, h=h: e.scalar_tensor_tensor(out=mkT[:, h, :], in0=ps[2][:, 0:256],
                                                                  scalar=cvec[:, CV_MKG:CV_MKG + 1], in1=t_rs[:],
                                                                  op0=ALU.mult, op1=ALU.mult),
                     r=['ps2', 'm_rs', 'cvec'], w=['mkT'])
            slot = load_w([(lambda wb: wb[:, :, 0:512], wsrc(w_mem_kv, 512, 512))])
            for ti in range(2):
                mm_group(ps[4][:, :], [(memT[:, kc, ti * 128:(ti + 1) * 128], wbuf[slot][:, kc, :]) for kc in range(KC)],
                         ['memT', f'wbuf{slot}'], 'ps4')
                b.op('act', lambda e, ti=ti: e.activation(out=mvd[:, ti, :], in_=ps[4][:, :], func=AF.Copy), r=['ps4'], w=['mvd'])
            b.barrier()

        for pi in range(NPASS):
            tok0 = pi * TS
            with ExitStack() as ms:
              try:
                  def msb(name, shape, dt=F32):
                      return ms.enter_context(nc.sbuf_tensor(f"{name}_p{pi}", list(shape), dt))
                  hT = msb("hT", [128, 16, TH], BF16)
                  qT = msb("qT", [128, 8, TH], BF16)
                  kTA = msb("kTA", [128, 4, TH], BF16)
                  kTB = msb("kTB", [128, 4, TH], BF16)
                  b.op('dve', lambda e: e.memset(kTA[64:128, :, :], 0.0), w=['kT2'])
                  b.op('dve', lambda e: e.memset(kTB[0:64, :, :], 0.0), w=['kT2'])
                  vdup = msb("vdup", [128, NT + 1, 4, 128], BF16)
                  attnT = msb("attnT", [128, 8, TS], BF16)
                  arena = msb("arena", [128, 8 * TH + 16 * TS], BF16)
                  o1 = 8 * TH
                  o2 = o1 + 8 * TS
                  o3 = o2 + 4 * TS
                  gluT = arena[:, 0:o1].bitcast(F32).rearrange("p (c t) -> p c t", c=4)
                  cT = arena[:, o1:o2].bitcast(F32).rearrange("p (c t) -> p c t", c=4)
                  cbf = arena[:, o2:o3].rearrange("p (c t) -> p c t", c=4)
                  csq = arena[:, o3:o3 + 4 * TS].rearrange("p (c t) -> p c t", c=4)
                  mergedT = arena[:, 0:16 * TS].rearrange("p (c t) -> p c t", c=16)
                  convT = msb("convT", [128, 4, TS], BF16)
                  mqT = msb("mqT", [128, 4, TS], BF16)
                  memoT = msb("memoT", [128, 4, TS], BF16)
                  cosT = msb("cosT", [128, TH])
                  sinS = msb("sinS", [128, TH])
                  posi = msb("posi", [128, TH], I32)
                  xhalo = msb("xhalo", [128, D])
                  tsq = msb("tsq", [128, D], BF16)
                  thb = msb("thb", [128, D], BF16)
                  stat = msb("stat", [128, 2])
                  tA = [msb(f"tA{i}", [128, 512]) for i in range(4)]
                  tB = [msb(f"tB{i}", [128, 512], BF16) for i in range(4)]

                  b.op('sp', lambda e: e.dma_start(out=posi[:], in_=posb[:, tok0:tok0 + TH].partition_broadcast(128)),
                       w=['posi'], dsem='pl')
                  ang = cosT
                  kk = sinS
                  b.op('dve', lambda e: e.tensor_copy(out=ang[:], in_=posi[:]), r=['posi'], w=['cosT'])
                  b.op('dve', lambda e: e.tensor_scalar(out=ang[:], in0=ang[:], scalar1=cvec[:, CV_INVF:CV_INVF + 1], scalar2=None,
                                                        op0=ALU.mult), r=['cosT', 'cvec'], w=['cosT'])
                  MAGIC = 12582912.0
                  b.op('dve', lambda e: e.tensor_scalar(out=kk[:], in0=ang[:], scalar1=1.0 / TWO_PI, scalar2=MAGIC,
                                                        op0=ALU.mult, op1=ALU.add), r=['cosT'], w=['sinS'])
                  b.op('dve', lambda e: e.tensor_scalar(out=kk[:], in0=kk[:], scalar1=MAGIC, scalar2=None,
                                                        op0=ALU.subtract), r=['sinS'], w=['sinS'])
                  C1 = 6.28125
                  C2 = float(np.float32(TWO_PI - 6.28125))
                  C3 = float(TWO_PI - 6.28125 - np.float64(np.float32(TWO_PI - 6.28125)))
                  for cc in (C1, C2, C3):
                      b.op('dve', lambda e, cc=cc: e.scalar_tensor_tensor(out=ang[:], in0=kk[:], scalar=-cc, in1=ang[:],
                                                                          op0=ALU.mult, op1=ALU.add),
                           r=['sinS', 'cosT'], w=['cosT'])
                  PI_LO = 3.1415925
                  b.op('dve', lambda e: e.tensor_scalar(out=ang[:], in0=ang[:], scalar1=PI_LO, scalar2=-PI_LO,
                                                        op0=ALU.min, op1=ALU.max), r=['cosT'], w=['cosT'])
                  b.op('act', lambda e: e.activation(out=sinS[:], in_=ang[:], func=AF.Sin), r=['cosT'], w=['sinS'])
                  b.op('dve', lambda e: e.tensor_scalar(out=sinS[:], in0=sinS[:], scalar1=cvec[:, CV_SGN:CV_SGN + 1], scalar2=None,
                                                        op0=ALU.mult), r=['sinS', 'cvec'], w=['sinS'])
                  wr = tA[0]
                  b.op('dve', lambda e: e.tensor_scalar(out=ang[:], in0=ang[:], scalar1=float(np.pi / 2), scalar2=None,
                                                        op0=ALU.add), r=['cosT'], w=['cosT'])
                  for (c0, cw) in blocks_of(TH):
                      b.op('dve', lambda e, c0=c0, cw=cw: e.tensor_scalar(out=wr[:, 0:cw], in0=ang[:, c0:c0 + cw], scalar1=PI_LO,
                                                                          scalar2=-TWO_PI, op0=ALU.is_gt, op1=ALU.mult),
                           r=['cosT'], w=['tA0'])
                      b.op('dve', lambda e, c0=c0, cw=cw: e.tensor_tensor(out=ang[:, c0:c0 + cw], in0=ang[:, c0:c0 + cw],
                                                                          in1=wr[:, 0:cw], op=ALU.add),
                           r=['tA0', 'cosT'], w=['cosT'])
                  b.op('dve', lambda e: e.tensor_scalar(out=ang[:], in0=ang[:], scalar1=PI_LO, scalar2=-PI_LO,
                                                        op0=ALU.min, op1=ALU.max), r=['cosT'], w=['cosT'])
                  b.op('act', lambda e: e.activation(out=cosT[:], in_=ang[:], func=AF.Sin), r=['cosT'], w=['cosT'])

                  ckpt(1)
                  for ti in range(NT + 1):
                      if ti == 0:
                          dst, tok = xhalo[:], 'xhalo'
                      else:
                          dst, tok = xacc[:, ti - 1, :], f'xacc{ti - 1}'
                      b.op('sp', lambda e, dst=dst, ti=ti: e.dma_start(out=dst, in_=xh[tok0 + ti * 128: tok0 + (ti + 1) * 128, :]),
                           w=[tok], dsem=f'xl{ti}')
                      norm_transpose('x', dst, tok, hT, ti * 128, CV_GMIX, 'hT', tsq[:], thb[:], stat, (0, 1))

                  ckpt(2)
                  def qk_epilogue_factory(dstT, gcol, nblk_items):
                      def epi(idx, c0, cw, ba, bb):
                          sq = tB[0]
                          rs = tA[1]
                          t1 = tA[2]
                          t2 = tA[3]
                          b.op('act', lambda e: e.activation(out=sq[:, 0:cw], in_=ps[ba][:, 0:cw], func=AF.Square),
                               r=[f'ps{ba}'], w=['tB0'])
                          mm_group(ps[6][:, 0:cw], [(bones, sq[:, 0:cw])], ['tB0', 'cmat'], 'ps6')
                          b.op('act', lambda e: e.activation(out=rs[:, 0:cw], in_=ps[6][:, 0:cw], func=AF.Sqrt,
                                                             bias=cvec_eps, scale=1.0 / 64), r=['ps6', 'eps'], w=['tA1'])
                          b.op('dve', lambda e: e.reciprocal(out=rs[:, 0:cw], in_=rs[:, 0:cw]), r=['tA1'], w=['tA1'])
                          b.op('dve', lambda e: e.scalar_tensor_tensor(out=t1[:, 0:cw], in0=ps[ba][:, 0:cw],
                                                                       scalar=cvec[:, gcol:gcol + 1], in1=cosT[:, c0:c0 + cw],
                                                                       op0=ALU.mult, op1=ALU.mult),
                               r=[f'ps{ba}', 'cvec', 'cosT'], w=['tA2'])
                          b.op('dve', lambda e: e.scalar_tensor_tensor(out=t2[:, 0:cw], in0=ps[bb][:, 0:cw],
                                                                       scalar=cvec[:, gcol + 1:gcol + 2], in1=sinS[:, c0:c0 + cw],
                                                                       op0=ALU.mult, op1=ALU.mult),
                               r=[f'ps{bb}', 'cvec', 'sinS'], w=['tA3'])
                          b.op('dve', lambda e: e.tensor_tensor(out=t1[:, 0:cw], in0=t1[:, 0:cw], in1=t2[:, 0:cw], op=ALU.add),
                               r=['tA2', 'tA3'], w=['tA2'])
                          if isinstance(dstT, tuple):
                              for (dd, p0) in zip(dstT, (0, 64)):
                                  b.op('dve', lambda e, dd=dd, p0=p0: e.tensor_tensor(out=dd[p0:p0 + 64, idx, c0:c0 + cw], in0=t1[p0:p0 + 64, 0:cw],
                                                                                      in1=rs[p0:p0 + 64, 0:cw], op=ALU.mult),
                                       r=['tA2', 'tA1'], w=[nblk_items])
                          else:
                              b.op('dve', lambda e: e.tensor_tensor(out=dstT[:, idx, c0:c0 + cw], in0=t1[:, 0:cw], in1=rs[:, 0:cw],
                                                                    op=ALU.mult), r=['tA2', 'tA1'], w=[nblk_items])
                      return epi

                  banks2 = [(2, 3), (4, 5)]
                  for blk in range(4):
                      slot = load_w([(lambda wb: wb[:, :, 0:256], wsrc(w_in, blk * 256, 256)),
                                     (lambda wb: wb[:, :, 256:512], wsrc(wqk_sw, blk * 256, 256))])
                      items = []
                      for j in range(2):
                          items.append((slot, [wbuf[slot][:, kc, j * 128:(j + 1) * 128] for kc in range(KC)],
                                        [wbuf[slot][:, kc, 256 + j * 128:256 + (j + 1) * 128] for kc in range(KC)]))
                      epi = qk_epilogue_factory(qT, CV_GQ, 'qT')
                      fm_pair_gemm(hT, 'hT', TH, items, lambda idx, c0, cw, ba, bb, blk=blk, epi=epi: epi(blk * 2 + idx, c0, cw, ba, bb), banks2)
                  for blk in range(2):
                      slot = load_w([(lambda wb: wb[:, :, 0:256], wsrc(wkdup, blk * 256, 256)),
                                     (lambda wb: wb[:, :, 256:512], wsrc(wqk_sw, 1024 + blk * 256, 256))])
                      items = []
                      for j in range(2):
                          items.append((slot, [wbuf[slot][:, kc, j * 128:(j + 1) * 128] for kc in range(KC)],
                                        [wbuf[slot][:, kc, 256 + j * 128:256 + (j + 1) * 128] for kc in range(KC)]))
                      epi = qk_epilogue_factory((kTA, kTB), CV_GK, 'kT2')
                      fm_pair_gemm(hT, 'hT', TH, items, lambda idx, c0, cw, ba, bb, blk=blk, epi=epi: epi(blk * 2 + idx, c0, cw, ba, bb), banks2)

                  ckpt(3)
                  slot = load_w([(lambda wb: wb[:, :, 0:256], wsrc(w_in, 1280, 256))])
                  for ti in range(NT + 1):
                      bk = 2 + (ti % 2)
                      mm_group(ps[bk][:, 0:256], [(hT[:, kc, ti * 128:(ti + 1) * 128], wbuf[slot][:, kc, 0:256]) for kc in range(KC)],
                               ['hT', f'wbuf{slot}'], f'ps{bk}')
                      for dup in range(2):
                          b.op('act' if dup == 0 else 'dve',
                               (lambda e, ti=ti, bk=bk: e.activation(out=vdup[:, ti, :, 0:64], in_=ps[bk][:, 0:256].rearrange("p (g d) -> p g d", g=4), func=AF.Copy))
                               if dup == 0 else
                               (lambda e, ti=ti, bk=bk: e.tensor_copy(out=vdup[:, ti, :, 64:128], in_=ps[bk][:, 0:256].rearrange("p (g d) -> p g d", g=4))),
                               r=[f'ps{bk}'], w=['vdup'])

                  ckpt(4)
                  def glu_epi(idx, c0, cw, ba, bb):
                      sg = tA[1]
                      b.op('act', lambda e: e.activation(out=sg[:, 0:cw], in_=ps[bb][:, 0:cw], func=AF.Sigmoid), r=[f'ps{bb}'], w=['tA1'])
                      b.op('dve', lambda e: e.tensor_tensor(out=gluT[:, idx, c0:c0 + cw], in0=ps[ba][:, 0:cw], in1=sg[:, 0:cw], op=ALU.mult),
                           r=[f'ps{ba}', 'tA1'], w=['gluT'])
                  for blk in range(2):
                      slot = load_w([(lambda wb: wb[:, :, 0:256], wsrc(w_in, 1536 + blk * 256, 256)),
                                     (lambda wb: wb[:, :, 256:512], wsrc(w_in, 2048 + blk * 256, 256))])
                      items = []
                      for j in range(2):
                          items.append((slot, [wbuf[slot][:, kc, j * 128:(j + 1) * 128] for kc in range(KC)],
                                        [wbuf[slot][:, kc, 256 + j * 128:256 + (j + 1) * 128] for kc in range(KC)]))
                      fm_pair_gemm(hT, 'hT', TH, items, lambda idx, c0, cw, ba, bb, blk=blk: glu_epi(blk * 2 + idx, c0, cw, ba, bb), banks2)
                  for c in range(4):
                      b.op('dve', lambda e, c=c: e.tensor_scalar(out=cT[:, c, :], in0=gluT[:, c, 98:98 + TS],
                                                                 scalar1=cvec[:, CV_DW + c * 31:CV_DW + c * 31 + 1],
                                                                 scalar2=cvec[:, CV_DWB + c:CV_DWB + c + 1], op0=ALU.mult, op1=ALU.add),
                           r=['gluT', 'cvec'], w=[f'cT{c}'])
                  for w_ in range(1, 31):
                      for c in range(4):
                          b.op('dve', lambda e, c=c, w_=w_: e.scalar_tensor_tensor(
                              out=cT[:, c, :], in0=gluT[:, c, 98 + w_:98 + w_ + TS],
                              scalar=cvec[:, CV_DW + c * 31 + w_:CV_DW + c * 31 + w_ + 1], in1=cT[:, c, :],
                              op0=ALU.mult, op1=ALU.add), r=['gluT', 'cvec', f'cT{c}'], w=[f'cT{c}'])
                  for c in range(4):
                      b.op('act', lambda e, c=c: e.activation(out=cbf[:, c, :], in_=cT[:, c, :], func=AF.Copy), r=[f'cT{c}'], w=['cbf'])
                      b.op('act', lambda e, c=c: e.activation(out=csq[:, c, :], in_=cT[:, c, :], func=AF.Square), r=[f'cT{c}'], w=['csq'])
                  for (c0, cw) in blocks_of(TS):
                      mm_group(ps[2][:, 0:cw], [(ones, cbf[:, c, c0:c0 + cw]) for c in range(4)], ['cbf', 'cmat'], 'ps2')
                      mm_group(ps[3][:, 0:cw], [(ones, csq[:, c, c0:c0 + cw]) for c in range(4)], ['csq', 'cmat'], 'ps3')
                      mean, msq_, rstd = tA[0], tA[1], tA[2]
                      b.op('dve', lambda e: e.tensor_scalar(out=mean[:, 0:cw], in0=ps[2][:, 0:cw], scalar1=1.0 / 512, scalar2=None, op0=ALU.mult),
                           r=['ps2'], w=['tA0'])
                      b.op('dve', lambda e: e.tensor_tensor(out=msq_[:, 0:cw], in0=mean[:, 0:cw], in1=mean[:, 0:cw], op=ALU.mult),
                           r=['tA0'], w=['tA1'])
                      b.op('dve', lambda e: e.scalar_tensor_tensor(out=rstd[:, 0:cw], in0=ps[3][:, 0:cw], scalar=1.0 / 512, in1=msq_[:, 0:cw],
                                                                   op0=ALU.mult, op1=ALU.subtract), r=['ps3', 'tA1'], w=['tA2'])
                      b.op('act', lambda e: e.activation(out=rstd[:, 0:cw], in_=rstd[:, 0:cw], func=AF.Sqrt, bias=cvec_eps, scale=1.0),
                           r=['tA2', 'eps'], w=['tA2'])
                      b.op('dve', lambda e: e.reciprocal(out=rstd[:, 0:cw], in_=rstd[:, 0:cw]), r=['tA2'], w=['tA2'])
                      for c in range(4):
                          xc = tA[3]
                          b.op('dve', lambda e, c=c: e.tensor_tensor(out=xc[:, 0:cw], in0=cT[:, c, c0:c0 + cw], in1=mean[:, 0:cw], op=ALU.subtract),
                               r=[f'cT{c}', 'tA0'], w=['tA3'])
                          b.op('dve', lambda e: e.tensor_tensor(out=xc[:, 0:cw], in0=xc[:, 0:cw], in1=rstd[:, 0:cw], op=ALU.mult),
                               r=['tA3', 'tA2'], w=['tA3'])
                          b.op('act', lambda e, c=c: e.activation(out=convT[:, c, c0:c0 + cw], in_=xc[:, 0:cw], func=AF.Silu,
                                                                  bias=cvec[:, CV_LNB + c:CV_LNB + c + 1], scale=cvec[:, CV_LNG + c:CV_LNG + c + 1]),
                               r=['tA3', 'cvec'], w=['convT'])

                  ckpt(5)
                  def mq_epi(idx, c0, cw, ba, bb):
                      sq = tB[0]
                      rs = tA[1]
                      b.op('act', lambda e: e.activation(out=sq[:, 0:cw], in_=ps[ba][:, 0:cw], func=AF.Square), r=[f'ps{ba}'], w=['tB0'])
                      mm_group(ps[6][:, 0:cw], [(ones, sq[:, 0:cw])], ['tB0', 'cmat'], 'ps6')
                      b.op('act', lambda e: e.activation(out=rs[:, 0:cw], in_=ps[6][:, 0:cw], func=AF.Sqrt, bias=cvec_eps, scale=1.0 / 128),
                           r=['ps6', 'eps'], w=['tA1'])
                      b.op('dve', lambda e: e.reciprocal(out=rs[:, 0:cw], in_=rs[:, 0:cw]), r=['tA1'], w=['tA1'])
                      b.op('dve', lambda e: e.scalar_tensor_tensor(out=mqT[:, idx, c0:c0 + cw], in0=ps[ba][:, 0:cw],
                                                                   scalar=cvec[:, CV_MQG:CV_MQG + 1], in1=rs[:, 0:cw],
                                                                   op0=ALU.mult, op1=ALU.mult), r=[f'ps{ba}', 'tA1', 'cvec'], w=['mqT'])
                  slot = load_w([(lambda wb: wb[:, :, 0:512], wsrc(w_in, 2560, 512))])
                  items = [(slot, [wbuf[slot][:, kc, j * 128:(j + 1) * 128] for kc in range(KC)], None) for j in range(4)]
                  hT_own = hT[:, :, 128:TH]
                  fm_pair_gemm(hT_own, 'hT', TS, items, mq_epi, banks2)

                  ckpt(6)
                  def attn_core(st_pairs_fn, nkb, v_lhsT_fn, mask_fn, scale, den_extra, out_fn, rtoks, cw=512):
                      for kb in range(nkb):
                          bk = 2 + kb
                          def fn(e, kb=kb, bk=bk):
                              inst = None
                              for (oc0, ocw, l, r_) in st_pairs_fn(kb):
                                  inst = e.matmul(ps[bk][:, oc0:oc0 + ocw], l, r_, start=True, stop=True)
                              return inst
                          b.op('pe', fn, r=rtoks, w=[f'ps{bk}'])
                          b.op('act', lambda e, kb=kb, bk=bk: e.activation(out=tB[kb][:, 0:cw], in_=ps[bk][:, 0:cw], func=AF.Exp, scale=scale),
                               r=[f'ps{bk}'], w=[f'tB{kb}'])
                          m = mask_fn(kb)
                          if m is not None:
                              b.op('dve', lambda e, kb=kb, m=m: e.tensor_tensor(out=tB[kb][:, 0:cw], in0=tB[kb][:, 0:cw], in1=m, op=ALU.mult),
                                   r=[f'tB{kb}', 'masks'], w=[f'tB{kb}'])
                      mm_group(ps[4][:, 0:cw], [(v_lhsT_fn(kb), tB[kb][:, 0:cw]) for kb in range(nkb)],
                               [f'tB{kb}' for kb in range(nkb)] + rtoks, 'ps4')
                      mm_group(ps[5][:, 0:cw], [(ones, tB[kb][:, 0:cw]) for kb in range(nkb)],
                               [f'tB{kb}' for kb in range(nkb)] + ['cmat'], 'ps5')
                      rden = tA[0]
                      if den_extra is not None:
                          b.op('dve', lambda e: e.tensor_tensor(out=rden[:, 0:cw].rearrange("p (h q) -> p h q", h=4), in0=ps[5][:, 0:cw].rearrange("p (h q) -> p h q", h=4),
                                                                in1=den_extra, op=ALU.add), r=['ps5', 'esink'], w=['tA0'])
                          b.op('dve', lambda e: e.reciprocal(out=rden[:, 0:cw], in_=rden[:, 0:cw]), r=['tA0'], w=['tA0'])
                      else:
                          b.op('dve', lambda e: e.reciprocal(out=rden[:, 0:cw], in_=ps[5][:, 0:cw]), r=['ps5'], w=['tA0'])
                      out_fn(rden)

                  for n in range(NT):
                      qc0 = 128 * (n + 1)
                      for g in range(4):
                          def st_pairs(kb, n=n, g=g, qc0=qc0):
                              kc0 = 128 * (n + kb)
                              return [(0, 256, kTA[:, g, kc0:kc0 + 128], qT[:, 2 * g:2 * g + 2, qc0:qc0 + 128]),
                                      (256, 256, kTB[:, g, kc0:kc0 + 128], qT[:, 2 * g:2 * g + 2, qc0:qc0 + 128])]

                          def mask_fn(kb, n=n):
                              if kb == 1:
                                  return masks[:, 512:1024]
                              if n == 0 and pi == 0:
                                  return masks[:, 1024:1536]
                              return masks[:, 0:512]

                          def out_fn(rden, n=n, g=g):
                              b.op('dve', lambda e: e.tensor_tensor(out=attnT[0:64, 2 * g:2 * g + 2, n * 128:(n + 1) * 128],
                                                                    in0=ps[4][0:64, 0:256].rearrange("p (h q) -> p h q", h=2),
                                                                    in1=rden[0:64, 0:256].rearrange("p (h q) -> p h q", h=2), op=ALU.mult),
                                   r=['ps4', 'tA0'], w=['attnT'])
                              b.op('dve', lambda e: e.tensor_tensor(out=attnT[64:128, 2 * g:2 * g + 2, n * 128:(n + 1) * 128],
                                                                    in0=ps[4][64:128, 256:512].rearrange("p (h q) -> p h q", h=2),
                                                                    in1=rden[64:128, 256:512].rearrange("p (h q) -> p h q", h=2), op=ALU.mult),
                                   r=['ps4', 'tA0'], w=['attnT'])
                          attn_core(st_pairs, 2, lambda kb, n=n, g=g: vdup[:, n + kb, g, :], mask_fn, 0.125,
                                    esink[:, 4 * g:4 * g + 4].unsqueeze(2).to_broadcast([128, 4, 128]), out_fn,
                                    ['qT', 'kT2', 'vdup'])
                  for h in range(4):
                      for (c0, cw) in blocks_of(TS):
                          def st_pairs(kb, h=h, c0=c0, cw=cw):
                              return [(0, cw, mkT[:, h, kb * 128:(kb + 1) * 128], mqT[:, h, c0:c0 + cw])]

                          def out_fn(rden, h=h, c0=c0, cw=cw):
                              b.op('dve', lambda e: e.tensor_tensor(out=memoT[:, h, c0:c0 + cw], in0=ps[4][:, 0:cw], in1=rden[:, 0:cw], op=ALU.mult),
                                   r=['ps4', 'tA0'], w=['memoT'])
                          attn_core(st_pairs, 2, lambda kb, h=h: mvd[:, kb, h * 128:(h + 1) * 128], lambda kb: None,
                                    float(128 ** -0.5), None, out_fn, ['mqT', 'mkT', 'mvd'], cw=cw)

                  ckpt(7)
                  b.barrier()
                  for n in range(16):
                      col = n * 128
                      i = wstate['i'] % 2
                      wv = lambda wb: wb[:].rearrange("p a (b c) -> p (a b) c", c=128)
                      slot = load_w([(lambda wb: wb[:].rearrange("p a n -> p (a n)").rearrange("p (c n) -> p c n", c=4),
                                      wmerge[n].rearrange("p (c n) -> p c n", c=4))])
                      wb = wv(wbuf[slot])
                      for (c0, cw) in blocks_of(TS):
                          wt = [f'wbuf{slot}']
                          mm_group(ps[0][:, 0:cw], [(wb[:, kc, :], hT[:, kc, 128 + c0:128 + c0 + cw]) for kc in range(16)], ['hT'] + wt, 'ps0')
                          mm_group(ps[1][:, 0:cw], [(wb[:, 16 + kc, :], attnT[:, kc, c0:c0 + cw]) for kc in range(8)], ['attnT'] + wt, 'ps1')
                          mm_group(ps[2][:, 0:cw], [(wb[:, 24 + kc, :], hT[:, kc, 128 + c0:128 + c0 + cw]) for kc in range(16)], ['hT'] + wt, 'ps2')
                          mm_group(ps[3][:, 0:cw], [(wb[:, 40 + kc, :], convT[:, kc, c0:c0 + cw]) for kc in range(4)], ['convT'] + wt, 'ps3')
                          mm_group(ps[4][:, 0:cw], [(wb[:, 44 + kc, :], hT[:, kc, 128 + c0:128 + c0 + cw]) for kc in range(16)], ['hT'] + wt, 'ps4')
                          mm_group(ps[5][:, 0:cw], [(wb[:, 60 + kc, :], memoT[:, kc, c0:c0 + cw]) for kc in range(4)], ['memoT'] + wt, 'ps5')
                          sg, m1, m2 = tA[0], tA[1], tA[2]
                          for bi, (gb, ob) in enumerate([(0, 1), (2, 3), (4, 5)]):
                              b.op('act', lambda e, gb=gb: e.activation(out=sg[:, 0:cw], in_=ps[gb][:, 0:cw], func=AF.Sigmoid), r=[f'ps{gb}'], w=['tA0'])
                              dst = m1 if bi == 0 else m2
                              b.op('dve', lambda e, ob=ob, dst=dst: e.tensor_tensor(out=dst[:, 0:cw], in0=ps[ob][:, 0:cw], in1=sg[:, 0:cw], op=ALU.mult),
                                   r=[f'ps{ob}', 'tA0'], w=['tA1' if bi == 0 else 'tA2'])
                              if bi == 1:
                                  b.op('dve', lambda e: e.tensor_tensor(out=m1[:, 0:cw], in0=m1[:, 0:cw], in1=m2[:, 0:cw], op=ALU.add),
                                       r=['tA1', 'tA2'], w=['tA1'])
                              if bi == 2:
                                  b.op('dve', lambda e, n=n: e.tensor_tensor(out=mergedT[:, n, c0:c0 + cw], in0=m1[:, 0:cw], in1=m2[:, 0:cw], op=ALU.add),
                                       r=['tA1', 'tA2'], w=['mergedT'])

                  ckpt(8)
                  for nb in range(4):
                      slot = load_w([(lambda wb: wb[:, :, 0:512], wsrc(w_out, nb * 512, 512))])
                      for ti in range(NT):
                          bk = 6 + (ti % 2)
                          mm_group(ps[bk][:, :], [(mergedT[:, kc, ti * 128:(ti + 1) * 128], wbuf[slot][:, kc, :]) for kc in range(KC)],
                                   ['mergedT', f'wbuf{slot}'], f'ps{bk}')
                          b.op('dve', lambda e, ti=ti, nb=nb, bk=bk: e.tensor_tensor(out=xacc[:, ti, nb * 512:(nb + 1) * 512],
                                                                                     in0=xacc[:, ti, nb * 512:(nb + 1) * 512], in1=ps[bk][:, :], op=ALU.add),
                               r=[f'ps{bk}', f'xacc{ti}'], w=[f'xacc{ti}'])
                  b.barrier()

              except _Stop:
                b.barrier()
            if do_peer:
                peer_phase(nc, b, ps, xacc, cvec, cmat, wbuf, load_w, wsrc, mm_group, norm_transpose, dr, NT, TS, cvec_eps, pi, fm_pair_gemm, wstate)
                b.barrier()

            for ti in range(NT):
                b.op('sp', lambda e, ti=ti: e.dma_start(out=out_d[tok0 + ti * 128: tok0 + (ti + 1) * 128, :], in_=xacc[:, ti, :]),
                     r=[f'xacc{ti}'], dsem=f'st{ti}')
            b.barrier()
        b.barrier()
    return nc


def peer_phase(nc, b, ps, xacc, cvec, cmat, wbuf, load_w, wsrc, mm_group, norm_transpose, dr, NT, TS, cvec_eps, pi, fm_pair_gemm, wstate):
    ident = cmat[:, 0:128]
    NEG = -1.0e30
    w_query, skT_d, uT_d, ev_d, iota_d, cmat_d = dr["w_query"], dr["skT"], dr["uT"], dr["ev"], dr["iota"], dr["cmat"]
    gsc = dr["gsc"]
    with ExitStack() as ms:
        def msb(name, shape, dt=F32):
            return ms.enter_context(nc.sbuf_tensor(f"{name}_q{pi}", list(shape), dt))
        hnT = msb("hnT", [128, 16, TS], BF16)
        iota_f = msb("iota_f", [128, 128])
        ident_f = msb("ident_f", [128, 128])
        b.op('sp', lambda e: e.dma_start(out=iota_f[:], in_=iota_d), w=['iota_f'], dsem='pl')
        b.op('sp', lambda e: e.dma_start(out=ident_f[:], in_=cmat_d[:, 0:128]), w=['ident_f'], dsem='pl')
        fin = ('pl', b.cnt['pl'])
        b.lastw['iota_f'] = fin
        b.lastw['ident_f'] = fin
        with ExitStack() as m1:
            def sb1(name, shape, dt=F32):
                return m1.enter_context(nc.sbuf_tensor(f"{name}_q{pi}", list(shape), dt))
            tsq = sb1("ptsq", [128, D])
            thb = sb1("pthb", [128, D], BF16)
            stat = sb1("pstat", [128, 2])
            qpT = sb1("qpT", [128, 16, TS], BF16)
            skT = sb1("skT", [128, 16, 128], BF16)
            Gst = sb1("Gst", [128, 128, 128], BF16)
            jhot = sb1("jhot", [128, 4, 128], BF16)
            gih = sb1("gih", [128, 4, 128], BF16)
            jsq = sb1("jsq", [128, 4, 128])
            one_t = sb1("one_t", [128, 1])
            b.op('dve', lambda e: e.memset(one_t[:], 1.0), w=['one_t'])
            s12x = [sb1(f"s12_{i}", [128, 256]) for i in range(2)]
            s12bx = [sb1(f"s12b_{i}", [128, 256]) for i in range(2)]
            cand2x = [sb1(f"cand2_{i}", [128, 256]) for i in range(2)]
            vals = sb1("vals", [128, 16, 16])
            idxu = sb1("idxu", [128, 16, 16], U32)
            idxf = sb1("idxf", [128, 16, 16])
            cand = sb1("cand", [128, 8, 256])
            cand2 = sb1("cand2", [128, 256])
            cvals = sb1("cvals", [128, 8, 16])
            cposu = sb1("cposu", [128, 8, 16], U32)
            au = sb1("au", [128, 128], U32)
            bu = sb1("bu", [128, 128], U32)
            af = sb1("af", [128, 128])
            bf_ = sb1("bf_", [128, 128])
            eq = sb1("eq", [128, 128, 16])
            negm = sb1("negm", [128, 8])
            gsum = sb1("gsum", [128, 8])
            Itm = sb1("Itm", [128, 128])
            Jtm = sb1("Jtm", [128, 128])
            Gtm = sb1("Gtm", [128, 128])
            iT = sb1("iT", [128, 128])
            jT = sb1("jT", [128, 128])
            gT = sb1("gT", [128, 128])
            b.op('pool', lambda e: e.dma_start(out=skT[:], in_=skT_d.rearrange("p (c k) -> p c k", c=16)), w=['skT'], dsem='cst')
            for ti in range(NT):
                norm_transpose('h', xacc[:, ti, :], f'xacc{ti}', hnT, ti * 128, CV_GFFN, 'hnT', tsq[:], thb[:], stat, (0, 1))
            def q_epi(idx, c0, cw, ba, bb):
                b.op('act', lambda e: e.activation(out=qpT[:, q_epi.base + idx, c0:c0 + cw], in_=ps[ba][:, 0:cw], func=AF.Copy),
                     r=[f'ps{ba}'], w=['qpT'])
            for blk in range(4):
                slot = load_w([(lambda wb: wb[:, :, 0:512], wsrc(w_query, blk * 512, 512))])
                items = [(slot, [wbuf[slot][:, kc, j * 128:(j + 1) * 128] for kc in range(KC)], None) for j in range(4)]
                q_epi.base = blk * 4
                fm_pair_gemm(hnT, 'hnT', TS, items, q_epi, [(2, 3), (4, 5)])
            for ti in range(NT):
                tc = slice(ti * 128, (ti + 1) * 128)
                for h0 in range(0, 8, 2):
                    chains = []
                    for h in (h0, h0 + 1):
                        q_ = h % 2
                        bk = 2 + q_
                        s12h, s12bh = s12x[q_], s12bx[q_]
                        def fn(e, h=h, bk=bk):
                            e.matmul(ps[bk][:, 0:128], qpT[:, 2 * h, tc], skT[:, 2 * h, :], start=True, stop=True)
                            return e.matmul(ps[bk][:, 128:256], qpT[:, 2 * h + 1, tc], skT[:, 2 * h + 1, :], start=True, stop=True)
                        b.op('pe', fn, r=['qpT', 'skT'], w=[f'ps{bk}'])
                        b.op('act', lambda e, bk=bk, s12h=s12h: e.activation(out=s12h[:], in_=ps[bk][:, 0:256], func=AF.Copy), r=[f'ps{bk}'], w=[f's12_{q_}'])
                        for p in range(2):
                            hp = 2 * h + p
                            sv = s12h[:, p * 128:(p + 1) * 128]
                            sv2 = s12bh[:, p * 128:(p + 1) * 128]
                            vt, it_, st, st2 = f'vals{hp}', f'idxu{hp}', f's12_{q_}', f's12b_{q_}_{p}'
                            chains.append([
                                ('dve', lambda e, hp=hp, sv=sv: e.max(out=vals[:, hp, 0:8], in_=sv), [st], [vt + 'a']),
                                ('dve', lambda e, hp=hp, sv=sv: e.max_index(out=idxu[:, hp, 0:8], in_max=vals[:, hp, 0:8], in_values=sv), [st, vt + 'a'], [it_ + 'a']),
                                ('dve', lambda e, hp=hp, sv=sv, sv2=sv2: e.match_replace(out=sv2, in_to_replace=vals[:, hp, 0:8], in_values=sv, imm_value=NEG), [st, vt + 'a'], [st2]),
                                ('dve', lambda e, hp=hp, sv2=sv2: e.max(out=vals[:, hp, 8:16], in_=sv2), [st2], [vt + 'b']),
                                ('dve', lambda e, hp=hp, sv2=sv2: e.max_index(out=idxu[:, hp, 8:16], in_max=vals[:, hp, 8:16], in_values=sv2), [st2, vt + 'b'], [it_ + 'b']),
                            ])
                    for step in range(5):
                        for ch in chains:
                            eng_, f_, r_, w_ = ch[step]
                            b.op(eng_, f_, r=r_, w=w_)
                    chains = []
                    for h in (h0, h0 + 1):
                        cv = cand[:, h, :]
                        c2 = cand2x[h % 2]
                        vr = [f'vals{2 * h}a', f'vals{2 * h}b', f'vals{2 * h + 1}a', f'vals{2 * h + 1}b']
                        chains.append([
                            ('dve', lambda e, h=h, cv=cv: e.tensor_tensor(out=cv.rearrange("p (a c) -> p a c", a=16),
                                                                          in0=vals[:, 2 * h, :].unsqueeze(2).to_broadcast([128, 16, 16]),
                                                                          in1=vals[:, 2 * h + 1, :].unsqueeze(1).to_broadcast([128, 16, 16]), op=ALU.add), vr, [f'cand{h}']),
                            ('dve', lambda e, h=h, cv=cv: e.max(out=cvals[:, h, 0:8], in_=cv), [f'cand{h}'], [f'cvals{h}a']),
                            ('dve', lambda e, h=h, cv=cv: e.max_index(out=cposu[:, h, 0:8], in_max=cvals[:, h, 0:8], in_values=cv), [f'cand{h}', f'cvals{h}a'], [f'cposu{h}a']),
                            ('dve', lambda e, h=h, cv=cv, c2=c2: e.match_replace(out=c2[:], in_to_replace=cvals[:, h, 0:8], in_values=cv, imm_value=NEG),
                             [f'cand{h}', f'cvals{h}a'], [f'cand2_{h % 2}']),
                            ('dve', lambda e, h=h, c2=c2: e.max(out=cvals[:, h, 8:16], in_=c2[:]), [f'cand2_{h % 2}'], [f'cvals{h}b']),
                            ('dve', lambda e, h=h, c2=c2: e.max_index(out=cposu[:, h, 8:16], in_max=cvals[:, h, 8:16], in_values=c2[:]), [f'cand2_{h % 2}', f'cvals{h}b'], [f'cposu{h}b']),
                        ])
                    for step in range(6):
                        for ch in chains:
                            eng_, f_, r_, w_ = ch[step]
                            b.op(eng_, f_, r=r_, w=w_)
                ALLV = [f'vals{i}{x}' for i in range(16) for x in 'ab']
                ALLI = [f'idxu{i}{x}' for i in range(16) for x in 'ab']
                ALLC = [f'cvals{i}{x}' for i in range(8) for x in 'ab']
                ALLP = [f'cposu{i}{x}' for i in range(8) for x in 'ab']
                b.op('dve', lambda e: e.tensor_copy(out=idxf[:], in_=idxu[:]), r=ALLI, w=['idxf'])
                b.op('dve', lambda e: e.tensor_scalar(out=negm[:], in0=cvals[:, :, 0], scalar1=-1.0, scalar2=None, op0=ALU.mult), r=ALLC, w=['negm'])
                for h in range(8):
                    b.op('act', lambda e, h=h: e.activation(out=Gtm[:, h * 16:(h + 1) * 16], in_=cvals[:, h, :], func=AF.Exp, bias=negm[:, h:h + 1],
                                                            scale=1.0, accum_out=gsum[:, h:h + 1]), r=ALLC + ['negm'], w=['Gtm', 'gsum'])
                b.op('dve', lambda e: e.reciprocal(out=gsum[:], in_=gsum[:]), r=['gsum'], w=['gsum'])
                b.op('dve', lambda e: e.tensor_tensor(out=Gtm[:].rearrange("p (h r) -> p h r", h=8), in0=Gtm[:].rearrange("p (h r) -> p h r", h=8),
                                                      in1=gsum[:].unsqueeze(2).to_broadcast([128, 8, 16]), op=ALU.mult), r=['Gtm', 'gsum'], w=['Gtm'])
                cpu_ = cposu[:].rearrange("p h r -> p (h r)")
                b.op('dve', lambda e: e.tensor_single_scalar(out=au[:], in_=cpu_, scalar=4, op=ALU.logical_shift_right), r=ALLP, w=['au'])
                b.op('dve', lambda e: e.tensor_single_scalar(out=bu[:], in_=cpu_, scalar=15, op=ALU.bitwise_and), r=ALLP, w=['bu'])
                b.op('dve', lambda e: e.tensor_copy(out=af[:], in_=au[:]), r=['au'], w=['af'])
                b.op('dve', lambda e: e.tensor_copy(out=bf_[:], in_=bu[:]), r=['bu'], w=['bf_'])
                for (srcf, half, dst, dtok) in ((af, 0, Itm, 'Itm'), (bf_, 1, Jtm, 'Jtm')):
                    b.op('dve', lambda e, srcf=srcf: e.tensor_tensor(out=eq[:], in0=srcf[:].unsqueeze(2).to_broadcast([128, 128, 16]),
                                                                     in1=iota_f[:, 0:16].unsqueeze(1).to_broadcast([128, 128, 16]), op=ALU.is_equal),
                         r=['af', 'bf_', 'iota_f'], w=['eq'])
                    idx_h = idxf[:].rearrange("p (h t) a -> p h t a", t=2)[:, :, half, :]
                    b.op('dve', lambda e, idx_h=idx_h: e.tensor_tensor(out=eq[:].rearrange("p (h r) a -> p h r a", h=8),
                                                                       in0=eq[:].rearrange("p (h r) a -> p h r a", h=8),
                                                                       in1=idx_h.unsqueeze(2).to_broadcast([128, 8, 16, 16]), op=ALU.mult),
                         r=['eq', 'idxf'], w=['eq'])
                    b.op('dve', lambda e, dst=dst: e.tensor_reduce(out=dst[:], in_=eq[:], axis=AX.X, op=ALU.add), r=['eq'], w=[dtok])
                for (srct, stok, dstt, dtok, bk) in ((Itm, 'Itm', iT, 'iT', 4), (Jtm, 'Jtm', jT, 'jT', 5), (Gtm, 'Gtm', gT, 'gT', 6)):
                    b.op('pe', lambda e, srct=srct, bk=bk: e.transpose(ps[bk][:, 0:128], srct[:], ident_f[:]), r=[stok, 'ident_f'], w=[f'ps{bk}'])
                    b.op('act', lambda e, dstt=dstt, bk=bk, dtok=dtok: e.activation(out=dstt[:], in_=ps[bk][:, 0:128], func=AF.Copy, scale=(-1.0 if dtok == 'jT' else 1.0)),
                         r=[f'ps{bk}'], w=[dtok])
                for q4 in range(32):
                    bk = q4 % 2
                    for u in range(4):
                        t = q4 * 4 + u
                        b.op('act', lambda e, u=u, t=t: e.activation(out=jsq[:, u, :], in_=iota_f[:], func=AF.Square, bias=jT[:, t:t + 1], scale=1.0),
                             r=['iota_f', 'jT'], w=[f'jsq{u}'])
                        b.op('act', lambda e, u=u, t=t: e.activation(out=jhot[:, u, :], in_=jsq[:, u, :], func=AF.Relu, bias=one_t[:, 0:1], scale=-1.0),
                             r=[f'jsq{u}', 'one_t'], w=[f'jhot{u}'])
                        b.op('dve', lambda e, u=u, t=t: e.tensor_scalar(out=gih[:, u, :], in0=iota_f[:], scalar1=iT[:, t:t + 1], scalar2=gT[:, t:t + 1],
                                                                        op0=ALU.is_equal, op1=ALU.mult), r=['iota_f', 'iT', 'gT'], w=[f'gih{u}'])
                    def fn(e, bk=bk):
                        inst = None
                        for u in range(4):
                            inst = e.matmul(ps[bk][:, u * 128:(u + 1) * 128], jhot[:, u, :], gih[:, u, :], start=True, stop=True)
                        return inst
                    b.op('pe', fn, r=[f'jhot{u}' for u in range(4)] + [f'gih{u}' for u in range(4)], w=[f'ps{bk}'])
                    b.op('act', lambda e, bk=bk, q4=q4: e.activation(out=Gst[:, :, q4 * 4:(q4 + 1) * 4], in_=ps[bk][:, :].rearrange("p (t c) -> p c t", t=4), func=AF.Copy),
                         r=[f'ps{bk}'], w=['Gst'])
                for c8 in range(8):
                    b.op('sp', lambda e, c8=c8, ti=ti: e.dma_start(out=gsc[c8 * 16:(c8 + 1) * 16, :, ti * 128:(ti + 1) * 128].rearrange("c j t -> j c t"),
                                                                  in_=Gst[:, c8 * 16:(c8 + 1) * 16, :]), r=['Gst'], w=['gsc'], dsem='gsp')
            b.barrier()
        with ExitStack() as m2:
            def sb2(name, shape, dt=F32):
                return m2.enter_context(nc.sbuf_tensor(f"{name}_q{pi}", list(shape), dt))
            wb2 = [sb2(f"wbx{i}", [128, 16, 512], BF16) for i in range(2)]
            gTg = [sb2(f"gTg{i}", [128, 4, TS], BF16) for i in range(2)]
            actT = [sb2(f"actT{i}", [128, 4, TS], BF16) for i in range(2)]
            gel = [sb2(f"gel{i}", [128, 512], BF16) for i in range(2)]
            ob = 0
            for gi in range(32):
                par = gi % 2
                b.op('pool', lambda e, gi=gi, par=par: e.dma_start(out=wbuf[par][:], in_=wsrc(uT_d, gi * 512, 512)), w=[f'wbuf{par}'], dsem=f'w{par}')
                b.op('pool', lambda e, gi=gi, par=par: e.dma_start(out=wb2[par][:].rearrange("p a n -> p (a n)").rearrange("p (c n) -> p c n", c=4),
                                                                   in_=ev_d[gi * 512:(gi + 1) * 512, :].rearrange("(c p) n -> p c n", p=128)),
                     w=[f'wbx{par}'], dsem=f'wx{par}')
                b.op('sp', lambda e, gi=gi, par=par: e.dma_start(out=gTg[par][:], in_=gsc[gi * 4:(gi + 1) * 4, :, :].rearrange("c j t -> j c t")),
                     r=['gsc'], w=[f'gTg{par}'], dsem=f'gl{par}')
                vview = wb2[par][:].rearrange("p a n -> p (a n)").rearrange("p (c n) -> p c n", c=4)
                for cc in range(4):
                    for (c0, cw) in blocks_of(TS):
                        ab = cc % 2
                        mm_group(ps[ab][:, 0:cw], [(wbuf[par][:, kc, cc * 128:(cc + 1) * 128], hnT[:, kc, c0:c0 + cw]) for kc in range(KC)],
                                 ['hnT', f'wbuf{par}'], f'ps{ab}')
                        b.op('act', lambda e, ab=ab, cw=cw: e.activation(out=gel[ab][:, 0:cw], in_=ps[ab][:, 0:cw], func=AF.Gelu), r=[f'ps{ab}'], w=[f'gel{ab}'])
                        b.op('dve', lambda e, ab=ab, cc=cc, c0=c0, cw=cw, par=par: e.tensor_tensor(out=actT[par][:, cc, c0:c0 + cw], in0=gel[ab][:, 0:cw],
                                                                                                   in1=gTg[par][:, cc, c0:c0 + cw], op=ALU.mult),
                             r=[f'gel{ab}', f'gTg{par}'], w=[f'actT{par}'])
                for ti in range(NT):
                    for db in range(4):
                        bk = 2 + (ob % 4)
                        ob += 1
                        mm_group(ps[bk][:, :], [(actT[par][:, cc, ti * 128:(ti + 1) * 128], vview[:, cc, db * 512:(db + 1) * 512]) for cc in range(4)],
                                 [f'actT{par}', f'wbx{par}'], f'ps{bk}')
                        b.op('dve', lambda e, ti=ti, db=db, bk=bk: e.tensor_tensor(out=xacc[:, ti, db * 512:(db + 1) * 512], in0=xacc[:, ti, db * 512:(db + 1) * 512],
                                                                                   in1=ps[bk][:, :], op=ALU.add), r=[f'ps{bk}', f'xacc{ti}'], w=[f'xacc{ti}'])
            b.barrier()


def host_prep(inp, NPASS, NT, do_peer=True):
    TS = NT * 128
    TTOT = NPASS * TS
    f = lambda a: np.ascontiguousarray(np.asarray(a, dtype=np.float32))
    x = f(inp["x"])[0]
    pos = np.asarray(inp["positions"])[0].astype(np.int32)
    w_in = f(inp["w_in"][0])
    def swap_cols(w, nh):
        w4 = w.reshape(w.shape[0], nh, 2, 32)
        return np.ascontiguousarray(w4[:, :, ::-1, :].reshape(w.shape[0], nh * 64))
    wq = w_in[:, 0:1024]
    wk = w_in[:, 1024:1280]
    wk_sw = swap_cols(wk, 4)
    def dup(w):
        w3 = w.reshape(w.shape[0], 4, 1, 64)
        return np.ascontiguousarray(np.repeat(w3, 2, axis=2).reshape(w.shape[0], 512))
    wqk_sw = np.ascontiguousarray(np.concatenate([swap_cols(wq, 16), dup(wk_sw)], axis=1))
    wkdup = dup(wk)
    fm = lambda v, c: np.ascontiguousarray(f(v).reshape(c, 128).T)
    cvec = np.zeros((128, NCV), np.float32)
    cvec[:, CV_GMIX:CV_GMIX + 16] = fm(inp["g_mix"][0], 16)
    cvec[:, CV_GFFN:CV_GFFN + 16] = fm(inp["g_ffn"][0], 16)
    cvec[:, CV_GMEM:CV_GMEM + 16] = fm(inp["g_mem"][0], 16)
    p = np.arange(128)
    gq = f(inp["q_norm_g"][0]); gk = f(inp["k_norm_g"][0])
    cvec[:, CV_GQ] = gq[p % 64]; cvec[:, CV_GQ + 1] = gq[(p % 64 + 32) % 64]
    cvec[:, CV_GK] = gk[p % 64]; cvec[:, CV_GK + 1] = gk[(p % 64 + 32) % 64]
    cvec[:, CV_MQG] = f(inp["mq_norm_g"][0]); cvec[:, CV_MKG] = f(inp["mk_norm_g"][0])
    sinks = f(inp["attn_sinks"][0])
    order = []
    for g in range(4):
        order += [4 * g, 4 * g + 2, 4 * g + 1, 4 * g + 3]
    cvec[:, CV_SINK:CV_SINK + 16] = sinks[order][None, :]
    dw = f(inp["conv_dw_w"][0])[:, 0, :]
    for c in range(4):
        cvec[:, CV_DW + c * 31:CV_DW + (c + 1) * 31] = dw[:, c * 128:(c + 1) * 128].T
    cvec[:, CV_DWB:CV_DWB + 4] = fm(inp["conv_dw_b"][0], 4)
    cvec[:, CV_LNG:CV_LNG + 4] = fm(inp["conv_ln_g"][0], 4)
    cvec[:, CV_LNB:CV_LNB + 4] = fm(inp["conv_ln_b"][0], 4)
    inv_freq = (10000.0 ** (-np.arange(0, 64, 2, dtype=np.float32) / 64)).astype(np.float32)
    cvec[:, CV_INVF] = inv_freq[p % 32]
    cvec[:, CV_SGN] = np.where((p % 64) < 32, -1.0, 1.0)
    cmat = np.zeros((128, 384), np.float32)
    cmat[:, 0:128] = np.eye(128)
    cmat[:, 128:256] = 1.0
    cmat[0:64, 256:320] = 1.0
    cmat[64:128, 320:384] = 1.0
    kk = np.arange(128)[:, None]; qq = np.arange(128)[None, :]
    mprev = (kk > qq).astype(np.float32); mcur = (kk <= qq).astype(np.float32)
    def slabs(w, n):
        return w[:, n * 128:(n + 1) * 128].reshape(-1, 128, 128)
    wa, wc, wm = f(inp["w_attn_o"][0]), f(inp["w_conv_o"][0]), f(inp["w_mem_o"][0])
    wmerge = np.empty((16, 128, 64, 128), np.float32)
    for n in range(16):
        parts = [slabs(w_in[:, 3072:5120], n), slabs(wa, n), slabs(w_in[:, 5120:7168], n), slabs(wc, n),
                 slabs(w_in[:, 7168:9216], n), slabs(wm, n)]
        wmerge[n] = np.concatenate(parts, axis=0).transpose(1, 0, 2)
    wmerge = wmerge.reshape(16, 128, 8192)
    common = dict(w_in=np.ascontiguousarray(w_in[:, 0:3072]), wqk_sw=wqk_sw, wkdup=wkdup, wmerge=wmerge,
                  w_out=f(inp["w_out"][0]), w_mem_kv=f(inp["w_mem_kv"][0]),
                  mem=f(inp["mem"][0]), cvec=cvec, cmat=cmat)
    if do_peer:
        common["w_query"] = f(inp["w_query"][0])
        sk = f(inp["sub_keys"][0]).reshape(16, 128, 128)
        common["skT"] = np.ascontiguousarray(sk.transpose(2, 0, 1).reshape(128, 16 * 128))
        common["uT"] = np.ascontiguousarray(f(inp["expert_u"][0]).T)
        common["ev"] = f(inp["expert_v"][0])
        common["iota"] = np.ascontiguousarray(np.tile(np.arange(128, dtype=np.float32)[None, :], (128, 1)))
    in_maps = []
    for c in range(NCORES):
        s0 = c * TTOT
        if c == 0:
            xhc = np.concatenate([np.zeros((128, D), np.float32), x[0:TTOT]], axis=0)
            posc = np.concatenate([np.zeros(128, np.int32), pos[0:TTOT]])
            mfirst = np.zeros_like(mprev)
        else:
            xhc = x[s0 - 128:s0 + TTOT]
            posc = pos[s0 - 128:s0 + TTOT]
            mfirst = mprev
        m = dict(common)
        m["xh"] = np.ascontiguousarray(xhc)
        m["posb"] = np.ascontiguousarray(posc[None, :])
        m["masks"] = np.ascontiguousarray(np.concatenate([np.tile(mprev, (1, 4)), np.tile(mcur, (1, 4)), np.tile(mfirst, (1, 4))], axis=1))
        in_maps.append(m)
    return in_maps


_CACHE = {}


def run(inp, NPASS, NT, do_peer=True, trace=False):
    key = (NPASS, NT, do_peer)
    if key not in _CACHE:
        _CACHE[key] = build_program(NPASS, NT, do_peer)
    nc = _CACHE[key]
    in_maps = host_prep(inp, NPASS, NT, do_peer)
    res = run_bass_kernel_spmd(nc, in_maps, core_ids=list(range(NCORES)), **({"trace": True} if trace else {}))
    out = np.concatenate([r["out"] for r in res.results], axis=0)
    return out[None].astype(np.float32), res


def kernel(**inputs):
    out, _ = run(inputs, 4, 4, True)
    return out
```

```python
import numpy as np
from contextlib import ExitStack
import concourse.bass as bass
import concourse.mybir as mybir
from concourse.bass_utils import run_bass_kernel_spmd

F32 = mybir.dt.float32
BF16 = mybir.dt.bfloat16
I32 = mybir.dt.int32
U32 = mybir.dt.uint32
AF = mybir.ActivationFunctionType
ALU = mybir.AluOpType
AX = mybir.AxisListType

NCORES = 8
D = 2048
KC = 16
EPS = 1e-6
TWO_PI = 2.0 * np.pi

CV_GMIX, CV_GFFN, CV_GMEM = 0, 16, 32
CV_GQ, CV_GK, CV_MQG, CV_MKG = 48, 50, 52, 53
CV_SINK = 54
CV_DW = 70
CV_DWB, CV_LNG, CV_LNB = 194, 198, 202
CV_INVF, CV_SGN = 206, 207
NCV = 208


class B:
    def __init__(s, nc, es):
        s.nc = nc
        s.es = es
        s.engs = {'pe': nc.tensor, 'act': nc.scalar, 'dve': nc.vector, 'pool': nc.gpsimd, 'sp': nc.sync}
        s.sems = {}
        s.cnt = {}
        s.seen = {e: {} for e in s.engs}
        s.lastw = {}
        s.readers = {}
        for e in ['pe', 'act', 'dve', 'pool']:
            s.newsem(e)
        s.same_sync = {'pe': False, 'act': True, 'dve': True, 'pool': True, 'sp': True}

    def newsem(s, name):
        if name not in s.sems:
            s.sems[name] = s.es.enter_context(s.nc.semaphore(name))
            s.cnt[name] = 0
        return name

    def op(s, e, fn, r=(), w=(), dsem=None):
        eng = s.engs[e]
        need = {}

        def add(ev):
            if ev is not None:
                need[ev[0]] = max(need.get(ev[0], 0), ev[1])

        for t in r:
            add(s.lastw.get(t))
        for t in w:
            add(s.lastw.get(t))
            for sm, v in s.readers.get(t, {}).items():
                add((sm, v))
        for sm, v in need.items():
            if s.seen[e].get(sm, 0) < v:
                eng.wait_ge(s.sems[sm], v)
                s.seen[e][sm] = v
        inst = fn(eng)
        if dsem is not None:
            sm, inc = dsem, 16
        else:
            sm, inc = e, 1
        s.cnt[sm] += inc
        inst.then_inc(s.sems[sm], inc)
        ev = (sm, s.cnt[sm])
        if dsem is None and not s.same_sync[e]:
            s.seen[e][sm] = s.cnt[sm]
        for t in w:
            s.lastw[t] = ev
            s.readers[t] = {}
        for t in r:
            d = s.readers.setdefault(t, {})
            d[sm] = max(d.get(sm, 0), ev[1])
        return ev

    def barrier(s):
        for e, eng in s.engs.items():
            for sm, c in s.cnt.items():
                if c > 0 and s.seen[e].get(sm, 0) < c:
                    eng.wait_ge(s.sems[sm], c)
                    s.seen[e][sm] = c


import os
class _Stop(Exception):
    pass


def ckpt(k):
    if int(os.environ.get("KSTOP", "99")) == k:
        raise _Stop()


def blocks_of(total, bs=512):
    out = []
    o = 0
    while o < total:
        out.append((o, min(bs, total - o)))
        o += bs
    return out


def build_program(NPASS, NT, do_peer=True, first_core_flag=None):
    TS = NT * 128
    TH = TS + 128
    TTOT = NPASS * TS
    nc = bass.Bass("TRN2", target_bir_lowering=False)
    dr = {}

    def din(name, shape, dt=F32):
        dr[name] = nc.dram_tensor(name, list(shape), dt, kind="ExternalInput").ap()
        return dr[name]

    xh = din("xh", [TTOT + 128, D])
    posb = din("posb", [1, TTOT + 128], I32)
    w_in = din("w_in", [D, 3072])
    wqk_sw = din("wqk_sw", [D, 1536])
    wkdup = din("wkdup", [D, 512])
    wmerge = din("wmerge", [16, 128, 8192])
    w_out = din("w_out", [D, D])
    w_mem_kv = din("w_mem_kv", [D, 1024])
    memx = din("mem", [256, D])
    cvec_d = din("cvec", [128, NCV])
    cmat_d = din("cmat", [128, 384])
    masks_d = din("masks", [128, 1536])
    if do_peer:
        w_query = din("w_query", [D, D])
        skT_d = din("skT", [128, 16 * 128])
        uT_d = din("uT", [D, 16384])
        ev_d = din("ev", [16384, D])
        iota_d = din("iota", [128, 128])
    out_d = nc.dram_tensor("out", [TTOT, D], F32, kind="ExternalOutput").ap()
    if do_peer:
        dr["gsc"] = nc.dram_tensor("gsc", [128, 128, TS], BF16, kind="Internal").ap()

    es = ExitStack()
    with es:
        b = B(nc, es)

        def sb(name, shape, dt=F32):
            return es.enter_context(nc.sbuf_tensor(name, list(shape), dt))

        xacc = sb("xacc", [128, NT, D])
        cvec = sb("cvec_s", [128, NCV])
        cmat = sb("cmat_s", [128, 384], BF16)
        masks = sb("masks_s", [128, 1536], BF16)
        esink = sb("esink", [128, 16])
        mkT = sb("mkT", [128, 4, 256], BF16)
        mvd = sb("mvd", [128, 2, 512], BF16)
        wbuf = [sb(f"wbuf{i}", [128, 16, 512], BF16) for i in range(2)]
        for i in range(2):
            b.newsem(f"w{i}")
        ps = [es.enter_context(nc.psum_tensor(f"ps{i}", [128, 512], F32)) for i in range(8)]
        ident = cmat[:, 0:128]
        ones = cmat[:, 128:256]
        bones = cmat[:, 256:384]
        b.newsem("cst")
        b.newsem("mxl")
        b.newsem("pl")
        for i in range(NT + 1):
            b.newsem(f"xl{i}")
        for i in range(NT):
            b.newsem(f"st{i}")
        for nm in ["gsp", "gl0", "gl1", "wx0", "wx1"]:
            b.newsem(nm)

        b.newsem("cst0")
        b.op('sp', lambda e: e.dma_start(out=cvec[:], in_=cvec_d), w=['cvec'], dsem='cst0')
        b.op('pool', lambda e: e.dma_start(out=cmat[:], in_=cmat_d), w=['cmat'], dsem='cst')
        b.op('pool', lambda e: e.dma_start(out=masks[:], in_=masks_d), w=['masks'], dsem='cst')
        fin = ('cst', b.cnt['cst'])
        for t in ['cmat', 'masks']:
            b.lastw[t] = fin
        b.op('act', lambda e: e.activation(out=esink[:], in_=cvec[:, CV_SINK:CV_SINK + 16], func=AF.Exp),
             r=['cvec'], w=['esink'])

        wstate = {'i': 0}

        def load_w(pieces):
            i = wstate['i'] % 2
            wstate['i'] += 1
            for dst_fn, src in pieces:
                b.op('pool', lambda e, dst_fn=dst_fn, src=src: e.dma_start(out=dst_fn(wbuf[i]), in_=src),
                     w=[f'wbuf{i}'], dsem=f'w{i}')
            return i

        def wsrc(w_ap, c0, ncols, k0=0, kcn=KC):
            return w_ap[k0 * 128:(k0 + kcn) * 128, c0:c0 + ncols].rearrange("(c p) n -> p c n", p=128)

        def mm_group(out_ap, pairs, rtoks, wtok):
            def fn(e):
                inst = None
                n = len(pairs)
                for j, (l, r_) in enumerate(pairs):
                    inst = e.matmul(out_ap, l, r_, start=(j == 0), stop=(j == n - 1))
                return inst
            return b.op('pe', fn, r=rtoks, w=[wtok])

        def norm_transpose(pfx, src_ap, src_tok, dstT, dst_col0, gcol, dst_tok, tmp_sq, tmp_hb, stat, pbanks):
            ssq = stat[:, 0:1]
            rs = stat[:, 1:2]
            b.op('act', lambda e: e.activation(out=tmp_sq, in_=src_ap, func=AF.Square, accum_out=ssq),
                 r=[src_tok], w=[pfx + 'sq', pfx + 'stat'])
            b.op('act', lambda e: e.activation(out=rs, in_=ssq, func=AF.Sqrt, bias=cvec_eps, scale=1.0 / D),
                 r=[pfx + 'stat', 'eps'], w=[pfx + 'stat2'])
            b.op('dve', lambda e: e.reciprocal(out=rs, in_=rs), r=[pfx + 'stat2'], w=[pfx + 'stat2'])
            b.op('act', lambda e: e.activation(out=tmp_hb, in_=src_ap, func=AF.Copy, scale=rs),
                 r=[src_tok, pfx + 'stat2'], w=[pfx + 'hb'])
            for half in range(2):
                pb = pbanks[half]
                pview = ps[pb][:].bitcast(BF16)

                def fn(e, half=half, pview=pview):
                    inst = None
                    for j in range(8):
                        kc = half * 8 + j
                        inst = e.transpose(pview[:, j * 128:(j + 1) * 128], tmp_hb[:, kc * 128:(kc + 1) * 128], ident)
                    return inst
                b.op('pe', fn, r=[pfx + 'hb', 'cmat'], w=[f'ps{pb}'])
                b.op('dve', lambda e, half=half, pview=pview: e.tensor_tensor(
                    out=dstT[:, half * 8:(half + 1) * 8, dst_col0:dst_col0 + 128],
                    in0=pview.rearrange("p (c t) -> p c t", c=8),
                    in1=cvec[:, gcol + half * 8:gcol + half * 8 + 8].unsqueeze(2).to_broadcast([128, 8, 128]),
                    op=ALU.mult), r=[f'ps{pb}', 'cvec'], w=[dst_tok])

        eps_t = sb("eps_t", [128, 1])
        b.op('dve', lambda e: e.memset(eps_t[:], EPS), w=['eps'])
        cvec_eps = eps_t[:, 0:1]

        def fm_pair_gemm(XT, xtok, ncols_tok, items, epilogue, banks):
            pend = None
            it = 0
            for idx, (slot, la, lb) in enumerate(items):
                for (c0, cw) in blocks_of(ncols_tok):
                    ba, bb = banks[it % len(banks)]
                    it += 1
                    mm_group(ps[ba][:, 0:cw], [(l, XT[:, kc, c0:c0 + cw]) for kc, l in enumerate(la)],
                             [xtok, f'wbuf{slot}'], f'ps{ba}')
                    if lb is not None:
                        mm_group(ps[bb][:, 0:cw], [(l, XT[:, kc, c0:c0 + cw]) for kc, l in enumerate(lb)],
                                 [xtok, f'wbuf{slot}'], f'ps{bb}')
                    if pend is not None:
                        epilogue(*pend)
                    pend = (idx, c0, cw, ba, bb)
            if pend is not None:
                epilogue(*pend)

        with ExitStack() as ms:
            def msb(name, shape, dt=F32):
                return ms.enter_context(nc.sbuf_tensor(name, list(shape), dt))
            memT = msb("memT", [128, 16, 256], BF16)
            mx = msb("mx", [128, D])
            msq = msb("msq", [128, D])
            mhb = msb("mhb", [128, D], BF16)
            mstat = msb("mstat", [128, 2])
            t_sq = msb("m_t_sq", [128, 256], BF16)
            t_rs = msb("m_t_rs", [128, 256])
            for ti in range(2):
                b.op('sp', lambda e, ti=ti: e.dma_start(out=mx[:], in_=memx[ti * 128:(ti + 1) * 128, :]),
                     w=['mx'], dsem='mxl')
                norm_transpose('m', mx[:], 'mx', memT, ti * 128, CV_GMEM, 'memT', msq[:], mhb[:], mstat, (0, 1))
            slot = load_w([(lambda wb: wb[:, :, 0:512], wsrc(w_mem_kv, 0, 512))])
            for h in range(4):
                mm_group(ps[2][:, 0:256], [(wbuf[slot][:, kc, h * 128:(h + 1) * 128], memT[:, kc, :]) for kc in range(KC)],
                         ['memT', f'wbuf{slot}'], 'ps2')
                b.op('act', lambda e: e.activation(out=t_sq[:], in_=ps[2][:, 0:256], func=AF.Square), r=['ps2'], w=['m_sq'])
                mm_group(ps[3][:, 0:256], [(ones, t_sq[:])], ['m_sq', 'cmat'], 'ps3')
                b.op('act', lambda e: e.activation(out=t_rs[:], in_=ps[3][:, 0:256], func=AF.Sqrt, bias=cvec_eps, scale=1.0 / 128),
                     r=['ps3', 'eps'], w=['m_rs'])
                b.op('dve', lambda e: e.reciprocal(out=t_rs[:], in_=t_rs[:]), r=['m_rs'], w=['m_rs'])
                b.op('dve', lambda e, h=h: e.scalar_tensor_tensor(out=mkT[:, h, :], in0=ps[2][:, 0:256],
                                                                  scalar=cvec[:, CV_MKG:CV_MKG + 1], in1=t_rs[:],
                                                                  op0=ALU.mult, op1=ALU.mult),
                     r=['ps2', 'm_rs', 'cvec'], w=['mkT'])
            slot = load_w([(lambda wb: wb[:, :, 0:512], wsrc(w_mem_kv, 512, 512))])
            for ti in range(2):
                mm_group(ps[4][:, :], [(memT[:, kc, ti * 128:(ti + 1) * 128], wbuf[slot][:, kc, :]) for kc in range(KC)],
                         ['memT', f'wbuf{slot}'], 'ps4')
                b.op('act', lambda e, ti=ti: e.activation(out=mvd[:, ti, :], in_=ps[4][:, :], func=AF.Copy), r=['ps4'], w=['mvd'])
            b.barrier()

        for pi in range(NPASS):
            tok0 = pi * TS
            with ExitStack() as ms:
              try:
                  def msb(name, shape, dt=F32):
                      return ms.enter_context(nc.sbuf_tensor(f"{name}_p{pi}", list(shape), dt))
                  hT = msb("hT", [128, 16, TH], BF16)
                  qT = msb("qT", [128, 8, TH], BF16)
                  kTA = msb("kTA", [128, 4, TH], BF16)
                  kTB = msb("kTB", [128, 4, TH], BF16)
                  b.op('dve', lambda e: e.memset(kTA[64:128, :, :], 0.0), w=['kT2'])
                  b.op('dve', lambda e: e.memset(kTB[0:64, :, :], 0.0), w=['kT2'])
                  vdup = msb("vdup", [128, NT + 1, 4, 128], BF16)
                  attnT = msb("attnT", [128, 8, TS], BF16)
                  arena = msb("arena", [128, 8 * TH + 16 * TS], BF16)
                  o1 = 8 * TH
                  o2 = o1 + 8 * TS
                  o3 = o2 + 4 * TS
                  gluT = arena[:, 0:o1].bitcast(F32).rearrange("p (c t) -> p c t", c=4)
                  cT = arena[:, o1:o2].bitcast(F32).rearrange("p (c t) -> p c t", c=4)
                  cbf = arena[:, o2:o3].rearrange("p (c t) -> p c t", c=4)
                  csq = arena[:, o3:o3 + 4 * TS].rearrange("p (c t) -> p c t", c=4)
                  mergedT = arena[:, 0:16 * TS].rearrange("p (c t) -> p c t", c=16)
                  convT = msb("convT", [128, 4, TS], BF16)
                  mqT = msb("mqT", [128, 4, TS], BF16)
                  memoT = msb("memoT", [128, 4, TS], BF16)
                  cosT = msb("cosT", [128, TH])
                  sinS = msb("sinS", [128, TH])
                  posi = msb("posi", [128, TH], I32)
                  xhalo = msb("xhalo", [128, D])
                  tsq = msb("tsq", [128, D], BF16)
                  thb = msb("thb", [128, D], BF16)
                  stat = msb("stat", [128, 2])
                  tA = [msb(f"tA{i}", [128, 512]) for i in range(4)]
                  tB = [msb(f"tB{i}", [128, 512], BF16) for i in range(4)]

                  b.op('sp', lambda e: e.dma_start(out=posi[:], in_=posb[:, tok0:tok0 + TH].partition_broadcast(128)),
                       w=['posi'], dsem='pl')
                  ang = cosT
                  kk = sinS
                  b.op('dve', lambda e: e.tensor_copy(out=ang[:], in_=posi[:]), r=['posi'], w=['cosT'])
                  b.op('dve', lambda e: e.tensor_scalar(out=ang[:], in0=ang[:], scalar1=cvec[:, CV_INVF:CV_INVF + 1], scalar2=None,
                                                        op0=ALU.mult), r=['cosT', 'cvec'], w=['cosT'])
                  MAGIC = 12582912.0
                  b.op('dve', lambda e: e.tensor_scalar(out=kk[:], in0=ang[:], scalar1=1.0 / TWO_PI, scalar2=MAGIC,
                                                        op0=ALU.mult, op1=ALU.add), r=['cosT'], w=['sinS'])
                  b.op('dve', lambda e: e.tensor_scalar(out=kk[:], in0=kk[:], scalar1=MAGIC, scalar2=None,
                                                        op0=ALU.subtract), r=['sinS'], w=['sinS'])
                  C1 = 6.28125
                  C2 = float(np.float32(TWO_PI - 6.28125))
                  C3 = float(TWO_PI - 6.28125 - np.float64(np.float32(TWO_PI - 6.28125)))
                  for cc in (C1, C2, C3):
                      b.op('dve', lambda e, cc=cc: e.scalar_tensor_tensor(out=ang[:], in0=kk[:], scalar=-cc, in1=ang[:],
                                                                          op0=ALU.mult, op1=ALU.add),
                           r=['sinS', 'cosT'], w=['cosT'])
                  PI_LO = 3.1415925
                  b.op('dve', lambda e: e.tensor_scalar(out=ang[:], in0=ang[:], scalar1=PI_LO, scalar2=-PI_LO,
                                                        op0=ALU.min, op1=ALU.max), r=['cosT'], w=['cosT'])
                  b.op('act', lambda e: e.activation(out=sinS[:], in_=ang[:], func=AF.Sin), r=['cosT'], w=['sinS'])
                  b.op('dve', lambda e: e.tensor_scalar(out=sinS[:], in0=sinS[:], scalar1=cvec[:, CV_SGN:CV_SGN + 1], scalar2=None,
                                                        op0=ALU.mult), r=['sinS', 'cvec'], w=['sinS'])
                  wr = tA[0]
                  b.op('dve', lambda e: e.tensor_scalar(out=ang[:], in0=ang[:], scalar1=float(np.pi / 2), scalar2=None,
                                                        op0=ALU.add), r=['cosT'], w=['cosT'])
                  for (c0, cw) in blocks_of(TH):
                      b.op('dve', lambda e, c0=c0, cw=cw: e.tensor_scalar(out=wr[:, 0:cw], in0=ang[:, c0:c0 + cw], scalar1=PI_LO,
                                                                          scalar2=-TWO_PI, op0=ALU.is_gt, op1=ALU.mult),
                           r=['cosT'], w=['tA0'])
                      b.op('dve', lambda e, c0=c0, cw=cw: e.tensor_tensor(out=ang[:, c0:c0 + cw], in0=ang[:, c0:c0 + cw],
                                                                          in1=wr[:, 0:cw], op=ALU.add),
                           r=['tA0', 'cosT'], w=['cosT'])
                  b.op('dve', lambda e: e.tensor_scalar(out=ang[:], in0=ang[:], scalar1=PI_LO, scalar2=-PI_LO,
                                                        op0=ALU.min, op1=ALU.max), r=['cosT'], w=['cosT'])
                  b.op('act', lambda e: e.activation(out=cosT[:], in_=ang[:], func=AF.Sin), r=['cosT'], w=['cosT'])

                  ckpt(1)
                  for ti in range(NT + 1):
                      if ti == 0:
                          dst, tok = xhalo[:], 'xhalo'
                      else:
                          dst, tok = xacc[:, ti - 1, :], f'xacc{ti - 1}'
                      b.op('sp', lambda e, dst=dst, ti=ti: e.dma_start(out=dst, in_=xh[tok0 + ti * 128: tok0 + (ti + 1) * 128, :]),
                           w=[tok], dsem=f'xl{ti}')
                      norm_transpose('x', dst, tok, hT, ti * 128, CV_GMIX, 'hT', tsq[:], thb[:], stat, (0, 1))

                  ckpt(2)
                  def qk_epilogue_factory(dstT, gcol, nblk_items):
                      def epi(idx, c0, cw, ba, bb):
                          sq = tB[0]
                          rs = tA[1]
                          t1 = tA[2]
                          t2 = tA[3]
                          b.op('act', lambda e: e.activation(out=sq[:, 0:cw], in_=ps[ba][:, 0:cw], func=AF.Square),
                               r=[f'ps{ba}'], w=['tB0'])
                          mm_group(ps[6][:, 0:cw], [(bones, sq[:, 0:cw])], ['tB0', 'cmat'], 'ps6')
                          b.op('act', lambda e: e.activation(out=rs[:, 0:cw], in_=ps[6][:, 0:cw], func=AF.Sqrt,
                                                             bias=cvec_eps, scale=1.0 / 64), r=['ps6', 'eps'], w=['tA1'])
                          b.op('dve', lambda e: e.reciprocal(out=rs[:, 0:cw], in_=rs[:, 0:cw]), r=['tA1'], w=['tA1'])
                          b.op('dve', lambda e: e.scalar_tensor_tensor(out=t1[:, 0:cw], in0=ps[ba][:, 0:cw],
                                                                       scalar=cvec[:, gcol:gcol + 1], in1=cosT[:, c0:c0 + cw],
                                                                       op0=ALU.mult, op1=ALU.mult),
                               r=[f'ps{ba}', 'cvec', 'cosT'], w=['tA2'])
                          b.op('dve', lambda e: e.scalar_tensor_tensor(out=t2[:, 0:cw], in0=ps[bb][:, 0:cw],
                                                                       scalar=cvec[:, gcol + 1:gcol + 2], in1=sinS[:, c0:c0 + cw],
                                                                       op0=ALU.mult, op1=ALU.mult),
                               r=[f'ps{bb}', 'cvec', 'sinS'], w=['tA3'])
                          b.op('dve', lambda e: e.tensor_tensor(out=t1[:, 0:cw], in0=t1[:, 0:cw], in1=t2[:, 0:cw], op=ALU.add),
                               r=['tA2', 'tA3'], w=['tA2'])
                          if isinstance(dstT, tuple):
                              for (dd, p0) in zip(dstT, (0, 64)):
                                  b.op('dve', lambda e, dd=dd, p0=p0: e.tensor_tensor(out=dd[p0:p0 + 64, idx, c0:c0 + cw], in0=t1[p0:p0 + 64, 0:cw],
                                                                                      in1=rs[p0:p0 + 64, 0:cw], op=ALU.mult),
                                       r=['tA2', 'tA1'], w=[nblk_items])
                          else:
                              b.op('dve', lambda e: e.tensor_tensor(out=dstT[:, idx, c0:c0 + cw], in0=t1[:, 0:cw], in1=rs[:, 0:cw],
                                                                    op=ALU.mult), r=['tA2', 'tA1'], w=[nblk_items])
                      return epi

                  banks2 = [(2, 3), (4, 5)]
                  for blk in range(4):
                      slot = load_w([(lambda wb: wb[:, :, 0:256], wsrc(w_in, blk * 256, 256)),
                                     (lambda wb: wb[:, :, 256:512], wsrc(wqk_sw, blk * 256, 256))])
                      items = []
                      for j in range(2):
                          items.append((slot, [wbuf[slot][:, kc, j * 128:(j + 1) * 128] for kc in range(KC)],
                                        [wbuf[slot][:, kc, 256 + j * 128:256 + (j + 1) * 128] for kc in range(KC)]))
                      epi = qk_epilogue_factory(qT, CV_GQ, 'qT')
                      fm_pair_gemm(hT, 'hT', TH, items, lambda idx, c0, cw, ba, bb, blk=blk, epi=epi: epi(blk * 2 + idx, c0, cw, ba, bb), banks2)
                  for blk in range(2):
                      slot = load_w([(lambda wb: wb[:, :, 0:256], wsrc(wkdup, blk * 256, 256)),
                                     (lambda wb: wb[:, :, 256:512], wsrc(wqk_sw, 1024 + blk * 256, 256))])
                      items = []
                      for j in range(2):
                          items.append((slot, [wbuf[slot][:, kc, j * 128:(j + 1) * 128] for kc in range(KC)],
                                        [wbuf[slot][:, kc, 256 + j * 128:256 + (j + 1) * 128] for kc in range(KC)]))
                      epi = qk_epilogue_factory((kTA, kTB), CV_GK, 'kT2')
                      fm_pair_gemm(hT, 'hT', TH, items, lambda idx, c0, cw, ba, bb, blk=blk, epi=epi: epi(blk * 2 + idx, c0, cw, ba, bb), banks2)

                  ckpt(3)
                  slot = load_w([(lambda wb: wb[:, :, 0:256], wsrc(w_in, 1280, 256))])
                  for ti in range(NT + 1):
                      bk = 2 + (ti % 2)
                      mm_group(ps[bk][:, 0:256], [(hT[:, kc, ti * 128:(ti + 1) * 128], wbuf[slot][:, kc, 0:256]) for kc in range(KC)],
                               ['hT', f'wbuf{slot}'], f'ps{bk}')
                      for dup in range(2):
                          b.op('act' if dup == 0 else 'dve',
                               (lambda e, ti=ti, bk=bk: e.activation(out=vdup[:, ti, :, 0:64], in_=ps[bk][:, 0:256].rearrange("p (g d) -> p g d", g=4), func=AF.Copy))
                               if dup == 0 else
                               (lambda e, ti=ti, bk=bk: e.tensor_copy(out=vdup[:, ti, :, 64:128], in_=ps[bk][:, 0:256].rearrange("p (g d) -> p g d", g=4))),
                               r=[f'ps{bk}'], w=['vdup'])

                  ckpt(4)
                  def glu_epi(idx, c0, cw, ba, bb):
                      sg = tA[1]
                      b.op('act', lambda e: e.activation(out=sg[:, 0:cw], in_=ps[bb][:, 0:cw], func=AF.Sigmoid), r=[f'ps{bb}'], w=['tA1'])
                      b.op('dve', lambda e: e.tensor_tensor(out=gluT[:, idx, c0:c0 + cw], in0=ps[ba][:, 0:cw], in1=sg[:, 0:cw], op=ALU.mult),
                           r=[f'ps{ba}', 'tA1'], w=['gluT'])
                  for blk in range(2):
                      slot = load_w([(lambda wb: wb[:, :, 0:256], wsrc(w_in, 1536 + blk * 256, 256)),
                                     (lambda wb: wb[:, :, 256:512], wsrc(w_in, 2048 + blk * 256, 256))])
                      items = []
                      for j in range(2):
                          items.append((slot, [wbuf[slot][:, kc, j * 128:(j + 1) * 128] for kc in range(KC)],
                                        [wbuf[slot][:, kc, 256 + j * 128:256 + (j + 1) * 128] for kc in range(KC)]))
                      fm_pair_gemm(hT, 'hT', TH, items, lambda idx, c0, cw, ba, bb, blk=blk: glu_epi(blk * 2 + idx, c0, cw, ba, bb), banks2)
                  for c in range(4):
                      b.op('dve', lambda e, c=c: e.tensor_scalar(out=cT[:, c, :], in0=gluT[:, c, 98:98 + TS],
                                                                 scalar1=cvec[:, CV_DW + c * 31:CV_DW + c * 31 + 1],
                                                                 scalar2=cvec[:, CV_DWB + c:CV_DWB + c + 1], op0=ALU.mult, op1=ALU.add),
                           r=['gluT', 'cvec'], w=[f'cT{c}'])
                  for w_ in range(1, 31):
                      for c in range(4):
                          b.op('dve', lambda e, c=c, w_=w_: e.scalar_tensor_tensor(
                              out=cT[:, c, :], in0=gluT[:, c, 98 + w_:98 + w_ + TS],
                              scalar=cvec[:, CV_DW + c * 31 + w_:CV_DW + c * 31 + w_ + 1], in1=cT[:, c, :],
                              op0=ALU.mult, op1=ALU.add), r=['gluT', 'cvec', f'cT{c}'], w=[f'cT{c}'])
                  for c in range(4):
                      b.op('act', lambda e, c=c: e.activation(out=cbf[:, c, :], in_=cT[:, c, :], func=AF.Copy), r=[f'cT{c}'], w=['cbf'])
                      b.op('act', lambda e, c=c: e.activation(out=csq[:, c, :], in_=cT[:, c, :], func=AF.Square), r=[f'cT{c}'], w=['csq'])
                  for (c0, cw) in blocks_of(TS):
                      mm_group(ps[2][:, 0:cw], [(ones, cbf[:, c, c0:c0 + cw]) for c in range(4)], ['cbf', 'cmat'], 'ps2')
                      mm_group(ps[3][:, 0:cw], [(ones, csq[:, c, c0:c0 + cw]) for c in range(4)], ['csq', 'cmat'], 'ps3')
                      mean, msq_, rstd = tA[0], tA[1], tA[2]
                      b.op('dve', lambda e: e.tensor_scalar(out=mean[:, 0:cw], in0=ps[2][:, 0:cw], scalar1=1.0 / 512, scalar2=None, op0=ALU.mult),
                           r=['ps2'], w=['tA0'])
                      b.op('dve', lambda e: e.tensor_tensor(out=msq_[:, 0:cw], in0=mean[:, 0:cw], in1=mean[:, 0:cw], op=ALU.mult),
                           r=['tA0'], w=['tA1'])
                      b.op('dve', lambda e: e.scalar_tensor_tensor(out=rstd[:, 0:cw], in0=ps[3][:, 0:cw], scalar=1.0 / 512, in1=msq_[:, 0:cw],
                                                                   op0=ALU.mult, op1=ALU.subtract), r=['ps3', 'tA1'], w=['tA2'])
                      b.op('act', lambda e: e.activation(out=rstd[:, 0:cw], in_=rstd[:, 0:cw], func=AF.Sqrt, bias=cvec_eps, scale=1.0),
                           r=['tA2', 'eps'], w=['tA2'])
                      b.op('dve', lambda e: e.reciprocal(out=rstd[:, 0:cw], in_=rstd[:, 0:cw]), r=['tA2'], w=['tA2'])
                      for c in range(4):
                          xc = tA[3]
                          b.op('dve', lambda e, c=c: e.tensor_tensor(out=xc[:, 0:cw], in0=cT[:, c, c0:c0 + cw], in1=mean[:, 0:cw], op=ALU.subtract),
                               r=[f'cT{c}', 'tA0'], w=['tA3'])
                          b.op('dve', lambda e: e.tensor_tensor(out=xc[:, 0:cw], in0=xc[:, 0:cw], in1=rstd[:, 0:cw], op=ALU.mult),
                               r=['tA3', 'tA2'], w=['tA3'])
                          b.op('act', lambda e, c=c: e.activation(out=convT[:, c, c0:c0 + cw], in_=xc[:, 0:cw], func=AF.Silu,
                                                                  bias=cvec[:, CV_LNB + c:CV_LNB + c + 1], scale=cvec[:, CV_LNG + c:CV_LNG + c + 1]),
                               r=['tA3', 'cvec'], w=['convT'])

                  ckpt(5)
                  def mq_epi(idx, c0, cw, ba, bb):
                      sq = tB[0]
                      rs = tA[1]
                      b.op('act', lambda e: e.activation(out=sq[:, 0:cw], in_=ps[ba][:, 0:cw], func=AF.Square), r=[f'ps{ba}'], w=['tB0'])
                      mm_group(ps[6][:, 0:cw], [(ones, sq[:, 0:cw])], ['tB0', 'cmat'], 'ps6')
                      b.op('act', lambda e: e.activation(out=rs[:, 0:cw], in_=ps[6][:, 0:cw], func=AF.Sqrt, bias=cvec_eps, scale=1.0 / 128),
                           r=['ps6', 'eps'], w=['tA1'])
                      b.op('dve', lambda e: e.reciprocal(out=rs[:, 0:cw], in_=rs[:, 0:cw]), r=['tA1'], w=['tA1'])
                      b.op('dve', lambda e: e.scalar_tensor_tensor(out=mqT[:, idx, c0:c0 + cw], in0=ps[ba][:, 0:cw],
                                                                   scalar=cvec[:, CV_MQG:CV_MQG + 1], in1=rs[:, 0:cw],
                                                                   op0=ALU.mult, op1=ALU.mult), r=[f'ps{ba}', 'tA1', 'cvec'], w=['mqT'])
                  slot = load_w([(lambda wb: wb[:, :, 0:512], wsrc(w_in, 2560, 512))])
                  items = [(slot, [wbuf[slot][:, kc, j * 128:(j + 1) * 128] for kc in range(KC)], None) for j in range(4)]
                  hT_own = hT[:, :, 128:TH]
                  fm_pair_gemm(hT_own, 'hT', TS, items, mq_epi, banks2)

                  ckpt(6)
                  def attn_core(st_pairs_fn, nkb, v_lhsT_fn, mask_fn, scale, den_extra, out_fn, rtoks, cw=512):
                      for kb in range(nkb):
                          bk = 2 + kb
                          def fn(e, kb=kb, bk=bk):
                              inst = None
                              for (oc0, ocw, l, r_) in st_pairs_fn(kb):
                                  inst = e.matmul(ps[bk][:, oc0:oc0 + ocw], l, r_, start=True, stop=True)
                              return inst
                          b.op('pe', fn, r=rtoks, w=[f'ps{bk}'])
                          b.op('act', lambda e, kb=kb, bk=bk: e.activation(out=tB[kb][:, 0:cw], in_=ps[bk][:, 0:cw], func=AF.Exp, scale=scale),
                               r=[f'ps{bk}'], w=[f'tB{kb}'])
                          m = mask_fn(kb)
                          if m is not None:
                              b.op('dve', lambda e, kb=kb, m=m: e.tensor_tensor(out=tB[kb][:, 0:cw], in0=tB[kb][:, 0:cw], in1=m, op=ALU.mult),
                                   r=[f'tB{kb}', 'masks'], w=[f'tB{kb}'])
                      mm_group(ps[4][:, 0:cw], [(v_lhsT_fn(kb), tB[kb][:, 0:cw]) for kb in range(nkb)],
                               [f'tB{kb}' for kb in range(nkb)] + rtoks, 'ps4')
                      mm_group(ps[5][:, 0:cw], [(ones, tB[kb][:, 0:cw]) for kb in range(nkb)],
                               [f'tB{kb}' for kb in range(nkb)] + ['cmat'], 'ps5')
                      rden = tA[0]
                      if den_extra is not None:
                          b.op('dve', lambda e: e.tensor_tensor(out=rden[:, 0:cw].rearrange("p (h q) -> p h q", h=4), in0=ps[5][:, 0:cw].rearrange("p (h q) -> p h q", h=4),
                                                                in1=den_extra, op=ALU.add), r=['ps5', 'esink'], w=['tA0'])
                          b.op('dve', lambda e: e.reciprocal(out=rden[:, 0:cw], in_=rden[:, 0:cw]), r=['tA0'], w=['tA0'])
                      else:
                          b.op('dve', lambda e: e.reciprocal(out=rden[:, 0:cw], in_=ps[5][:, 0:cw]), r=['ps5'], w=['tA0'])
                      out_fn(rden)

                  for n in range(NT):
                      qc0 = 128 * (n + 1)
                      for g in range(4):
                          def st_pairs(kb, n=n, g=g, qc0=qc0):
                              kc0 = 128 * (n + kb)
                              return [(0, 256, kTA[:, g, kc0:kc0 + 128], qT[:, 2 * g:2 * g + 2, qc0:qc0 + 128]),
                                      (256, 256, kTB[:, g, kc0:kc0 + 128], qT[:, 2 * g:2 * g + 2, qc0:qc0 + 128])]

                          def mask_fn(kb, n=n):
                              if kb == 1:
                                  return masks[:, 512:1024]
                              if n == 0 and pi == 0:
                                  return masks[:, 1024:1536]
                              return masks[:, 0:512]

                          def out_fn(rden, n=n, g=g):
                              b.op('dve', lambda e: e.tensor_tensor(out=attnT[0:64, 2 * g:2 * g + 2, n * 128:(n + 1) * 128],
                                                                    in0=ps[4][0:64, 0:256].rearrange("p (h q) -> p h q", h=2),
                                                                    in1=rden[0:64, 0:256].rearrange("p (h q) -> p h q", h=2), op=ALU.mult),
                                   r=['ps4', 'tA0'], w=['attnT'])
                              b.op('dve', lambda e: e.tensor_tensor(out=attnT[64:128, 2 * g:2 * g + 2, n * 128:(n + 1) * 128],
                                                                    in0=ps[4][64:128, 256:512].rearrange("p (h q) -> p h q", h=2),
                                                                    in1=rden[64:128, 256:512].rearrange("p (h q) -> p h q", h=2), op=ALU.mult),
                                   r=['ps4', 'tA0'], w=['attnT'])
                          attn_core(st_pairs, 2, lambda kb, n=n, g=g: vdup[:, n + kb, g, :], mask_fn, 0.125,
                                    esink[:, 4 * g:4 * g + 4].unsqueeze(2).to_broadcast([128, 4, 128]), out_fn,
                                    ['qT', 'kT2', 'vdup'])
                  for h in range(4):
                      for (c0, cw) in blocks_of(TS):
                          def st_pairs(kb, h=h, c0=c0, cw=cw):
                              return [(0, cw, mkT[:, h, kb * 128:(kb + 1) * 128], mqT[:, h, c0:c0 + cw])]

                          def out_fn(rden, h=h, c0=c0, cw=cw):
                              b.op('dve', lambda e: e.tensor_tensor(out=memoT[:, h, c0:c0 + cw], in0=ps[4][:, 0:cw], in1=rden[:, 0:cw], op=ALU.mult),
                                   r=['ps4', 'tA0'], w=['memoT'])
                          attn_core(st_pairs, 2, lambda kb, h=h: mvd[:, kb, h * 128:(h + 1) * 128], lambda kb: None,
                                    float(128 ** -0.5), None, out_fn, ['mqT', 'mkT', 'mvd'], cw=cw)

                  ckpt(7)
                  b.barrier()
                  for n in range(16):
                      col = n * 128
                      i = wstate['i'] % 2
                      wv = lambda wb: wb[:].rearrange("p a (b c) -> p (a b) c", c=128)
                      slot = load_w([(lambda wb: wb[:].rearrange("p a n -> p (a n)").rearrange("p (c n) -> p c n", c=4),
                                      wmerge[n].rearrange("p (c n) -> p c n", c=4))])
                      wb = wv(wbuf[slot])
                      for (c0, cw) in blocks_of(TS):
                          wt = [f'wbuf{slot}']
                          mm_group(ps[0][:, 0:cw], [(wb[:, kc, :], hT[:, kc, 128 + c0:128 + c0 + cw]) for kc in range(16)], ['hT'] + wt, 'ps0')
                          mm_group(ps[1][:, 0:cw], [(wb[:, 16 + kc, :], attnT[:, kc, c0:c0 + cw]) for kc in range(8)], ['attnT'] + wt, 'ps1')
                          mm_group(ps[2][:, 0:cw], [(wb[:, 24 + kc, :], hT[:, kc, 128 + c0:128 + c0 + cw]) for kc in range(16)], ['hT'] + wt, 'ps2')
                          mm_group(ps[3][:, 0:cw], [(wb[:, 40 + kc, :], convT[:, kc, c0:c0 + cw]) for kc in range(4)], ['convT'] + wt, 'ps3')
                          mm_group(ps[4][:, 0:cw], [(wb[:, 44 + kc, :], hT[:, kc, 128 + c0:128 + c0 + cw]) for kc in range(16)], ['hT'] + wt, 'ps4')
                          mm_group(ps[5][:, 0:cw], [(wb[:, 60 + kc, :], memoT[:, kc, c0:c0 + cw]) for kc in range(4)], ['memoT'] + wt, 'ps5')
                          sg, m1, m2 = tA[0], tA[1], tA[2]
                          for bi, (gb, ob) in enumerate([(0, 1), (2, 3), (4, 5)]):
                              b.op('act', lambda e, gb=gb: e.activation(out=sg[:, 0:cw], in_=ps[gb][:, 0:cw], func=AF.Sigmoid), r=[f'ps{gb}'], w=['tA0'])
                              dst = m1 if bi == 0 else m2
                              b.op('dve', lambda e, ob=ob, dst=dst: e.tensor_tensor(out=dst[:, 0:cw], in0=ps[ob][:, 0:cw], in1=sg[:, 0:cw], op=ALU.mult),
                                   r=[f'ps{ob}', 'tA0'], w=['tA1' if bi == 0 else 'tA2'])
                              if bi == 1:
                                  b.op('dve', lambda e: e.tensor_tensor(out=m1[:, 0:cw], in0=m1[:, 0:cw], in1=m2[:, 0:cw], op=ALU.add),
                                       r=['tA1', 'tA2'], w=['tA1'])
                              if bi == 2:
                                  b.op('dve', lambda e, n=n: e.tensor_tensor(out=mergedT[:, n, c0:c0 + cw], in0=m1[:, 0:cw], in1=m2[:, 0:cw], op=ALU.add),
                                       r=['tA1', 'tA2'], w=['mergedT'])

                  ckpt(8)
                  for nb in range(4):
                      slot = load_w([(lambda wb: wb[:, :, 0:512], wsrc(w_out, nb * 512, 512))])
                      for ti in range(NT):
                          bk = 6 + (ti % 2)
                          mm_group(ps[bk][:, :], [(mergedT[:, kc, ti * 128:(ti + 1) * 128], wbuf[slot][:, kc, :]) for kc in range(KC)],
                                   ['mergedT', f'wbuf{slot}'], f'ps{bk}')
                          b.op('dve', lambda e, ti=ti, nb=nb, bk=bk: e.tensor_tensor(out=xacc[:, ti, nb * 512:(nb + 1) * 512],
                                                                                     in0=xacc[:, ti, nb * 512:(nb + 1) * 512], in1=ps[bk][:, :], op=ALU.add),
                               r=[f'ps{bk}', f'xacc{ti}'], w=[f'xacc{ti}'])
                  b.barrier()

              except _Stop:
                b.barrier()
            if do_peer:
                peer_phase(nc, b, ps, xacc, cvec, cmat, wbuf, load_w, wsrc, mm_group, norm_transpose, dr, NT, TS, cvec_eps, pi, fm_pair_gemm, wstate)
                b.barrier()

            for ti in range(NT):
                b.op('sp', lambda e, ti=ti: e.dma_start(out=out_d[tok0 + ti * 128: tok0 + (ti + 1) * 128, :], in_=xacc[:, ti, :]),
                     r=[f'xacc{ti}'], dsem=f'st{ti}')
            b.barrier()
        b.barrier()
    return nc


def peer_phase(nc, b, ps, xacc, cvec, cmat, wbuf, load_w, wsrc, mm_group, norm_transpose, dr, NT, TS, cvec_eps, pi, fm_pair_gemm, wstate):
    ident = cmat[:, 0:128]
    NEG = -1.0e30
    w_query, skT_d, uT_d, ev_d, iota_d, cmat_d = dr["w_query"], dr["skT"], dr["uT"], dr["ev"], dr["iota"], dr["cmat"]
    gsc = dr["gsc"]
    with ExitStack() as ms:
        def msb(name, shape, dt=F32):
            return ms.enter_context(nc.sbuf_tensor(f"{name}_q{pi}", list(shape), dt))
        hnT = msb("hnT", [128, 16, TS], BF16)
        iota_f = msb("iota_f", [128, 128])
        ident_f = msb("ident_f", [128, 128])
        b.op('sp', lambda e: e.dma_start(out=iota_f[:], in_=iota_d), w=['iota_f'], dsem='pl')
        b.op('sp', lambda e: e.dma_start(out=ident_f[:], in_=cmat_d[:, 0:128]), w=['ident_f'], dsem='pl')
        fin = ('pl', b.cnt['pl'])
        b.lastw['iota_f'] = fin
        b.lastw['ident_f'] = fin
        with ExitStack() as m1:
            def sb1(name, shape, dt=F32):
                return m1.enter_context(nc.sbuf_tensor(f"{name}_q{pi}", list(shape), dt))
            tsq = sb1("ptsq", [128, D])
            thb = sb1("pthb", [128, D], BF16)
            stat = sb1("pstat", [128, 2])
            qpT = sb1("qpT", [128, 16, TS], BF16)
            skT = sb1("skT", [128, 16, 128], BF16)
            Gst = sb1("Gst", [128, 128, 128], BF16)
            jhot = sb1("jhot", [128, 4, 128], BF16)
            gih = sb1("gih", [128, 4, 128], BF16)
            jsq = sb1("jsq", [128, 4, 128])
            one_t = sb1("one_t", [128, 1])
            b.op('dve', lambda e: e.memset(one_t[:], 1.0), w=['one_t'])
            s12x = [sb1(f"s12_{i}", [128, 256]) for i in range(2)]
            s12bx = [sb1(f"s12b_{i}", [128, 256]) for i in range(2)]
            cand2x = [sb1(f"cand2_{i}", [128, 256]) for i in range(2)]
            vals = sb1("vals", [128, 16, 16])
            idxu = sb1("idxu", [128, 16, 16], U32)
            idxf = sb1("idxf", [128, 16, 16])
            cand = sb1("cand", [128, 8, 256])
            cand2 = sb1("cand2", [128, 256])
            cvals = sb1("cvals", [128, 8, 16])
            cposu = sb1("cposu", [128, 8, 16], U32)
            au = sb1("au", [128, 128], U32)
            bu = sb1("bu", [128, 128], U32)
            af = sb1("af", [128, 128])
            bf_ = sb1("bf_", [128, 128])
            eq = sb1("eq", [128, 128, 16])
            negm = sb1("negm", [128, 8])
            gsum = sb1("gsum", [128, 8])
            Itm = sb1("Itm", [128, 128])
            Jtm = sb1("Jtm", [128, 128])
            Gtm = sb1("Gtm", [128, 128])
            iT = sb1("iT", [128, 128])
            jT = sb1("jT", [128, 128])
            gT = sb1("gT", [128, 128])
            b.op('pool', lambda e: e.dma_start(out=skT[:], in_=skT_d.rearrange("p (c k) -> p c k", c=16)), w=['skT'], dsem='cst')
            for ti in range(NT):
                norm_transpose('h', xacc[:, ti, :], f'xacc{ti}', hnT, ti * 128, CV_GFFN, 'hnT', tsq[:], thb[:], stat, (0, 1))
            def q_epi(idx, c0, cw, ba, bb):
                b.op('act', lambda e: e.activation(out=qpT[:, q_epi.base + idx, c0:c0 + cw], in_=ps[ba][:, 0:cw], func=AF.Copy),
                     r=[f'ps{ba}'], w=['qpT'])
            for blk in range(4):
                slot = load_w([(lambda wb: wb[:, :, 0:512], wsrc(w_query, blk * 512, 512))])
                items = [(slot, [wbuf[slot][:, kc, j * 128:(j + 1) * 128] for kc in range(KC)], None) for j in range(4)]
                q_epi.base = blk * 4
                fm_pair_gemm(hnT, 'hnT', TS, items, q_epi, [(2, 3), (4, 5)])
            for ti in range(NT):
                tc = slice(ti * 128, (ti + 1) * 128)
                for h0 in range(0, 8, 2):
                    chains = []
                    for h in (h0, h0 + 1):
                        q_ = h % 2
                        bk = 2 + q_
                        s12h, s12bh = s12x[q_], s12bx[q_]
                        def fn(e, h=h, bk=bk):
                            e.matmul(ps[bk][:, 0:128], qpT[:, 2 * h, tc], skT[:, 2 * h, :], start=True, stop=True)
                            return e.matmul(ps[bk][:, 128:256], qpT[:, 2 * h + 1, tc], skT[:, 2 * h + 1, :], start=True, stop=True)
                        b.op('pe', fn, r=['qpT', 'skT'], w=[f'ps{bk}'])
                        b.op('act', lambda e, bk=bk, s12h=s12h: e.activation(out=s12h[:], in_=ps[bk][:, 0:256], func=AF.Copy), r=[f'ps{bk}'], w=[f's12_{q_}'])
                        for p in range(2):
                            hp = 2 * h + p
                            sv = s12h[:, p * 128:(p + 1) * 128]
                            sv2 = s12bh[:, p * 128:(p + 1) * 128]
                            vt, it_, st, st2 = f'vals{hp}', f'idxu{hp}', f's12_{q_}', f's12b_{q_}_{p}'
                            chains.append([
                                ('dve', lambda e, hp=hp, sv=sv: e.max(out=vals[:, hp, 0:8], in_=sv), [st], [vt + 'a']),
                                ('dve', lambda e, hp=hp, sv=sv: e.max_index(out=idxu[:, hp, 0:8], in_max=vals[:, hp, 0:8], in_values=sv), [st, vt + 'a'], [it_ + 'a']),
                                ('dve', lambda e, hp=hp, sv=sv, sv2=sv2: e.match_replace(out=sv2, in_to_replace=vals[:, hp, 0:8], in_values=sv, imm_value=NEG), [st, vt + 'a'], [st2]),
                                ('dve', lambda e, hp=hp, sv2=sv2: e.max(out=vals[:, hp, 8:16], in_=sv2), [st2], [vt + 'b']),
                                ('dve', lambda e, hp=hp, sv2=sv2: e.max_index(out=idxu[:, hp, 8:16], in_max=vals[:, hp, 8:16], in_values=sv2), [st2, vt + 'b'], [it_ + 'b']),
                            ])
                    for step in range(5):
                        for ch in chains:
                            eng_, f_, r_, w_ = ch[step]
                            b.op(eng_, f_, r=r_, w=w_)
                    chains = []
                    for h in (h0, h0 + 1):
                        cv = cand[:, h, :]
                        c2 = cand2x[h % 2]
                        vr = [f'vals{2 * h}a', f'vals{2 * h}b', f'vals{2 * h + 1}a', f'vals{2 * h + 1}b']
                        chains.append([
                            ('dve', lambda e, h=h, cv=cv: e.tensor_tensor(out=cv.rearrange("p (a c) -> p a c", a=16),
                                                                          in0=vals[:, 2 * h, :].unsqueeze(2).to_broadcast([128, 16, 16]),
                                                                          in1=vals[:, 2 * h + 1, :].unsqueeze(1).to_broadcast([128, 16, 16]), op=ALU.add), vr, [f'cand{h}']),
                            ('dve', lambda e, h=h, cv=cv: e.max(out=cvals[:, h, 0:8], in_=cv), [f'cand{h}'], [f'cvals{h}a']),
                            ('dve', lambda e, h=h, cv=cv: e.max_index(out=cposu[:, h, 0:8], in_max=cvals[:, h, 0:8], in_values=cv), [f'cand{h}', f'cvals{h}a'], [f'cposu{h}a']),
                            ('dve', lambda e, h=h, cv=cv, c2=c2: e.match_replace(out=c2[:], in_to_replace=cvals[:, h, 0:8], in_values=cv, imm_value=NEG),
                             [f'cand{h}', f'cvals{h}a'], [f'cand2_{h % 2}']),
                            ('dve', lambda e, h=h, c2=c2: e.max(out=cvals[:, h, 8:16], in_=c2[:]), [f'cand2_{h % 2}'], [f'cvals{h}b']),
                            ('dve', lambda e, h=h, c2=c2: e.max_index(out=cposu[:, h, 8:16], in_max=cvals[:, h, 8:16], in_values=c2[:]), [f'cand2_{h % 2}', f'cvals{h}b'], [f'cposu{h}b']),
                        ])
                    for step in range(6):
                        for ch in chains:
                            eng_, f_, r_, w_ = ch[step]
                            b.op(eng_, f_, r=r_, w=w_)
                ALLV = [f'vals{i}{x}' for i in range(16) for x in 'ab']
                ALLI = [f'idxu{i}{x}' for i in range(16) for x in 'ab']
                ALLC = [f'cvals{i}{x}' for i in range(8) for x in 'ab']
                ALLP = [f'cposu{i}{x}' for i in range(8) for x in 'ab']
                b.op('dve', lambda e: e.tensor_copy(out=idxf[:], in_=idxu[:]), r=ALLI, w=['idxf'])
                b.op('dve', lambda e: e.tensor_scalar(out=negm[:], in0=cvals[:, :, 0], scalar1=-1.0, scalar2=None, op0=ALU.mult), r=ALLC, w=['negm'])
                for h in range(8):
                    b.op('act', lambda e, h=h: e.activation(out=Gtm[:, h * 16:(h + 1) * 16], in_=cvals[:, h, :], func=AF.Exp, bias=negm[:, h:h + 1],
                                                            scale=1.0, accum_out=gsum[:, h:h + 1]), r=ALLC + ['negm'], w=['Gtm', 'gsum'])
                b.op('dve', lambda e: e.reciprocal(out=gsum[:], in_=gsum[:]), r=['gsum'], w=['gsum'])
                b.op('dve', lambda e: e.tensor_tensor(out=Gtm[:].rearrange("p (h r) -> p h r", h=8), in0=Gtm[:].rearrange("p (h r) -> p h r", h=8),
                                                      in1=gsum[:].unsqueeze(2).to_broadcast([128, 8, 16]), op=ALU.mult), r=['Gtm', 'gsum'], w=['Gtm'])
                cpu_ = cposu[:].rearrange("p h r -> p (h r)")
                b.op('dve', lambda e: e.tensor_single_scalar(out=au[:], in_=cpu_, scalar=4, op=ALU.logical_shift_right), r=ALLP, w=['au'])
                b.op('dve', lambda e: e.tensor_single_scalar(out=bu[:], in_=cpu_, scalar=15, op=ALU.bitwise_and), r=ALLP, w=['bu'])
                b.op('dve', lambda e: e.tensor_copy(out=af[:], in_=au[:]), r=['au'], w=['af'])
                b.op('dve', lambda e: e.tensor_copy(out=bf_[:], in_=bu[:]), r=['bu'], w=['bf_'])
                for (srcf, half, dst, dtok) in ((af, 0, Itm, 'Itm'), (bf_, 1, Jtm, 'Jtm')):
                    b.op('dve', lambda e, srcf=srcf: e.tensor_tensor(out=eq[:], in0=srcf[:].unsqueeze(2).to_broadcast([128, 128, 16]),
                                                                     in1=iota_f[:, 0:16].unsqueeze(1).to_broadcast([128, 128, 16]), op=ALU.is_equal),
                         r=['af', 'bf_', 'iota_f'], w=['eq'])
                    idx_h = idxf[:].rearrange("p (h t) a -> p h t a", t=2)[:, :, half, :]
                    b.op('dve', lambda e, idx_h=idx_h: e.tensor_tensor(out=eq[:].rearrange("p (h r) a -> p h r a", h=8),
                                                                       in0=eq[:].rearrange("p (h r) a -> p h r a", h=8),
                                                                       in1=idx_h.unsqueeze(2).to_broadcast([128, 8, 16, 16]), op=ALU.mult),
                         r=['eq', 'idxf'], w=['eq'])
                    b.op('dve', lambda e, dst=dst: e.tensor_reduce(out=dst[:], in_=eq[:], axis=AX.X, op=ALU.add), r=['eq'], w=[dtok])
                for (srct, stok, dstt, dtok, bk) in ((Itm, 'Itm', iT, 'iT', 4), (Jtm, 'Jtm', jT, 'jT', 5), (Gtm, 'Gtm', gT, 'gT', 6)):
                    b.op('pe', lambda e, srct=srct, bk=bk: e.transpose(ps[bk][:, 0:128], srct[:], ident_f[:]), r=[stok, 'ident_f'], w=[f'ps{bk}'])
                    b.op('act', lambda e, dstt=dstt, bk=bk, dtok=dtok: e.activation(out=dstt[:], in_=ps[bk][:, 0:128], func=AF.Copy, scale=(-1.0 if dtok == 'jT' else 1.0)),
                         r=[f'ps{bk}'], w=[dtok])
                for q4 in range(32):
                    bk = q4 % 2
                    for u in range(4):
                        t = q4 * 4 + u
                        b.op('act', lambda e, u=u, t=t: e.activation(out=jsq[:, u, :], in_=iota_f[:], func=AF.Square, bias=jT[:, t:t + 1], scale=1.0),
                             r=['iota_f', 'jT'], w=[f'jsq{u}'])
                    for u in range(4):
                        t = q4 * 4 + u
                        b.op('act', lambda e, u=u, t=t: e.activation(out=jhot[:, u, :], in_=jsq[:, u, :], func=AF.Relu, bias=one_t[:, 0:1], scale=-1.0),
                             r=[f'jsq{u}', 'one_t'], w=[f'jhot{u}'])
                    for u in range(4):
                        t = q4 * 4 + u
                        b.op('dve', lambda e, u=u, t=t: e.tensor_scalar(out=gih[:, u, :], in0=iota_f[:], scalar1=iT[:, t:t + 1], scalar2=gT[:, t:t + 1],
                                                                        op0=ALU.is_equal, op1=ALU.mult), r=['iota_f', 'iT', 'gT'], w=[f'gih{u}'])
                    def fn(e, bk=bk):
                        inst = None
                        for u in range(4):
                            inst = e.matmul(ps[bk][:, u * 128:(u + 1) * 128], jhot[:, u, :], gih[:, u, :], start=True, stop=True)
                        return inst
                    b.op('pe', fn, r=[f'jhot{u}' for u in range(4)] + [f'gih{u}' for u in range(4)], w=[f'ps{bk}'])
                    b.op('act', lambda e, bk=bk, q4=q4: e.activation(out=Gst[:, :, q4 * 4:(q4 + 1) * 4], in_=ps[bk][:, :].rearrange("p (t c) -> p c t", t=4), func=AF.Copy),
                         r=[f'ps{bk}'], w=[f'Gst{q4}'])
                for c8 in range(8):
                    b.op('sp', lambda e, c8=c8, ti=ti: e.dma_start(out=gsc[c8 * 16:(c8 + 1) * 16, :, ti * 128:(ti + 1) * 128].rearrange("c j t -> j c t"),
                                                                  in_=Gst[:, c8 * 16:(c8 + 1) * 16, :]), r=[f'Gst{q}' for q in range(32)], w=['gsc'], dsem='gsp')
            b.barrier()
        with ExitStack() as m2:
            def sb2(name, shape, dt=F32):
                return m2.enter_context(nc.sbuf_tensor(f"{name}_q{pi}", list(shape), dt))
            wb2 = [sb2(f"wbx{i}", [128, 16, 512], BF16) for i in range(2)]
            gTg = [sb2(f"gTg{i}", [128, 4, TS], BF16) for i in range(2)]
            actT = [sb2(f"actT{i}", [128, 4, TS], BF16) for i in range(2)]
            gel = [sb2(f"gel{i}", [128, 512], BF16) for i in range(2)]
            ob = 0
            for gi in range(32):
                par = gi % 2
                b.op('pool', lambda e, gi=gi, par=par: e.dma_start(out=wbuf[par][:], in_=wsrc(uT_d, gi * 512, 512)), w=[f'wbuf{par}'], dsem=f'w{par}')
                b.op('pool', lambda e, gi=gi, par=par: e.dma_start(out=wb2[par][:].rearrange("p a n -> p (a n)").rearrange("p (c n) -> p c n", c=4),
                                                                   in_=ev_d[gi * 512:(gi + 1) * 512, :].rearrange("(c p) n -> p c n", p=128)),
                     w=[f'wbx{par}'], dsem=f'wx{par}')
                b.op('sp', lambda e, gi=gi, par=par: e.dma_start(out=gTg[par][:], in_=gsc[gi * 4:(gi + 1) * 4, :, :].rearrange("c j t -> j c t")),
                     r=['gsc'], w=[f'gTg{par}'], dsem=f'gl{par}')
                vview = wb2[par][:].rearrange("p a n -> p (a n)").rearrange("p (c n) -> p c n", c=4)
                for cc in range(4):
                    for (c0, cw) in blocks_of(TS):
                        ab = cc % 2
                        mm_group(ps[ab][:, 0:cw], [(wbuf[par][:, kc, cc * 128:(cc + 1) * 128], hnT[:, kc, c0:c0 + cw]) for kc in range(KC)],
                                 ['hnT', f'wbuf{par}'], f'ps{ab}')
                        b.op('act', lambda e, ab=ab, cw=cw: e.activation(out=gel[ab][:, 0:cw], in_=ps[ab][:, 0:cw], func=AF.Gelu), r=[f'ps{ab}'], w=[f'gel{ab}'])
                        b.op('dve', lambda e, ab=ab, cc=cc, c0=c0, cw=cw, par=par: e.tensor_tensor(out=actT[par][:, cc, c0:c0 + cw], in0=gel[ab][:, 0:cw],
                                                                                                   in1=gTg[par][:, cc, c0:c0 + cw], op=ALU.mult),
                             r=[f'gel{ab}', f'gTg{par}'], w=[f'actT{par}'])
                for ti in range(NT):
                    for db in range(4):
                        bk = 2 + (ob % 4)
                        ob += 1
                        mm_group(ps[bk][:, :], [(actT[par][:, cc, ti * 128:(ti + 1) * 128], vview[:, cc, db * 512:(db + 1) * 512]) for cc in range(4)],
                                 [f'actT{par}', f'wbx{par}'], f'ps{bk}')
                        b.op('dve', lambda e, ti=ti, db=db, bk=bk: e.tensor_tensor(out=xacc[:, ti, db * 512:(db + 1) * 512], in0=xacc[:, ti, db * 512:(db + 1) * 512],
                                                                                   in1=ps[bk][:, :], op=ALU.add), r=[f'ps{bk}', f'xacc{ti}'], w=[f'xacc{ti}'])
            b.barrier()


def host_prep(inp, NPASS, NT, do_peer=True):
    TS = NT * 128
    TTOT = NPASS * TS
    f = lambda a: np.ascontiguousarray(np.asarray(a, dtype=np.float32))
    x = f(inp["x"])[0]
    pos = np.asarray(inp["positions"])[0].astype(np.int32)
    w_in = f(inp["w_in"][0])
    def swap_cols(w, nh):
        w4 = w.reshape(w.shape[0], nh, 2, 32)
        return np.ascontiguousarray(w4[:, :, ::-1, :].reshape(w.shape[0], nh * 64))
    wq = w_in[:, 0:1024]
    wk = w_in[:, 1024:1280]
    wk_sw = swap_cols(wk, 4)
    def dup(w):
        w3 = w.reshape(w.shape[0], 4, 1, 64)
        return np.ascontiguousarray(np.repeat(w3, 2, axis=2).reshape(w.shape[0], 512))
    wqk_sw = np.ascontiguousarray(np.concatenate([swap_cols(wq, 16), dup(wk_sw)], axis=1))
    wkdup = dup(wk)
    fm = lambda v, c: np.ascontiguousarray(f(v).reshape(c, 128).T)
    cvec = np.zeros((128, NCV), np.float32)
    cvec[:, CV_GMIX:CV_GMIX + 16] = fm(inp["g_mix"][0], 16)
    cvec[:, CV_GFFN:CV_GFFN + 16] = fm(inp["g_ffn"][0], 16)
    cvec[:, CV_GMEM:CV_GMEM + 16] = fm(inp["g_mem"][0], 16)
    p = np.arange(128)
    gq = f(inp["q_norm_g"][0]); gk = f(inp["k_norm_g"][0])
    cvec[:, CV_GQ] = gq[p % 64]; cvec[:, CV_GQ + 1] = gq[(p % 64 + 32) % 64]
    cvec[:, CV_GK] = gk[p % 64]; cvec[:, CV_GK + 1] = gk[(p % 64 + 32) % 64]
    cvec[:, CV_MQG] = f(inp["mq_norm_g"][0]); cvec[:, CV_MKG] = f(inp["mk_norm_g"][0])
    sinks = f(inp["attn_sinks"][0])
    order = []
    for g in range(4):
        order += [4 * g, 4 * g + 2, 4 * g + 1, 4 * g + 3]
    cvec[:, CV_SINK:CV_SINK + 16] = sinks[order][None, :]
    dw = f(inp["conv_dw_w"][0])[:, 0, :]
    for c in range(4):
        cvec[:, CV_DW + c * 31:CV_DW + (c + 1) * 31] = dw[:, c * 128:(c + 1) * 128].T
    cvec[:, CV_DWB:CV_DWB + 4] = fm(inp["conv_dw_b"][0], 4)
    cvec[:, CV_LNG:CV_LNG + 4] = fm(inp["conv_ln_g"][0], 4)
    cvec[:, CV_LNB:CV_LNB + 4] = fm(inp["conv_ln_b"][0], 4)
    inv_freq = (10000.0 ** (-np.arange(0, 64, 2, dtype=np.float32) / 64)).astype(np.float32)
    cvec[:, CV_INVF] = inv_freq[p % 32]
    cvec[:, CV_SGN] = np.where((p % 64) < 32, -1.0, 1.0)
    cmat = np.zeros((128, 384), np.float32)
    cmat[:, 0:128] = np.eye(128)
    cmat[:, 128:256] = 1.0
    cmat[0:64, 256:320] = 1.0
    cmat[64:128, 320:384] = 1.0
    kk = np.arange(128)[:, None]; qq = np.arange(128)[None, :]
    mprev = (kk > qq).astype(np.float32); mcur = (kk <= qq).astype(np.float32)
    def slabs(w, n):
        return w[:, n * 128:(n + 1) * 128].reshape(-1, 128, 128)
    wa, wc, wm = f(inp["w_attn_o"][0]), f(inp["w_conv_o"][0]), f(inp["w_mem_o"][0])
    wmerge = np.empty((16, 128, 64, 128), np.float32)
    for n in range(16):
        parts = [slabs(w_in[:, 3072:5120], n), slabs(wa, n), slabs(w_in[:, 5120:7168], n), slabs(wc, n),
                 slabs(w_in[:, 7168:9216], n), slabs(wm, n)]
        wmerge[n] = np.concatenate(parts, axis=0).transpose(1, 0, 2)
    wmerge = wmerge.reshape(16, 128, 8192)
    common = dict(w_in=np.ascontiguousarray(w_in[:, 0:3072]), wqk_sw=wqk_sw, wkdup=wkdup, wmerge=wmerge,
                  w_out=f(inp["w_out"][0]), w_mem_kv=f(inp["w_mem_kv"][0]),
                  mem=f(inp["mem"][0]), cvec=cvec, cmat=cmat)
    if do_peer:
        common["w_query"] = f(inp["w_query"][0])
        sk = f(inp["sub_keys"][0]).reshape(16, 128, 128)
        common["skT"] = np.ascontiguousarray(sk.transpose(2, 0, 1).reshape(128, 16 * 128))
        common["uT"] = np.ascontiguousarray(f(inp["expert_u"][0]).T)
        common["ev"] = f(inp["expert_v"][0])
        common["iota"] = np.ascontiguousarray(np.tile(np.arange(128, dtype=np.float32)[None, :], (128, 1)))
    in_maps = []
    for c in range(NCORES):
        s0 = c * TTOT
        if c == 0:
            xhc = np.concatenate([np.zeros((128, D), np.float32), x[0:TTOT]], axis=0)
            posc = np.concatenate([np.zeros(128, np.int32), pos[0:TTOT]])
            mfirst = np.zeros_like(mprev)
        else:
            xhc = x[s0 - 128:s0 + TTOT]
            posc = pos[s0 - 128:s0 + TTOT]
            mfirst = mprev
        m = dict(common)
        m["xh"] = np.ascontiguousarray(xhc)
        m["posb"] = np.ascontiguousarray(posc[None, :])
        m["masks"] = np.ascontiguousarray(np.concatenate([np.tile(mprev, (1, 4)), np.tile(mcur, (1, 4)), np.tile(mfirst, (1, 4))], axis=1))
        in_maps.append(m)
    return in_maps


_CACHE = {}


def run(inp, NPASS, NT, do_peer=True, trace=False):
    key = (NPASS, NT, do_peer)
    if key not in _CACHE:
        _CACHE[key] = build_program(NPASS, NT, do_peer)
    nc = _CACHE[key]
    in_maps = host_prep(inp, NPASS, NT, do_peer)
    res = run_bass_kernel_spmd(nc, in_maps, core_ids=list(range(NCORES)), **({"trace": True} if trace else {}))
    out = np.concatenate([r["out"] for r in res.results], axis=0)
    return out[None].astype(np.float32), res


def kernel(**inputs):
    out, _ = run(inputs, 4, 4, True)
    return out
```

```python
import numpy as np
from contextlib import ExitStack
import concourse.bass as bass
import concourse.mybir as mybir
from concourse.bass_utils import run_bass_kernel_spmd

F32 = mybir.dt.float32
BF16 = mybir.dt.bfloat16
I32 = mybir.dt.int32
U32 = mybir.dt.uint32
AF = mybir.ActivationFunctionType
ALU = mybir.AluOpType
AX = mybir.AxisListType

NCORES = 8
D = 2048
KC = 16
EPS = 1e-6
TWO_PI = 2.0 * np.pi

CV_GMIX, CV_GFFN, CV_GMEM = 0, 16, 32
CV_GQ, CV_GK, CV_MQG, CV_MKG = 48, 50, 52, 53
CV_SINK = 54
CV_DW = 70
CV_DWB, CV_LNG, CV_LNB = 194, 198, 202
CV_INVF, CV_SGN = 206, 207
NCV = 208


class B:
    def __init__(s, nc, es):
        s.nc = nc
        s.es = es
        s.engs = {'pe': nc.tensor, 'act': nc.scalar, 'dve': nc.vector, 'pool': nc.gpsimd, 'sp': nc.sync}
        s.sems = {}
        s.cnt = {}
        s.seen = {e: {} for e in s.engs}
        s.lastw = {}
        s.readers = {}
        for e in ['pe', 'act', 'dve', 'pool']:
            s.newsem(e)
        s.same_sync = {'pe': False, 'act': True, 'dve': True, 'pool': True, 'sp': True}

    def newsem(s, name):
        if name not in s.sems:
            s.sems[name] = s.es.enter_context(s.nc.semaphore(name))
            s.cnt[name] = 0
        return name

    def op(s, e, fn, r=(), w=(), dsem=None):
        eng = s.engs[e]
        need = {}

        def add(ev):
            if ev is not None:
                need[ev[0]] = max(need.get(ev[0], 0), ev[1])

        for t in r:
            add(s.lastw.get(t))
        for t in w:
            add(s.lastw.get(t))
            for sm, v in s.readers.get(t, {}).items():
                add((sm, v))
        for sm, v in need.items():
            if s.seen[e].get(sm, 0) < v:
                eng.wait_ge(s.sems[sm], v)
                s.seen[e][sm] = v
        inst = fn(eng)
        if dsem is not None:
            sm, inc = dsem, 16
        else:
            sm, inc = e, 1
        s.cnt[sm] += inc
        inst.then_inc(s.sems[sm], inc)
        ev = (sm, s.cnt[sm])
        if dsem is None and not s.same_sync[e]:
            s.seen[e][sm] = s.cnt[sm]
        for t in w:
            s.lastw[t] = ev
            s.readers[t] = {}
        for t in r:
            d = s.readers.setdefault(t, {})
            d[sm] = max(d.get(sm, 0), ev[1])
        return ev

    def barrier(s):
        for e, eng in s.engs.items():
            for sm, c in s.cnt.items():
                if c > 0 and s.seen[e].get(sm, 0) < c:
                    eng.wait_ge(s.sems[sm], c)
                    s.seen[e][sm] = c


import os
class _Stop(Exception):
    pass


def ckpt(k):
    if int(os.environ.get("KSTOP", "99")) == k:
        raise _Stop()


def blocks_of(total, bs=512):
    out = []
    o = 0
    while o < total:
        out.append((o, min(bs, total - o)))
        o += bs
    return out


def build_program(NPASS, NT, do_peer=True, first_core_flag=None):
    TS = NT * 128
    TH = TS + 128
    TTOT = NPASS * TS
    nc = bass.Bass("TRN2", target_bir_lowering=False)
    dr = {}

    def din(name, shape, dt=F32):
        dr[name] = nc.dram_tensor(name, list(shape), dt, kind="ExternalInput").ap()
        return dr[name]

    xh = din("xh", [TTOT + 128, D])
    posb = din("posb", [1, TTOT + 128], I32)
    w_in = din("w_in", [D, 3072])
    wqk_sw = din("wqk_sw", [D, 1536])
    wkdup = din("wkdup", [D, 512])
    wmerge = din("wmerge", [16, 128, 8192])
    w_out = din("w_out", [D, D])
    w_mem_kv = din("w_mem_kv", [D, 1024])
    memx = din("mem", [256, D])
    cvec_d = din("cvec", [128, NCV])
    cmat_d = din("cmat", [128, 384])
    masks_d = din("masks", [128, 1536])
    if do_peer:
        w_query = din("w_query", [D, D])
        skT_d = din("skT", [128, 16 * 128])
        uT_d = din("uT", [D, 16384])
        ev_d = din("ev", [16384, D])
        iota_d = din("iota", [128, 128])
    out_d = nc.dram_tensor("out", [TTOT, D], F32, kind="ExternalOutput").ap()
    if do_peer:
        dr["gsc"] = nc.dram_tensor("gsc", [128, 128, TS], BF16, kind="Internal").ap()

    es = ExitStack()
    with es:
        b = B(nc, es)

        def sb(name, shape, dt=F32):
            return es.enter_context(nc.sbuf_tensor(name, list(shape), dt))

        xacc = sb("xacc", [128, NT, D])
        cvec = sb("cvec_s", [128, NCV])
        cmat = sb("cmat_s", [128, 384], BF16)
        masks = sb("masks_s", [128, 1536], BF16)
        esink = sb("esink", [128, 16])
        mkT = sb("mkT", [128, 4, 256], BF16)
        mvd = sb("mvd", [128, 2, 512], BF16)
        wbuf = [sb(f"wbuf{i}", [128, 16, 512], BF16) for i in range(2)]
        for i in range(2):
            b.newsem(f"w{i}")
        ps = [es.enter_context(nc.psum_tensor(f"ps{i}", [128, 512], F32)) for i in range(8)]
        ident = cmat[:, 0:128]
        ones = cmat[:, 128:256]
        bones = cmat[:, 256:384]
        b.newsem("cst")
        b.newsem("mxl")
        b.newsem("pl")
        for i in range(NT + 1):
            b.newsem(f"xl{i}")
        for i in range(NT):
            b.newsem(f"st{i}")
        for nm in ["gsp", "gl0", "gl1", "wx0", "wx1"]:
            b.newsem(nm)

        b.newsem("cst0")
        b.op('sp', lambda e: e.dma_start(out=cvec[:], in_=cvec_d), w=['cvec'], dsem='cst0')
        b.op('pool', lambda e: e.dma_start(out=cmat[:], in_=cmat_d), w=['cmat'], dsem='cst')
        b.op('pool', lambda e: e.dma_start(out=masks[:], in_=masks_d), w=['masks'], dsem='cst')
        fin = ('cst', b.cnt['cst'])
        for t in ['cmat', 'masks']:
            b.lastw[t] = fin
        b.op('act', lambda e: e.activation(out=esink[:], in_=cvec[:, CV_SINK:CV_SINK + 16], func=AF.Exp),
             r=['cvec'], w=['esink'])

        wstate = {'i': 0}

        def load_w(pieces):
            i = wstate['i'] % 2
            wstate['i'] += 1
            for dst_fn, src in pieces:
                b.op('pool', lambda e, dst_fn=dst_fn, src=src: e.dma_start(out=dst_fn(wbuf[i]), in_=src),
                     w=[f'wbuf{i}'], dsem=f'w{i}')
            return i

        def wsrc(w_ap, c0, ncols, k0=0, kcn=KC):
            return w_ap[k0 * 128:(k0 + kcn) * 128, c0:c0 + ncols].rearrange("(c p) n -> p c n", p=128)

        def mm_group(out_ap, pairs, rtoks, wtok):
            def fn(e):
                inst = None
                n = len(pairs)
                for j, (l, r_) in enumerate(pairs):
                    inst = e.matmul(out_ap, l, r_, start=(j == 0), stop=(j == n - 1))
                return inst
            return b.op('pe', fn, r=rtoks, w=[wtok])

        def norm_transpose(pfx, src_ap, src_tok, dstT, dst_col0, gcol, dst_tok, tmp_sq, tmp_hb, stat, pbanks):
            ssq = stat[:, 0:1]
            rs = stat[:, 1:2]
            b.op('act', lambda e: e.activation(out=tmp_sq, in_=src_ap, func=AF.Square, accum_out=ssq),
                 r=[src_tok], w=[pfx + 'sq', pfx + 'stat'])
            b.op('act', lambda e: e.activation(out=rs, in_=ssq, func=AF.Sqrt, bias=cvec_eps, scale=1.0 / D),
                 r=[pfx + 'stat', 'eps'], w=[pfx + 'stat2'])
            b.op('dve', lambda e: e.reciprocal(out=rs, in_=rs), r=[pfx + 'stat2'], w=[pfx + 'stat2'])
            b.op('act', lambda e: e.activation(out=tmp_hb, in_=src_ap, func=AF.Copy, scale=rs),
                 r=[src_tok, pfx + 'stat2'], w=[pfx + 'hb'])
            for half in range(2):
                pb = pbanks[half]
                pview = ps[pb][:].bitcast(BF16)

                def fn(e, half=half, pview=pview):
                    inst = None
                    for j in range(8):
                        kc = half * 8 + j
                        inst = e.transpose(pview[:, j * 128:(j + 1) * 128], tmp_hb[:, kc * 128:(kc + 1) * 128], ident)
                    return inst
                b.op('pe', fn, r=[pfx + 'hb', 'cmat'], w=[f'ps{pb}'])
                b.op('dve', lambda e, half=half, pview=pview: e.tensor_tensor(
                    out=dstT[:, half * 8:(half + 1) * 8, dst_col0:dst_col0 + 128],
                    in0=pview.rearrange("p (c t) -> p c t", c=8),
                    in1=cvec[:, gcol + half * 8:gcol + half * 8 + 8].unsqueeze(2).to_broadcast([128, 8, 128]),
                    op=ALU.mult), r=[f'ps{pb}', 'cvec'], w=[dst_tok])

        eps_t = sb("eps_t", [128, 1])
        b.op('dve', lambda e: e.memset(eps_t[:], EPS), w=['eps'])
        cvec_eps = eps_t[:, 0:1]

        def fm_pair_gemm(XT, xtok, ncols_tok, items, epilogue, banks):
            pend = None
            it = 0
            for idx, (slot, la, lb) in enumerate(items):
                for (c0, cw) in blocks_of(ncols_tok):
                    ba, bb = banks[it % len(banks)]
                    it += 1
                    mm_group(ps[ba][:, 0:cw], [(l, XT[:, kc, c0:c0 + cw]) for kc, l in enumerate(la)],
                             [xtok, f'wbuf{slot}'], f'ps{ba}')
                    if lb is not None:
                        mm_group(ps[bb][:, 0:cw], [(l, XT[:, kc, c0:c0 + cw]) for kc, l in enumerate(lb)],
                                 [xtok, f'wbuf{slot}'], f'ps{bb}')
                    if pend is not None:
                        epilogue(*pend)
                    pend = (idx, c0, cw, ba, bb)
            if pend is not None:
                epilogue(*pend)

        with ExitStack() as ms:
            def msb(name, shape, dt=F32):
                return ms.enter_context(nc.sbuf_tensor(name, list(shape), dt))
            memT = msb("memT", [128, 16, 256], BF16)
            mx = msb("mx", [128, D])
            msq = msb("msq", [128, D])
            mhb = msb("mhb", [128, D], BF16)
            mstat = msb("mstat", [128, 2])
            t_sq = msb("m_t_sq", [128, 256], BF16)
            t_rs = msb("m_t_rs", [128, 256])
            for ti in range(2):
                b.op('sp', lambda e, ti=ti: e.dma_start(out=mx[:], in_=memx[ti * 128:(ti + 1) * 128, :]),
                     w=['mx'], dsem='mxl')
                norm_transpose('m', mx[:], 'mx', memT, ti * 128, CV_GMEM, 'memT', msq[:], mhb[:], mstat, (0, 1))
            slot = load_w([(lambda wb: wb[:, :, 0:512], wsrc(w_mem_kv, 0, 512))])
            for h in range(4):
                mm_group(ps[2][:, 0:256], [(wbuf[slot][:, kc, h * 128:(h + 1) * 128], memT[:, kc, :]) for kc in range(KC)],
                         ['memT', f'wbuf{slot}'], 'ps2')
                b.op('act', lambda e: e.activation(out=t_sq[:], in_=ps[2][:, 0:256], func=AF.Square), r=['ps2'], w=['m_sq'])
                mm_group(ps[3][:, 0:256], [(ones, t_sq[:])], ['m_sq', 'cmat'], 'ps3')
                b.op('act', lambda e: e.activation(out=t_rs[:], in_=ps[3][:, 0:256], func=AF.Sqrt, bias=cvec_eps, scale=1.0 / 128),
                     r=['ps3', 'eps'], w=['m_rs'])
                b.op('dve', lambda e: e.reciprocal(out=t_rs[:], in_=t_rs[:]), r=['m_rs'], w=['m_rs'])
                b.op('dve', lambda e, h=h: e.scalar_tensor_tensor(out=mkT[:, h, :], in0=ps[2][:, 0:256],
                                                                  scalar=cvec[:, CV_MKG:CV_MKG + 1], in1=t_rs[:],
                                                                  op0=ALU.mult, op1=ALU.mult),
                     r=['ps2', 'm_rs', 'cvec'], w=['mkT'])
            slot = load_w([(lambda wb: wb[:, :, 0:512], wsrc(w_mem_kv, 512, 512))])
            for ti in range(2):
                mm_group(ps[4][:, :], [(memT[:, kc, ti * 128:(ti + 1) * 128], wbuf[slot][:, kc, :]) for kc in range(KC)],
                         ['memT', f'wbuf{slot}'], 'ps4')
                b.op('act', lambda e, ti=ti: e.activation(out=mvd[:, ti, :], in_=ps[4][:, :], func=AF.Copy), r=['ps4'], w=['mvd'])
            b.barrier()

        for pi in range(NPASS):
            tok0 = pi * TS
            with ExitStack() as ms:
              try:
                  def msb(name, shape, dt=F32):
                      return ms.enter_context(nc.sbuf_tensor(f"{name}_p{pi}", list(shape), dt))
                  hT = msb("hT", [128, 16, TH], BF16)
                  qT = msb("qT", [128, 8, TH], BF16)
                  kTA = msb("kTA", [128, 4, TH], BF16)
                  kTB = msb("kTB", [128, 4, TH], BF16)
                  b.op('dve', lambda e: e.memset(kTA[64:128, :, :], 0.0), w=['kT2'])
                  b.op('dve', lambda e: e.memset(kTB[0:64, :, :], 0.0), w=['kT2'])
                  vdup = msb("vdup", [128, NT + 1, 4, 128], BF16)
                  attnT = msb("attnT", [128, 8, TS], BF16)
                  arena = msb("arena", [128, 8 * TH + 16 * TS], BF16)
                  o1 = 8 * TH
                  o2 = o1 + 8 * TS
                  o3 = o2 + 4 * TS
                  gluT = arena[:, 0:o1].bitcast(F32).rearrange("p (c t) -> p c t", c=4)
                  cT = arena[:, o1:o2].bitcast(F32).rearrange("p (c t) -> p c t", c=4)
                  cbf = arena[:, o2:o3].rearrange("p (c t) -> p c t", c=4)
                  csq = arena[:, o3:o3 + 4 * TS].rearrange("p (c t) -> p c t", c=4)
                  mergedT = arena[:, 0:16 * TS].rearrange("p (c t) -> p c t", c=16)
                  convT = msb("convT", [128, 4, TS], BF16)
                  mqT = msb("mqT", [128, 4, TS], BF16)
                  memoT = msb("memoT", [128, 4, TS], BF16)
                  cosT = msb("cosT", [128, TH])
                  sinS = msb("sinS", [128, TH])
                  posi = msb("posi", [128, TH], I32)
                  xhalo = msb("xhalo", [128, D])
                  tsq = msb("tsq", [128, D], BF16)
                  thb = msb("thb", [128, D], BF16)
                  stat = msb("stat", [128, 2])
                  tA = [msb(f"tA{i}", [128, 512]) for i in range(4)]
                  tB = [msb(f"tB{i}", [128, 512], BF16) for i in range(4)]

                  b.op('sp', lambda e: e.dma_start(out=posi[:], in_=posb[:, tok0:tok0 + TH].partition_broadcast(128)),
                       w=['posi'], dsem='pl')
                  ang = cosT
                  kk = sinS
                  b.op('dve', lambda e: e.tensor_copy(out=ang[:], in_=posi[:]), r=['posi'], w=['cosT'])
                  b.op('dve', lambda e: e.tensor_scalar(out=ang[:], in0=ang[:], scalar1=cvec[:, CV_INVF:CV_INVF + 1], scalar2=None,
                                                        op0=ALU.mult), r=['cosT', 'cvec'], w=['cosT'])
                  MAGIC = 12582912.0
                  b.op('dve', lambda e: e.tensor_scalar(out=kk[:], in0=ang[:], scalar1=1.0 / TWO_PI, scalar2=MAGIC,
                                                        op0=ALU.mult, op1=ALU.add), r=['cosT'], w=['sinS'])
                  b.op('dve', lambda e: e.tensor_scalar(out=kk[:], in0=kk[:], scalar1=MAGIC, scalar2=None,
                                                        op0=ALU.subtract), r=['sinS'], w=['sinS'])
                  C1 = 6.28125
                  C2 = float(np.float32(TWO_PI - 6.28125))
                  C3 = float(TWO_PI - 6.28125 - np.float64(np.float32(TWO_PI - 6.28125)))
                  for cc in (C1, C2, C3):
                      b.op('dve', lambda e, cc=cc: e.scalar_tensor_tensor(out=ang[:], in0=kk[:], scalar=-cc, in1=ang[:],
                                                                          op0=ALU.mult, op1=ALU.add),
                           r=['sinS', 'cosT'], w=['cosT'])
                  PI_LO = 3.1415925
                  b.op('dve', lambda e: e.tensor_scalar(out=ang[:], in0=ang[:], scalar1=PI_LO, scalar2=-PI_LO,
                                                        op0=ALU.min, op1=ALU.max), r=['cosT'], w=['cosT'])
                  b.op('act', lambda e: e.activation(out=sinS[:], in_=ang[:], func=AF.Sin), r=['cosT'], w=['sinS'])
                  b.op('dve', lambda e: e.tensor_scalar(out=sinS[:], in0=sinS[:], scalar1=cvec[:, CV_SGN:CV_SGN + 1], scalar2=None,
                                                        op0=ALU.mult), r=['sinS', 'cvec'], w=['sinS'])
                  wr = tA[0]
                  b.op('dve', lambda e: e.tensor_scalar(out=ang[:], in0=ang[:], scalar1=float(np.pi / 2), scalar2=None,
                                                        op0=ALU.add), r=['cosT'], w=['cosT'])
                  for (c0, cw) in blocks_of(TH):
                      b.op('dve', lambda e, c0=c0, cw=cw: e.tensor_scalar(out=wr[:, 0:cw], in0=ang[:, c0:c0 + cw], scalar1=PI_LO,
                                                                          scalar2=-TWO_PI, op0=ALU.is_gt, op1=ALU.mult),
                           r=['cosT'], w=['tA0'])
                      b.op('dve', lambda e, c0=c0, cw=cw: e.tensor_tensor(out=ang[:, c0:c0 + cw], in0=ang[:, c0:c0 + cw],
                                                                          in1=wr[:, 0:cw], op=ALU.add),
                           r=['tA0', 'cosT'], w=['cosT'])
                  b.op('dve', lambda e: e.tensor_scalar(out=ang[:], in0=ang[:], scalar1=PI_LO, scalar2=-PI_LO,
                                                        op0=ALU.min, op1=ALU.max), r=['cosT'], w=['cosT'])
                  b.op('act', lambda e: e.activation(out=cosT[:], in_=ang[:], func=AF.Sin), r=['cosT'], w=['cosT'])

                  ckpt(1)
                  for ti in range(NT + 1):
                      if ti == 0:
                          dst, tok = xhalo[:], 'xhalo'
                      else:
                          dst, tok = xacc[:, ti - 1, :], f'xacc{ti - 1}'
                      b.op('sp', lambda e, dst=dst, ti=ti: e.dma_start(out=dst, in_=xh[tok0 + ti * 128: tok0 + (ti + 1) * 128, :]),
                           w=[tok], dsem=f'xl{ti}')
                      norm_transpose('x', dst, tok, hT, ti * 128, CV_GMIX, 'hT', tsq[:], thb[:], stat, (0, 1))

                  ckpt(2)
                  def qk_epilogue_factory(dstT, gcol, nblk_items):
                      def epi(idx, c0, cw, ba, bb):
                          sq = tB[0]
                          rs = tA[1]
                          t1 = tA[2]
                          t2 = tA[3]
                          b.op('act', lambda e: e.activation(out=sq[:, 0:cw], in_=ps[ba][:, 0:cw], func=AF.Square),
                               r=[f'ps{ba}'], w=['tB0'])
                          mm_group(ps[6][:, 0:cw], [(bones, sq[:, 0:cw])], ['tB0', 'cmat'], 'ps6')
                          b.op('act', lambda e: e.activation(out=rs[:, 0:cw], in_=ps[6][:, 0:cw], func=AF.Sqrt,
                                                             bias=cvec_eps, scale=1.0 / 64), r=['ps6', 'eps'], w=['tA1'])
                          b.op('dve', lambda e: e.reciprocal(out=rs[:, 0:cw], in_=rs[:, 0:cw]), r=['tA1'], w=['tA1'])
                          b.op('dve', lambda e: e.scalar_tensor_tensor(out=t1[:, 0:cw], in0=ps[ba][:, 0:cw],
                                                                       scalar=cvec[:, gcol:gcol + 1], in1=cosT[:, c0:c0 + cw],
                                                                       op0=ALU.mult, op1=ALU.mult),
                               r=[f'ps{ba}', 'cvec', 'cosT'], w=['tA2'])
                          b.op('dve', lambda e: e.scalar_tensor_tensor(out=t2[:, 0:cw], in0=ps[bb][:, 0:cw],
                                                                       scalar=cvec[:, gcol + 1:gcol + 2], in1=sinS[:, c0:c0 + cw],
                                                                       op0=ALU.mult, op1=ALU.mult),
                               r=[f'ps{bb}', 'cvec', 'sinS'], w=['tA3'])
                          b.op('dve', lambda e: e.tensor_tensor(out=t1[:, 0:cw], in0=t1[:, 0:cw], in1=t2[:, 0:cw], op=ALU.add),
                               r=['tA2', 'tA3'], w=['tA2'])
                          if isinstance(dstT, tuple):
                              for (dd, p0) in zip(dstT, (0, 64)):
                                  b.op('dve', lambda e, dd=dd, p0=p0: e.tensor_tensor(out=dd[p0:p0 + 64, idx, c0:c0 + cw], in0=t1[p0:p0 + 64, 0:cw],
                                                                                      in1=rs[p0:p0 + 64, 0:cw], op=ALU.mult),
                                       r=['tA2', 'tA1'], w=[nblk_items])
                          else:
                              b.op('dve', lambda e: e.tensor_tensor(out=dstT[:, idx, c0:c0 + cw], in0=t1[:, 0:cw], in1=rs[:, 0:cw],
                                                                    op=ALU.mult), r=['tA2', 'tA1'], w=[nblk_items])
                      return epi

                  banks2 = [(2, 3), (4, 5)]
                  for blk in range(4):
                      slot = load_w([(lambda wb: wb[:, :, 0:256], wsrc(w_in, blk * 256, 256)),
                                     (lambda wb: wb[:, :, 256:512], wsrc(wqk_sw, blk * 256, 256))])
                      items = []
                      for j in range(2):
                          items.append((slot, [wbuf[slot][:, kc, j * 128:(j + 1) * 128] for kc in range(KC)],
                                        [wbuf[slot][:, kc, 256 + j * 128:256 + (j + 1) * 128] for kc in range(KC)]))
                      epi = qk_epilogue_factory(qT, CV_GQ, 'qT')
                      fm_pair_gemm(hT, 'hT', TH, items, lambda idx, c0, cw, ba, bb, blk=blk, epi=epi: epi(blk * 2 + idx, c0, cw, ba, bb), banks2)
                  for blk in range(2):
                      slot = load_w([(lambda wb: wb[:, :, 0:256], wsrc(wkdup, blk * 256, 256)),
                                     (lambda wb: wb[:, :, 256:512], wsrc(wqk_sw, 1024 + blk * 256, 256))])
                      items = []
                      for j in range(2):
                          items.append((slot, [wbuf[slot][:, kc, j * 128:(j + 1) * 128] for kc in range(KC)],
                                        [wbuf[slot][:, kc, 256 + j * 128:256 + (j + 1) * 128] for kc in range(KC)]))
                      epi = qk_epilogue_factory((kTA, kTB), CV_GK, 'kT2')
                      fm_pair_gemm(hT, 'hT', TH, items, lambda idx, c0, cw, ba, bb, blk=blk, epi=epi: epi(blk * 2 + idx, c0, cw, ba, bb), banks2)

                  ckpt(3)
                  slot = load_w([(lambda wb: wb[:, :, 0:256], wsrc(w_in, 1280, 256))])
                  for ti in range(NT + 1):
                      bk = 2 + (ti % 2)
                      mm_group(ps[bk][:, 0:256], [(hT[:, kc, ti * 128:(ti + 1) * 128], wbuf[slot][:, kc, 0:256]) for kc in range(KC)],
                               ['hT', f'wbuf{slot}'], f'ps{bk}')
                      for dup in range(2):
                          b.op('act' if dup == 0 else 'dve',
                               (lambda e, ti=ti, bk=bk: e.activation(out=vdup[:, ti, :, 0:64], in_=ps[bk][:, 0:256].rearrange("p (g d) -> p g d", g=4), func=AF.Copy))
                               if dup == 0 else
                               (lambda e, ti=ti, bk=bk: e.tensor_copy(out=vdup[:, ti, :, 64:128], in_=ps[bk][:, 0:256].rearrange("p (g d) -> p g d", g=4))),
                               r=[f'ps{bk}'], w=['vdup'])

                  ckpt(4)
                  def glu_epi(idx, c0, cw, ba, bb):
                      sg = tA[1]
                      b.op('act', lambda e: e.activation(out=sg[:, 0:cw], in_=ps[bb][:, 0:cw], func=AF.Sigmoid), r=[f'ps{bb}'], w=['tA1'])
                      b.op('dve', lambda e: e.tensor_tensor(out=gluT[:, idx, c0:c0 + cw], in0=ps[ba][:, 0:cw], in1=sg[:, 0:cw], op=ALU.mult),
                           r=[f'ps{ba}', 'tA1'], w=['gluT'])
                  for blk in range(2):
                      slot = load_w([(lambda wb: wb[:, :, 0:256], wsrc(w_in, 1536 + blk * 256, 256)),
                                     (lambda wb: wb[:, :, 256:512], wsrc(w_in, 2048 + blk * 256, 256))])
                      items = []
                      for j in range(2):
                          items.append((slot, [wbuf[slot][:, kc, j * 128:(j + 1) * 128] for kc in range(KC)],
                                        [wbuf[slot][:, kc, 256 + j * 128:256 + (j + 1) * 128] for kc in range(KC)]))
                      fm_pair_gemm(hT, 'hT', TH, items, lambda idx, c0, cw, ba, bb, blk=blk: glu_epi(blk * 2 + idx, c0, cw, ba, bb), banks2)
                  for c in range(4):
                      b.op('dve', lambda e, c=c: e.tensor_scalar(out=cT[:, c, :], in0=gluT[:, c, 98:98 + TS],
                                                                 scalar1=cvec[:, CV_DW + c * 31:CV_DW + c * 31 + 1],
                                                                 scalar2=cvec[:, CV_DWB + c:CV_DWB + c + 1], op0=ALU.mult, op1=ALU.add),
                           r=['gluT', 'cvec'], w=[f'cT{c}'])
                  for w_ in range(1, 31):
                      for c in range(4):
                          b.op('dve', lambda e, c=c, w_=w_: e.scalar_tensor_tensor(
                              out=cT[:, c, :], in0=gluT[:, c, 98 + w_:98 + w_ + TS],
                              scalar=cvec[:, CV_DW + c * 31 + w_:CV_DW + c * 31 + w_ + 1], in1=cT[:, c, :],
                              op0=ALU.mult, op1=ALU.add), r=['gluT', 'cvec', f'cT{c}'], w=[f'cT{c}'])
                  for c in range(4):
                      b.op('act', lambda e, c=c: e.activation(out=cbf[:, c, :], in_=cT[:, c, :], func=AF.Copy), r=[f'cT{c}'], w=['cbf'])
                      b.op('act', lambda e, c=c: e.activation(out=csq[:, c, :], in_=cT[:, c, :], func=AF.Square), r=[f'cT{c}'], w=['csq'])
                  for (c0, cw) in blocks_of(TS):
                      mm_group(ps[2][:, 0:cw], [(ones, cbf[:, c, c0:c0 + cw]) for c in range(4)], ['cbf', 'cmat'], 'ps2')
                      mm_group(ps[3][:, 0:cw], [(ones, csq[:, c, c0:c0 + cw]) for c in range(4)], ['csq', 'cmat'], 'ps3')
                      mean, msq_, rstd = tA[0], tA[1], tA[2]
                      b.op('dve', lambda e: e.tensor_scalar(out=mean[:, 0:cw], in0=ps[2][:, 0:cw], scalar1=1.0 / 512, scalar2=None, op0=ALU.mult),
                           r=['ps2'], w=['tA0'])
                      b.op('dve', lambda e: e.tensor_tensor(out=msq_[:, 0:cw], in0=mean[:, 0:cw], in1=mean[:, 0:cw], op=ALU.mult),
                           r=['tA0'], w=['tA1'])
                      b.op('dve', lambda e: e.scalar_tensor_tensor(out=rstd[:, 0:cw], in0=ps[3][:, 0:cw], scalar=1.0 / 512, in1=msq_[:, 0:cw],
                                                                   op0=ALU.mult, op1=ALU.subtract), r=['ps3', 'tA1'], w=['tA2'])
                      b.op('act', lambda e: e.activation(out=rstd[:, 0:cw], in_=rstd[:, 0:cw], func=AF.Sqrt, bias=cvec_eps, scale=1.0),
                           r=['tA2', 'eps'], w=['tA2'])
                      b.op('dve', lambda e: e.reciprocal(out=rstd[:, 0:cw], in_=rstd[:, 0:cw]), r=['tA2'], w=['tA2'])
                      for c in range(4):
                          xc = tA[3]
                          b.op('dve', lambda e, c=c: e.tensor_tensor(out=xc[:, 0:cw], in0=cT[:, c, c0:c0 + cw], in1=mean[:, 0:cw], op=ALU.subtract),
                               r=[f'cT{c}', 'tA0'], w=['tA3'])
                          b.op('dve', lambda e: e.tensor_tensor(out=xc[:, 0:cw], in0=xc[:, 0:cw], in1=rstd[:, 0:cw], op=ALU.mult),
                               r=['tA3', 'tA2'], w=['tA3'])
                          b.op('act', lambda e, c=c: e.activation(out=convT[:, c, c0:c0 + cw], in_=xc[:, 0:cw], func=AF.Silu,
                                                                  bias=cvec[:, CV_LNB + c:CV_LNB + c + 1], scale=cvec[:, CV_LNG + c:CV_LNG + c + 1]),
                               r=['tA3', 'cvec'], w=['convT'])

                  ckpt(5)
                  def mq_epi(idx, c0, cw, ba, bb):
                      sq = tB[0]
                      rs = tA[1]
                      b.op('act', lambda e: e.activation(out=sq[:, 0:cw], in_=ps[ba][:, 0:cw], func=AF.Square), r=[f'ps{ba}'], w=['tB0'])
                      mm_group(ps[6][:, 0:cw], [(ones, sq[:, 0:cw])], ['tB0', 'cmat'], 'ps6')
                      b.op('act', lambda e: e.activation(out=rs[:, 0:cw], in_=ps[6][:, 0:cw], func=AF.Sqrt, bias=cvec_eps, scale=1.0 / 128),
                           r=['ps6', 'eps'], w=['tA1'])
                      b.op('dve', lambda e: e.reciprocal(out=rs[:, 0:cw], in_=rs[:, 0:cw]), r=['tA1'], w=['tA1'])
                      b.op('dve', lambda e: e.scalar_tensor_tensor(out=mqT[:, idx, c0:c0 + cw], in0=ps[ba][:, 0:cw],
                                                                   scalar=cvec[:, CV_MQG:CV_MQG + 1], in1=rs[:, 0:cw],
                                                                   op0=ALU.mult, op1=ALU.mult), r=[f'ps{ba}', 'tA1', 'cvec'], w=['mqT'])
                  slot = load_w([(lambda wb: wb[:, :, 0:512], wsrc(w_in, 2560, 512))])
                  items = [(slot, [wbuf[slot][:, kc, j * 128:(j + 1) * 128] for kc in range(KC)], None) for j in range(4)]
                  hT_own = hT[:, :, 128:TH]
                  fm_pair_gemm(hT_own, 'hT', TS, items, mq_epi, banks2)

                  ckpt(6)
                  def attn_core(st_pairs_fn, nkb, v_lhsT_fn, mask_fn, scale, den_extra, out_fn, rtoks, cw=512):
                      for kb in range(nkb):
                          bk = 2 + kb
                          def fn(e, kb=kb, bk=bk):
                              inst = None
                              for (oc0, ocw, l, r_) in st_pairs_fn(kb):
                                  inst = e.matmul(ps[bk][:, oc0:oc0 + ocw], l, r_, start=True, stop=True)
                              return inst
                          b.op('pe', fn, r=rtoks, w=[f'ps{bk}'])
                          b.op('act', lambda e, kb=kb, bk=bk: e.activation(out=tB[kb][:, 0:cw], in_=ps[bk][:, 0:cw], func=AF.Exp, scale=scale),
                               r=[f'ps{bk}'], w=[f'tB{kb}'])
                          m = mask_fn(kb)
                          if m is not None:
                              b.op('dve', lambda e, kb=kb, m=m: e.tensor_tensor(out=tB[kb][:, 0:cw], in0=tB[kb][:, 0:cw], in1=m, op=ALU.mult),
                                   r=[f'tB{kb}', 'masks'], w=[f'tB{kb}'])
                      mm_group(ps[4][:, 0:cw], [(v_lhsT_fn(kb), tB[kb][:, 0:cw]) for kb in range(nkb)],
                               [f'tB{kb}' for kb in range(nkb)] + rtoks, 'ps4')
                      mm_group(ps[5][:, 0:cw], [(ones, tB[kb][:, 0:cw]) for kb in range(nkb)],
                               [f'tB{kb}' for kb in range(nkb)] + ['cmat'], 'ps5')
                      rden = tA[0]
                      if den_extra is not None:
                          b.op('dve', lambda e: e.tensor_tensor(out=rden[:, 0:cw].rearrange("p (h q) -> p h q", h=4), in0=ps[5][:, 0:cw].rearrange("p (h q) -> p h q", h=4),
                                                                in1=den_extra, op=ALU.add), r=['ps5', 'esink'], w=['tA0'])
                          b.op('dve', lambda e: e.reciprocal(out=rden[:, 0:cw], in_=rden[:, 0:cw]), r=['tA0'], w=['tA0'])
                      else:
                          b.op('dve', lambda e: e.reciprocal(out=rden[:, 0:cw], in_=ps[5][:, 0:cw]), r=['ps5'], w=['tA0'])
                      out_fn(rden)

                  for n in range(NT):
                      qc0 = 128 * (n + 1)
                      for g in range(4):
                          def st_pairs(kb, n=n, g=g, qc0=qc0):
                              kc0 = 128 * (n + kb)
                              return [(0, 256, kTA[:, g, kc0:kc0 + 128], qT[:, 2 * g:2 * g + 2, qc0:qc0 + 128]),
                                      (256, 256, kTB[:, g, kc0:kc0 + 128], qT[:, 2 * g:2 * g + 2, qc0:qc0 + 128])]

                          def mask_fn(kb, n=n):
                              if kb == 1:
                                  return masks[:, 512:1024]
                              if n == 0 and pi == 0:
                                  return masks[:, 1024:1536]
                              return masks[:, 0:512]

                          def out_fn(rden, n=n, g=g):
                              b.op('dve', lambda e: e.tensor_tensor(out=attnT[0:64, 2 * g:2 * g + 2, n * 128:(n + 1) * 128],
                                                                    in0=ps[4][0:64, 0:256].rearrange("p (h q) -> p h q", h=2),
                                                                    in1=rden[0:64, 0:256].rearrange("p (h q) -> p h q", h=2), op=ALU.mult),
                                   r=['ps4', 'tA0'], w=['attnT'])
                              b.op('dve', lambda e: e.tensor_tensor(out=attnT[64:128, 2 * g:2 * g + 2, n * 128:(n + 1) * 128],
                                                                    in0=ps[4][64:128, 256:512].rearrange("p (h q) -> p h q", h=2),
                                                                    in1=rden[64:128, 256:512].rearrange("p (h q) -> p h q", h=2), op=ALU.mult),
                                   r=['ps4', 'tA0'], w=['attnT'])
                          attn_core(st_pairs, 2, lambda kb, n=n, g=g: vdup[:, n + kb, g, :], mask_fn, 0.125,
                                    esink[:, 4 * g:4 * g + 4].unsqueeze(2).to_broadcast([128, 4, 128]), out_fn,
                                    ['qT', 'kT2', 'vdup'])
                  for h in range(4):
                      for (c0, cw) in blocks_of(TS):
                          def st_pairs(kb, h=h, c0=c0, cw=cw):
                              return [(0, cw, mkT[:, h, kb * 128:(kb + 1) * 128], mqT[:, h, c0:c0 + cw])]

                          def out_fn(rden, h=h, c0=c0, cw=cw):
                              b.op('dve', lambda e: e.tensor_tensor(out=memoT[:, h, c0:c0 + cw], in0=ps[4][:, 0:cw], in1=rden[:, 0:cw], op=ALU.mult),
                                   r=['ps4', 'tA0'], w=['memoT'])
                          attn_core(st_pairs, 2, lambda kb, h=h: mvd[:, kb, h * 128:(h + 1) * 128], lambda kb: None,
                                    float(128 ** -0.5), None, out_fn, ['mqT', 'mkT', 'mvd'], cw=cw)

                  ckpt(7)
                  b.barrier()
                  for n in range(16):
                      col = n * 128
                      i = wstate['i'] % 2
                      wv = lambda wb: wb[:].rearrange("p a (b c) -> p (a b) c", c=128)
                      slot = load_w([(lambda wb: wb[:].rearrange("p a n -> p (a n)").rearrange("p (c n) -> p c n", c=4),
                                      wmerge[n].rearrange("p (c n) -> p c n", c=4))])
                      wb = wv(wbuf[slot])
                      for (c0, cw) in blocks_of(TS):
                          wt = [f'wbuf{slot}']
                          mm_group(ps[0][:, 0:cw], [(wb[:, kc, :], hT[:, kc, 128 + c0:128 + c0 + cw]) for kc in range(16)], ['hT'] + wt, 'ps0')
                          mm_group(ps[1][:, 0:cw], [(wb[:, 16 + kc, :], attnT[:, kc, c0:c0 + cw]) for kc in range(8)], ['attnT'] + wt, 'ps1')
                          mm_group(ps[2][:, 0:cw], [(wb[:, 24 + kc, :], hT[:, kc, 128 + c0:128 + c0 + cw]) for kc in range(16)], ['hT'] + wt, 'ps2')
                          mm_group(ps[3][:, 0:cw], [(wb[:, 40 + kc, :], convT[:, kc, c0:c0 + cw]) for kc in range(4)], ['convT'] + wt, 'ps3')
                          mm_group(ps[4][:, 0:cw], [(wb[:, 44 + kc, :], hT[:, kc, 128 + c0:128 + c0 + cw]) for kc in range(16)], ['hT'] + wt, 'ps4')
                          mm_group(ps[5][:, 0:cw], [(wb[:, 60 + kc, :], memoT[:, kc, c0:c0 + cw]) for kc in range(4)], ['memoT'] + wt, 'ps5')
                          sg, m1, m2 = tA[0], tA[1], tA[2]
                          for bi, (gb, ob) in enumerate([(0, 1), (2, 3), (4, 5)]):
                              b.op('act', lambda e, gb=gb: e.activation(out=sg[:, 0:cw], in_=ps[gb][:, 0:cw], func=AF.Sigmoid), r=[f'ps{gb}'], w=['tA0'])
                              dst = m1 if bi == 0 else m2
                              b.op('dve', lambda e, ob=ob, dst=dst: e.tensor_tensor(out=dst[:, 0:cw], in0=ps[ob][:, 0:cw], in1=sg[:, 0:cw], op=ALU.mult),
                                   r=[f'ps{ob}', 'tA0'], w=['tA1' if bi == 0 else 'tA2'])
                              if bi == 1:
                                  b.op('dve', lambda e: e.tensor_tensor(out=m1[:, 0:cw], in0=m1[:, 0:cw], in1=m2[:, 0:cw], op=ALU.add),
                                       r=['tA1', 'tA2'], w=['tA1'])
                              if bi == 2:
                                  b.op('dve', lambda e, n=n: e.tensor_tensor(out=mergedT[:, n, c0:c0 + cw], in0=m1[:, 0:cw], in1=m2[:, 0:cw], op=ALU.add),
                                       r=['tA1', 'tA2'], w=['mergedT'])

                  ckpt(8)
                  for nb in range(4):
                      slot = load_w([(lambda wb: wb[:, :, 0:512], wsrc(w_out, nb * 512, 512))])
                      for ti in range(NT):
                          bk = 6 + (ti % 2)
                          mm_group(ps[bk][:, :], [(mergedT[:, kc, ti * 128:(ti + 1) * 128], wbuf[slot][:, kc, :]) for kc in range(KC)],
                                   ['mergedT', f'wbuf{slot}'], f'ps{bk}')
                          b.op('dve', lambda e, ti=ti, nb=nb, bk=bk: e.tensor_tensor(out=xacc[:, ti, nb * 512:(nb + 1) * 512],
                                                                                     in0=xacc[:, ti, nb * 512:(nb + 1) * 512], in1=ps[bk][:, :], op=ALU.add),
                               r=[f'ps{bk}', f'xacc{ti}'], w=[f'xacc{ti}'])
                  b.barrier()

              except _Stop:
                b.barrier()
            if do_peer:
                peer_phase(nc, b, ps, xacc, cvec, cmat, wbuf, load_w, wsrc, mm_group, norm_transpose, dr, NT, TS, cvec_eps, pi, fm_pair_gemm, wstate)
                b.barrier()

            for ti in range(NT):
                b.op('sp', lambda e, ti=ti: e.dma_start(out=out_d[tok0 + ti * 128: tok0 + (ti + 1) * 128, :], in_=xacc[:, ti, :]),
                     r=[f'xacc{ti}'], dsem=f'st{ti}')
            b.barrier()
        b.barrier()
    return nc


def peer_phase(nc, b, ps, xacc, cvec, cmat, wbuf, load_w, wsrc, mm_group, norm_transpose, dr, NT, TS, cvec_eps, pi, fm_pair_gemm, wstate):
    ident = cmat[:, 0:128]
    NEG = -1.0e30
    w_query, skT_d, uT_d, ev_d, iota_d, cmat_d = dr["w_query"], dr["skT"], dr["uT"], dr["ev"], dr["iota"], dr["cmat"]
    gsc = dr["gsc"]
    with ExitStack() as ms:
        def msb(name, shape, dt=F32):
            return ms.enter_context(nc.sbuf_tensor(f"{name}_q{pi}", list(shape), dt))
        hnT = msb("hnT", [128, 16, TS], BF16)
        iota_f = msb("iota_f", [128, 128])
        ident_f = msb("ident_f", [128, 128])
        b.op('sp', lambda e: e.dma_start(out=iota_f[:], in_=iota_d), w=['iota_f'], dsem='pl')
        b.op('sp', lambda e: e.dma_start(out=ident_f[:], in_=cmat_d[:, 0:128]), w=['ident_f'], dsem='pl')
        fin = ('pl', b.cnt['pl'])
        b.lastw['iota_f'] = fin
        b.lastw['ident_f'] = fin
        with ExitStack() as m1:
            def sb1(name, shape, dt=F32):
                return m1.enter_context(nc.sbuf_tensor(f"{name}_q{pi}", list(shape), dt))
            tsq = sb1("ptsq", [128, D])
            thb = sb1("pthb", [128, D], BF16)
            stat = sb1("pstat", [128, 2])
            qpT = sb1("qpT", [128, 16, TS], BF16)
            skT = sb1("skT", [128, 16, 128], BF16)
            Gst = sb1("Gst", [128, 128, 128], BF16)
            jhot = sb1("jhot", [128, 4, 128], BF16)
            gih = sb1("gih", [128, 4, 128], BF16)
            jsq = sb1("jsq", [128, 4, 128])
            iota_n = sb1("iota_n", [128, 128])
            b.op('dve', lambda e: e.tensor_scalar(out=iota_n[:], in0=iota_f[:], scalar1=-1.0, scalar2=None, op0=ALU.mult), r=['iota_f'], w=['iota_n'])
            one_t = sb1("one_t", [128, 1])
            b.op('dve', lambda e: e.memset(one_t[:], 1.0), w=['one_t'])
            s12x = [sb1(f"s12_{i}", [128, 256]) for i in range(2)]
            s12bx = [sb1(f"s12b_{i}", [128, 256]) for i in range(2)]
            cand2x = [sb1(f"cand2_{i}", [128, 256]) for i in range(2)]
            vals = sb1("vals", [128, 16, 16])
            idxu = sb1("idxu", [128, 16, 16], U32)
            idxf = sb1("idxf", [128, 16, 16])
            cand = sb1("cand", [128, 8, 256])
            cand2 = sb1("cand2", [128, 256])
            cvals = sb1("cvals", [128, 8, 16])
            cposu = sb1("cposu", [128, 8, 16], U32)
            au = sb1("au", [128, 128], U32)
            bu = sb1("bu", [128, 128], U32)
            af = sb1("af", [128, 128])
            bf_ = sb1("bf_", [128, 128])
            eq = sb1("eq", [128, 128, 16])
            negm = sb1("negm", [128, 8])
            gsum = sb1("gsum", [128, 8])
            Itm = sb1("Itm", [128, 128])
            Jtm = sb1("Jtm", [128, 128])
            Gtm = sb1("Gtm", [128, 128])
            iT = sb1("iT", [128, 128])
            jT = sb1("jT", [128, 128])
            gT = sb1("gT", [128, 128])
            b.op('pool', lambda e: e.dma_start(out=skT[:], in_=skT_d.rearrange("p (c k) -> p c k", c=16)), w=['skT'], dsem='cst')
            for ti in range(NT):
                norm_transpose('h', xacc[:, ti, :], f'xacc{ti}', hnT, ti * 128, CV_GFFN, 'hnT', tsq[:], thb[:], stat, (0, 1))
            def q_epi(idx, c0, cw, ba, bb):
                b.op('act', lambda e: e.activation(out=qpT[:, q_epi.base + idx, c0:c0 + cw], in_=ps[ba][:, 0:cw], func=AF.Copy),
                     r=[f'ps{ba}'], w=['qpT'])
            for blk in range(4):
                slot = load_w([(lambda wb: wb[:, :, 0:512], wsrc(w_query, blk * 512, 512))])
                items = [(slot, [wbuf[slot][:, kc, j * 128:(j + 1) * 128] for kc in range(KC)], None) for j in range(4)]
                q_epi.base = blk * 4
                fm_pair_gemm(hnT, 'hnT', TS, items, q_epi, [(2, 3), (4, 5)])
            for ti in range(NT):
                tc = slice(ti * 128, (ti + 1) * 128)
                for h0 in range(0, 8, 2):
                    chains = []
                    for h in (h0, h0 + 1):
                        q_ = h % 2
                        bk = 2 + q_
                        s12h, s12bh = s12x[q_], s12bx[q_]
                        def fn(e, h=h, bk=bk):
                            e.matmul(ps[bk][:, 0:128], qpT[:, 2 * h, tc], skT[:, 2 * h, :], start=True, stop=True)
                            return e.matmul(ps[bk][:, 128:256], qpT[:, 2 * h + 1, tc], skT[:, 2 * h + 1, :], start=True, stop=True)
                        b.op('pe', fn, r=['qpT', 'skT'], w=[f'ps{bk}'])
                        b.op('act', lambda e, bk=bk, s12h=s12h: e.activation(out=s12h[:], in_=ps[bk][:, 0:256], func=AF.Copy), r=[f'ps{bk}'], w=[f's12_{q_}'])
                        for p in range(2):
                            hp = 2 * h + p
                            sv = s12h[:, p * 128:(p + 1) * 128]
                            sv2 = s12bh[:, p * 128:(p + 1) * 128]
                            vt, it_, st, st2 = f'vals{hp}', f'idxu{hp}', f's12_{q_}', f's12b_{q_}_{p}'
                            chains.append([
                                ('dve', lambda e, hp=hp, sv=sv: e.max(out=vals[:, hp, 0:8], in_=sv), [st], [vt + 'a']),
                                ('dve', lambda e, hp=hp, sv=sv: e.max_index(out=idxu[:, hp, 0:8], in_max=vals[:, hp, 0:8], in_values=sv), [st, vt + 'a'], [it_ + 'a']),
                                ('dve', lambda e, hp=hp, sv=sv, sv2=sv2: e.match_replace(out=sv2, in_to_replace=vals[:, hp, 0:8], in_values=sv, imm_value=NEG), [st, vt + 'a'], [st2]),
                                ('dve', lambda e, hp=hp, sv2=sv2: e.max(out=vals[:, hp, 8:16], in_=sv2), [st2], [vt + 'b']),
                                ('dve', lambda e, hp=hp, sv2=sv2: e.max_index(out=idxu[:, hp, 8:16], in_max=vals[:, hp, 8:16], in_values=sv2), [st2, vt + 'b'], [it_ + 'b']),
                            ])
                    for step in range(5):
                        for ch in chains:
                            eng_, f_, r_, w_ = ch[step]
                            b.op(eng_, f_, r=r_, w=w_)
                    chains = []
                    for h in (h0, h0 + 1):
                        cv = cand[:, h, :]
                        c2 = cand2x[h % 2]
                        vr = [f'vals{2 * h}a', f'vals{2 * h}b', f'vals{2 * h + 1}a', f'vals{2 * h + 1}b']
                        chains.append([
                            ('dve', lambda e, h=h, cv=cv: e.tensor_tensor(out=cv.rearrange("p (a c) -> p a c", a=16),
                                                                          in0=vals[:, 2 * h, :].unsqueeze(2).to_broadcast([128, 16, 16]),
                                                                          in1=vals[:, 2 * h + 1, :].unsqueeze(1).to_broadcast([128, 16, 16]), op=ALU.add), vr, [f'cand{h}']),
                            ('dve', lambda e, h=h, cv=cv: e.max(out=cvals[:, h, 0:8], in_=cv), [f'cand{h}'], [f'cvals{h}a']),
                            ('dve', lambda e, h=h, cv=cv: e.max_index(out=cposu[:, h, 0:8], in_max=cvals[:, h, 0:8], in_values=cv), [f'cand{h}', f'cvals{h}a'], [f'cposu{h}a']),
                            ('dve', lambda e, h=h, cv=cv, c2=c2: e.match_replace(out=c2[:], in_to_replace=cvals[:, h, 0:8], in_values=cv, imm_value=NEG),
                             [f'cand{h}', f'cvals{h}a'], [f'cand2_{h % 2}']),
                            ('dve', lambda e, h=h, c2=c2: e.max(out=cvals[:, h, 8:16], in_=c2[:]), [f'cand2_{h % 2}'], [f'cvals{h}b']),
                            ('dve', lambda e, h=h, c2=c2: e.max_index(out=cposu[:, h, 8:16], in_max=cvals[:, h, 8:16], in_values=c2[:]), [f'cand2_{h % 2}', f'cvals{h}b'], [f'cposu{h}b']),
                        ])
                    for step in range(6):
                        for ch in chains:
                            eng_, f_, r_, w_ = ch[step]
                            b.op(eng_, f_, r=r_, w=w_)
                ALLV = [f'vals{i}{x}' for i in range(16) for x in 'ab']
                ALLI = [f'idxu{i}{x}' for i in range(16) for x in 'ab']
                ALLC = [f'cvals{i}{x}' for i in range(8) for x in 'ab']
                ALLP = [f'cposu{i}{x}' for i in range(8) for x in 'ab']
                b.op('dve', lambda e: e.tensor_copy(out=idxf[:], in_=idxu[:]), r=ALLI, w=['idxf'])
                b.op('dve', lambda e: e.tensor_scalar(out=negm[:], in0=cvals[:, :, 0], scalar1=-1.0, scalar2=None, op0=ALU.mult), r=ALLC, w=['negm'])
                for h in range(8):
                    b.op('act', lambda e, h=h: e.activation(out=Gtm[:, h * 16:(h + 1) * 16], in_=cvals[:, h, :], func=AF.Exp, bias=negm[:, h:h + 1],
                                                            scale=1.0, accum_out=gsum[:, h:h + 1]), r=ALLC + ['negm'], w=['Gtm', 'gsum'])
                b.op('dve', lambda e: e.reciprocal(out=gsum[:], in_=gsum[:]), r=['gsum'], w=['gsum'])
                b.op('dve', lambda e: e.tensor_tensor(out=Gtm[:].rearrange("p (h r) -> p h r", h=8), in0=Gtm[:].rearrange("p (h r) -> p h r", h=8),
                                                      in1=gsum[:].unsqueeze(2).to_broadcast([128, 8, 16]), op=ALU.mult), r=['Gtm', 'gsum'], w=['Gtm'])
                cpu_ = cposu[:].rearrange("p h r -> p (h r)")
                b.op('dve', lambda e: e.tensor_single_scalar(out=au[:], in_=cpu_, scalar=4, op=ALU.logical_shift_right), r=ALLP, w=['au'])
                b.op('dve', lambda e: e.tensor_single_scalar(out=bu[:], in_=cpu_, scalar=15, op=ALU.bitwise_and), r=ALLP, w=['bu'])
                b.op('dve', lambda e: e.tensor_copy(out=af[:], in_=au[:]), r=['au'], w=['af'])
                b.op('dve', lambda e: e.tensor_copy(out=bf_[:], in_=bu[:]), r=['bu'], w=['bf_'])
                for (srcf, half, dst, dtok) in ((af, 0, Itm, 'Itm'), (bf_, 1, Jtm, 'Jtm')):
                    b.op('dve', lambda e, srcf=srcf: e.tensor_tensor(out=eq[:], in0=srcf[:].unsqueeze(2).to_broadcast([128, 128, 16]),
                                                                     in1=iota_f[:, 0:16].unsqueeze(1).to_broadcast([128, 128, 16]), op=ALU.is_equal),
                         r=['af', 'bf_', 'iota_f'], w=['eq'])
                    idx_h = idxf[:].rearrange("p (h t) a -> p h t a", t=2)[:, :, half, :]
                    b.op('dve', lambda e, idx_h=idx_h: e.tensor_tensor(out=eq[:].rearrange("p (h r) a -> p h r a", h=8),
                                                                       in0=eq[:].rearrange("p (h r) a -> p h r a", h=8),
                                                                       in1=idx_h.unsqueeze(2).to_broadcast([128, 8, 16, 16]), op=ALU.mult),
                         r=['eq', 'idxf'], w=['eq'])
                    b.op('dve', lambda e, dst=dst: e.tensor_reduce(out=dst[:], in_=eq[:], axis=AX.X, op=ALU.add), r=['eq'], w=[dtok])
                for (srct, stok, dstt, dtok, bk) in ((Itm, 'Itm', iT, 'iT', 4), (Jtm, 'Jtm', jT, 'jT', 5), (Gtm, 'Gtm', gT, 'gT', 6)):
                    b.op('pe', lambda e, srct=srct, bk=bk: e.transpose(ps[bk][:, 0:128], srct[:], ident_f[:]), r=[stok, 'ident_f'], w=[f'ps{bk}'])
                    b.op('act', lambda e, dstt=dstt, bk=bk, dtok=dtok: e.activation(out=dstt[:], in_=ps[bk][:, 0:128], func=AF.Copy, scale=(-1.0 if dtok == 'jT' else 1.0)),
                         r=[f'ps{bk}'], w=[dtok])
                for q4 in range(32):
                    bk = q4 % 2
                    for u in range(3):
                        t = q4 * 4 + u
                        b.op('act', lambda e, u=u, t=t: e.activation(out=jsq[:, u, :], in_=iota_f[:], func=AF.Square, bias=jT[:, t:t + 1], scale=1.0),
                             r=['iota_f', 'jT'], w=[f'jsq{u}'])
                    for u in range(3):
                        t = q4 * 4 + u
                        b.op('act', lambda e, u=u, t=t: e.activation(out=jhot[:, u, :], in_=jsq[:, u, :], func=AF.Relu, bias=one_t[:, 0:1], scale=-1.0),
                             r=[f'jsq{u}', 'one_t'], w=[f'jhot{u}'])
                    b.op('dve', lambda e, q4=q4: e.tensor_scalar(out=jhot[:, 3, :], in0=iota_n[:], scalar1=jT[:, q4 * 4 + 3:q4 * 4 + 4], scalar2=None, op0=ALU.is_equal),
                         r=['iota_n', 'jT'], w=['jhot3'])
                    for u in range(4):
                        t = q4 * 4 + u
                        b.op('dve', lambda e, u=u, t=t: e.tensor_scalar(out=gih[:, u, :], in0=iota_f[:], scalar1=iT[:, t:t + 1], scalar2=gT[:, t:t + 1],
                                                                        op0=ALU.is_equal, op1=ALU.mult), r=['iota_f', 'iT', 'gT'], w=[f'gih{u}'])
                    def fn(e, bk=bk):
                        inst = None
                        for u in range(4):
                            inst = e.matmul(ps[bk][:, u * 128:(u + 1) * 128], jhot[:, u, :], gih[:, u, :], start=True, stop=True)
                        return inst
                    b.op('pe', fn, r=[f'jhot{u}' for u in range(4)] + [f'gih{u}' for u in range(4)], w=[f'ps{bk}'])
                    b.op('act', lambda e, bk=bk, q4=q4: e.activation(out=Gst[:, :, q4 * 4:(q4 + 1) * 4], in_=ps[bk][:, :].rearrange("p (t c) -> p c t", t=4), func=AF.Copy),
                         r=[f'ps{bk}'], w=[f'Gst{q4}'])
                for c8 in range(8):
                    b.op('sp', lambda e, c8=c8, ti=ti: e.dma_start(out=gsc[c8 * 16:(c8 + 1) * 16, :, ti * 128:(ti + 1) * 128].rearrange("c j t -> j c t"),
                                                                  in_=Gst[:, c8 * 16:(c8 + 1) * 16, :]), r=[f'Gst{q}' for q in range(32)], w=['gsc'], dsem='gsp')
            b.barrier()
        with ExitStack() as m2:
            def sb2(name, shape, dt=F32):
                return m2.enter_context(nc.sbuf_tensor(f"{name}_q{pi}", list(shape), dt))
            wb2 = [sb2(f"wbx{i}", [128, 16, 512], BF16) for i in range(2)]
            gTg = [sb2(f"gTg{i}", [128, 4, TS], BF16) for i in range(2)]
            actT = [sb2(f"actT{i}", [128, 4, TS], BF16) for i in range(2)]
            gel = [sb2(f"gel{i}", [128, 512], BF16) for i in range(2)]
            ob = 0
            for gi in range(32):
                par = gi % 2
                b.op('pool', lambda e, gi=gi, par=par: e.dma_start(out=wbuf[par][:], in_=wsrc(uT_d, gi * 512, 512)), w=[f'wbuf{par}'], dsem=f'w{par}')
                b.op('pool', lambda e, gi=gi, par=par: e.dma_start(out=wb2[par][:].rearrange("p a n -> p (a n)").rearrange("p (c n) -> p c n", c=4),
                                                                   in_=ev_d[gi * 512:(gi + 1) * 512, :].rearrange("(c p) n -> p c n", p=128)),
                     w=[f'wbx{par}'], dsem=f'wx{par}')
                b.op('sp', lambda e, gi=gi, par=par: e.dma_start(out=gTg[par][:], in_=gsc[gi * 4:(gi + 1) * 4, :, :].rearrange("c j t -> j c t")),
                     r=['gsc'], w=[f'gTg{par}'], dsem=f'gl{par}')
                vview = wb2[par][:].rearrange("p a n -> p (a n)").rearrange("p (c n) -> p c n", c=4)
                for cc in range(4):
                    for (c0, cw) in blocks_of(TS):
                        ab = cc % 2
                        mm_group(ps[ab][:, 0:cw], [(wbuf[par][:, kc, cc * 128:(cc + 1) * 128], hnT[:, kc, c0:c0 + cw]) for kc in range(KC)],
                                 ['hnT', f'wbuf{par}'], f'ps{ab}')
                        b.op('act', lambda e, ab=ab, cw=cw: e.activation(out=gel[ab][:, 0:cw], in_=ps[ab][:, 0:cw], func=AF.Gelu), r=[f'ps{ab}'], w=[f'gel{ab}'])
                        b.op('dve', lambda e, ab=ab, cc=cc, c0=c0, cw=cw, par=par: e.tensor_tensor(out=actT[par][:, cc, c0:c0 + cw], in0=gel[ab][:, 0:cw],
                                                                                                   in1=gTg[par][:, cc, c0:c0 + cw], op=ALU.mult),
                             r=[f'gel{ab}', f'gTg{par}'], w=[f'actT{par}'])
                for ti in range(NT):
                    for db in range(4):
                        bk = 2 + (ob % 4)
                        ob += 1
                        mm_group(ps[bk][:, :], [(actT[par][:, cc, ti * 128:(ti + 1) * 128], vview[:, cc, db * 512:(db + 1) * 512]) for cc in range(4)],
                                 [f'actT{par}', f'wbx{par}'], f'ps{bk}')
                        b.op('dve', lambda e, ti=ti, db=db, bk=bk: e.tensor_tensor(out=xacc[:, ti, db * 512:(db + 1) * 512], in0=xacc[:, ti, db * 512:(db + 1) * 512],
                                                                                   in1=ps[bk][:, :], op=ALU.add), r=[f'ps{bk}', f'xacc{ti}'], w=[f'xacc{ti}'])
            b.barrier()


def host_prep(inp, NPASS, NT, do_peer=True):
    TS = NT * 128
    TTOT = NPASS * TS
    f = lambda a: np.ascontiguousarray(np.asarray(a, dtype=np.float32))
    x = f(inp["x"])[0]
    pos = np.asarray(inp["positions"])[0].astype(np.int32)
    w_in = f(inp["w_in"][0])
    def swap_cols(w, nh):
        w4 = w.reshape(w.shape[0], nh, 2, 32)
        return np.ascontiguousarray(w4[:, :, ::-1, :].reshape(w.shape[0], nh * 64))
    wq = w_in[:, 0:1024]
    wk = w_in[:, 1024:1280]
    wk_sw = swap_cols(wk, 4)
    def dup(w):
        w3 = w.reshape(w.shape[0], 4, 1, 64)
        return np.ascontiguousarray(np.repeat(w3, 2, axis=2).reshape(w.shape[0], 512))
    wqk_sw = np.ascontiguousarray(np.concatenate([swap_cols(wq, 16), dup(wk_sw)], axis=1))
    wkdup = dup(wk)
    fm = lambda v, c: np.ascontiguousarray(f(v).reshape(c, 128).T)
    cvec = np.zeros((128, NCV), np.float32)
    cvec[:, CV_GMIX:CV_GMIX + 16] = fm(inp["g_mix"][0], 16)
    cvec[:, CV_GFFN:CV_GFFN + 16] = fm(inp["g_ffn"][0], 16)
    cvec[:, CV_GMEM:CV_GMEM + 16] = fm(inp["g_mem"][0], 16)
    p = np.arange(128)
    gq = f(inp["q_norm_g"][0]); gk = f(inp["k_norm_g"][0])
    cvec[:, CV_GQ] = gq[p % 64]; cvec[:, CV_GQ + 1] = gq[(p % 64 + 32) % 64]
    cvec[:, CV_GK] = gk[p % 64]; cvec[:, CV_GK + 1] = gk[(p % 64 + 32) % 64]
    cvec[:, CV_MQG] = f(inp["mq_norm_g"][0]); cvec[:, CV_MKG] = f(inp["mk_norm_g"][0])
    sinks = f(inp["attn_sinks"][0])
    order = []
    for g in range(4):
        order += [4 * g, 4 * g + 2, 4 * g + 1, 4 * g + 3]
    cvec[:, CV_SINK:CV_SINK + 16] = sinks[order][None, :]
    dw = f(inp["conv_dw_w"][0])[:, 0, :]
    for c in range(4):
        cvec[:, CV_DW + c * 31:CV_DW + (c + 1) * 31] = dw[:, c * 128:(c + 1) * 128].T
    cvec[:, CV_DWB:CV_DWB + 4] = fm(inp["conv_dw_b"][0], 4)
    cvec[:, CV_LNG:CV_LNG + 4] = fm(inp["conv_ln_g"][0], 4)
    cvec[:, CV_LNB:CV_LNB + 4] = fm(inp["conv_ln_b"][0], 4)
    inv_freq = (10000.0 ** (-np.arange(0, 64, 2, dtype=np.float32) / 64)).astype(np.float32)
    cvec[:, CV_INVF] = inv_freq[p % 32]
    cvec[:, CV_SGN] = np.where((p % 64) < 32, -1.0, 1.0)
    cmat = np.zeros((128, 384), np.float32)
    cmat[:, 0:128] = np.eye(128)
    cmat[:, 128:256] = 1.0
    cmat[0:64, 256:320] = 1.0
    cmat[64:128, 320:384] = 1.0
    kk = np.arange(128)[:, None]; qq = np.arange(128)[None, :]
    mprev = (kk > qq).astype(np.float32); mcur = (kk <= qq).astype(np.float32)
    def slabs(w, n):
        return w[:, n * 128:(n + 1) * 128].reshape(-1, 128, 128)
    wa, wc, wm = f(inp["w_attn_o"][0]), f(inp["w_conv_o"][0]), f(inp["w_mem_o"][0])
    wmerge = np.empty((16, 128, 64, 128), np.float32)
    for n in range(16):
        parts = [slabs(w_in[:, 3072:5120], n), slabs(wa, n), slabs(w_in[:, 5120:7168], n), slabs(wc, n),
                 slabs(w_in[:, 7168:9216], n), slabs(wm, n)]
        wmerge[n] = np.concatenate(parts, axis=0).transpose(1, 0, 2)
    wmerge = wmerge.reshape(16, 128, 8192)
    common = dict(w_in=np.ascontiguousarray(w_in[:, 0:3072]), wqk_sw=wqk_sw, wkdup=wkdup, wmerge=wmerge,
                  w_out=f(inp["w_out"][0]), w_mem_kv=f(inp["w_mem_kv"][0]),
                  mem=f(inp["mem"][0]), cvec=cvec, cmat=cmat)
    if do_peer:
        common["w_query"] = f(inp["w_query"][0])
        sk = f(inp["sub_keys"][0]).reshape(16, 128, 128)
        common["skT"] = np.ascontiguousarray(sk.transpose(2, 0, 1).reshape(128, 16 * 128))
        common["uT"] = np.ascontiguousarray(f(inp["expert_u"][0]).T)
        common["ev"] = f(inp["expert_v"][0])
        common["iota"] = np.ascontiguousarray(np.tile(np.arange(128, dtype=np.float32)[None, :], (128, 1)))
    in_maps = []
    for c in range(NCORES):
        s0 = c * TTOT
        if c == 0:
            xhc = np.concatenate([np.zeros((128, D), np.float32), x[0:TTOT]], axis=0)
            posc = np.concatenate([np.zeros(128, np.int32), pos[0:TTOT]])
            mfirst = np.zeros_like(mprev)
        else:
            xhc = x[s0 - 128:s0 + TTOT]
            posc = pos[s0 - 128:s0 + TTOT]
            mfirst = mprev
        m = dict(common)
        m["xh"] = np.ascontiguousarray(xhc)
        m["posb"] = np.ascontiguousarray(posc[None, :])
        m["masks"] = np.ascontiguousarray(np.concatenate([np.tile(mprev, (1, 4)), np.tile(mcur, (1, 4)), np.tile(mfirst, (1, 4))], axis=1))
        in_maps.append(m)
    return in_maps


_CACHE = {}


def run(inp, NPASS, NT, do_peer=True, trace=False):
    key = (NPASS, NT, do_peer)
    if key not in _CACHE:
        _CACHE[key] = build_program(NPASS, NT, do_peer)
    nc = _CACHE[key]
    in_maps = host_prep(inp, NPASS, NT, do_peer)
    res = run_bass_kernel_spmd(nc, in_maps, core_ids=list(range(NCORES)), **({"trace": True} if trace else {}))
    out = np.concatenate([r["out"] for r in res.results], axis=0)
    return out[None].astype(np.float32), res


def kernel(**inputs):
    out, _ = run(inputs, 4, 4, True)
    return out
```

```python
import numpy as np
from contextlib import ExitStack
import concourse.bass as bass
import concourse.mybir as mybir
from concourse.bass_utils import run_bass_kernel_spmd

F32 = mybir.dt.float32
BF16 = mybir.dt.bfloat16
I32 = mybir.dt.int32
U32 = mybir.dt.uint32
AF = mybir.ActivationFunctionType
ALU = mybir.AluOpType
AX = mybir.AxisListType

NCORES = 8
D = 2048
KC = 16
EPS = 1e-6
TWO_PI = 2.0 * np.pi

CV_GMIX, CV_GFFN, CV_GMEM = 0, 16, 32
CV_GQ, CV_GK, CV_MQG, CV_MKG = 48, 50, 52, 53
CV_SINK = 54
CV_DW = 70
CV_DWB, CV_LNG, CV_LNB = 194, 198, 202
CV_INVF, CV_SGN = 206, 207
NCV = 208


class B:
    def __init__(s, nc, es):
        s.nc = nc
        s.es = es
        s.engs = {'pe': nc.tensor, 'act': nc.scalar, 'dve': nc.vector, 'pool': nc.gpsimd, 'sp': nc.sync}
        s.sems = {}
        s.cnt = {}
        s.seen = {e: {} for e in s.engs}
        s.lastw = {}
        s.readers = {}
        for e in ['pe', 'act', 'dve', 'pool']:
            s.newsem(e)
        s.same_sync = {'pe': False, 'act': True, 'dve': True, 'pool': True, 'sp': True}

    def newsem(s, name):
        if name not in s.sems:
            s.sems[name] = s.es.enter_context(s.nc.semaphore(name))
            s.cnt[name] = 0
        return name

    def op(s, e, fn, r=(), w=(), dsem=None):
        eng = s.engs[e]
        need = {}

        def add(ev):
            if ev is not None:
                need[ev[0]] = max(need.get(ev[0], 0), ev[1])

        for t in r:
            add(s.lastw.get(t))
        for t in w:
            add(s.lastw.get(t))
            for sm, v in s.readers.get(t, {}).items():
                add((sm, v))
        for sm, v in need.items():
            if s.seen[e].get(sm, 0) < v:
                eng.wait_ge(s.sems[sm], v)
                s.seen[e][sm] = v
        inst = fn(eng)
        if dsem is not None:
            sm, inc = dsem, 16
        else:
            sm, inc = e, 1
        s.cnt[sm] += inc
        inst.then_inc(s.sems[sm], inc)
        ev = (sm, s.cnt[sm])
        if dsem is None and not s.same_sync[e]:
            s.seen[e][sm] = s.cnt[sm]
        for t in w:
            s.lastw[t] = ev
            s.readers[t] = {}
        for t in r:
            d = s.readers.setdefault(t, {})
            d[sm] = max(d.get(sm, 0), ev[1])
        return ev

    def barrier(s):
        for e, eng in s.engs.items():
            for sm, c in s.cnt.items():
                if c > 0 and s.seen[e].get(sm, 0) < c:
                    eng.wait_ge(s.sems[sm], c)
                    s.seen[e][sm] = c


import os
class _Stop(Exception):
    pass


def ckpt(k):
    if int(os.environ.get("KSTOP", "99")) == k:
        raise _Stop()


def blocks_of(total, bs=512):
    out = []
    o = 0
    while o < total:
        out.append((o, min(bs, total - o)))
        o += bs
    return out


def build_program(NPASS, NT, do_peer=True, first_core_flag=None):
    TS = NT * 128
    TH = TS + 128
    TTOT = NPASS * TS
    nc = bass.Bass("TRN2", target_bir_lowering=False)
    dr = {}

    def din(name, shape, dt=F32):
        dr[name] = nc.dram_tensor(name, list(shape), dt, kind="ExternalInput").ap()
        return dr[name]

    xh = din("xh", [TTOT + 128, D])
    posb = din("posb", [1, TTOT + 128], I32)
    w_in = din("w_in", [D, 3072])
    wqk_sw = din("wqk_sw", [D, 1536])
    wkdup = din("wkdup", [D, 512])
    wmerge = din("wmerge", [16, 128, 8192])
    w_out = din("w_out", [D, D])
    w_mem_kv = din("w_mem_kv", [D, 1024])
    memx = din("mem", [256, D])
    cvec_d = din("cvec", [128, NCV])
    cmat_d = din("cmat", [128, 384])
    masks_d = din("masks", [128, 1536])
    if do_peer:
        w_query = din("w_query", [D, D])
        skT_d = din("skT", [128, 16 * 128])
        uT_d = din("uT", [D, 16384])
        ev_d = din("ev", [16384, D])
        iota_d = din("iota", [128, 128])
    out_d = nc.dram_tensor("out", [TTOT, D], F32, kind="ExternalOutput").ap()
    if do_peer:
        dr["gsc"] = nc.dram_tensor("gsc", [128, 128, TS], BF16, kind="Internal").ap()

    es = ExitStack()
    with es:
        b = B(nc, es)

        def sb(name, shape, dt=F32):
            return es.enter_context(nc.sbuf_tensor(name, list(shape), dt))

        xacc = sb("xacc", [128, NT, D])
        cvec = sb("cvec_s", [128, NCV])
        cmat = sb("cmat_s", [128, 384], BF16)
        masks = sb("masks_s", [128, 1536], BF16)
        esink = sb("esink", [128, 16])
        mkT = sb("mkT", [128, 4, 256], BF16)
        mvd = sb("mvd", [128, 2, 512], BF16)
        wbuf = [sb(f"wbuf{i}", [128, 16, 512], BF16) for i in range(2)]
        for i in range(2):
            b.newsem(f"w{i}")
        ps = [es.enter_context(nc.psum_tensor(f"ps{i}", [128, 512], F32)) for i in range(8)]
        ident = cmat[:, 0:128]
        ones = cmat[:, 128:256]
        bones = cmat[:, 256:384]
        b.newsem("cst")
        b.newsem("mxl")
        b.newsem("pl")
        for i in range(NT + 1):
            b.newsem(f"xl{i}")
        for i in range(NT):
            b.newsem(f"st{i}")
        for nm in ["gsp", "gl0", "gl1", "wx0", "wx1"]:
            b.newsem(nm)

        b.newsem("cst0")
        b.op('sp', lambda e: e.dma_start(out=cvec[:], in_=cvec_d), w=['cvec'], dsem='cst0')
        b.op('pool', lambda e: e.dma_start(out=cmat[:], in_=cmat_d), w=['cmat'], dsem='cst')
        b.op('pool', lambda e: e.dma_start(out=masks[:], in_=masks_d), w=['masks'], dsem='cst')
        fin = ('cst', b.cnt['cst'])
        for t in ['cmat', 'masks']:
            b.lastw[t] = fin
        b.op('act', lambda e: e.activation(out=esink[:], in_=cvec[:, CV_SINK:CV_SINK + 16], func=AF.Exp),
             r=['cvec'], w=['esink'])

        wstate = {'i': 0}

        def load_w(pieces):
            i = wstate['i'] % 2
            wstate['i'] += 1
            for dst_fn, src in pieces:
                b.op('pool', lambda e, dst_fn=dst_fn, src=src: e.dma_start(out=dst_fn(wbuf[i]), in_=src),
                     w=[f'wbuf{i}'], dsem=f'w{i}')
            return i

        def wsrc(w_ap, c0, ncols, k0=0, kcn=KC):
            return w_ap[k0 * 128:(k0 + kcn) * 128, c0:c0 + ncols].rearrange("(c p) n -> p c n", p=128)

        def mm_group(out_ap, pairs, rtoks, wtok):
            def fn(e):
                inst = None
                n = len(pairs)
                for j, (l, r_) in enumerate(pairs):
                    inst = e.matmul(out_ap, l, r_, start=(j == 0), stop=(j == n - 1))
                return inst
            return b.op('pe', fn, r=rtoks, w=[wtok])

        def norm_transpose(pfx, src_ap, src_tok, dstT, dst_col0, gcol, dst_tok, tmp_sq, tmp_hb, stat, pbanks):
            ssq = stat[:, 0:1]
            rs = stat[:, 1:2]
            b.op('act', lambda e: e.activation(out=tmp_sq, in_=src_ap, func=AF.Square, accum_out=ssq),
                 r=[src_tok], w=[pfx + 'sq', pfx + 'stat'])
            b.op('act', lambda e: e.activation(out=rs, in_=ssq, func=AF.Sqrt, bias=cvec_eps, scale=1.0 / D),
                 r=[pfx + 'stat', 'eps'], w=[pfx + 'stat2'])
            b.op('dve', lambda e: e.reciprocal(out=rs, in_=rs), r=[pfx + 'stat2'], w=[pfx + 'stat2'])
            b.op('act', lambda e: e.activation(out=tmp_hb, in_=src_ap, func=AF.Copy, scale=rs),
                 r=[src_tok, pfx + 'stat2'], w=[pfx + 'hb'])
            for half in range(2):
                pb = pbanks[half]
                pview = ps[pb][:].bitcast(BF16)

                def fn(e, half=half, pview=pview):
                    inst = None
                    for j in range(8):
                        kc = half * 8 + j
                        inst = e.transpose(pview[:, j * 128:(j + 1) * 128], tmp_hb[:, kc * 128:(kc + 1) * 128], ident)
                    return inst
                b.op('pe', fn, r=[pfx + 'hb', 'cmat'], w=[f'ps{pb}'])
                b.op('dve', lambda e, half=half, pview=pview: e.tensor_tensor(
                    out=dstT[:, half * 8:(half + 1) * 8, dst_col0:dst_col0 + 128],
                    in0=pview.rearrange("p (c t) -> p c t", c=8),
                    in1=cvec[:, gcol + half * 8:gcol + half * 8 + 8].unsqueeze(2).to_broadcast([128, 8, 128]),
                    op=ALU.mult), r=[f'ps{pb}', 'cvec'], w=[dst_tok])

        eps_t = sb("eps_t", [128, 1])
        b.op('dve', lambda e: e.memset(eps_t[:], EPS), w=['eps'])
        cvec_eps = eps_t[:, 0:1]

        def fm_pair_gemm(XT, xtok, ncols_tok, items, epilogue, banks):
            pend = None
            it = 0
            for idx, (slot, la, lb) in enumerate(items):
                for (c0, cw) in blocks_of(ncols_tok):
                    ba, bb = banks[it % len(banks)]
                    it += 1
                    mm_group(ps[ba][:, 0:cw], [(l, XT[:, kc, c0:c0 + cw]) for kc, l in enumerate(la)],
                             [xtok, f'wbuf{slot}'], f'ps{ba}')
                    if lb is not None:
                        mm_group(ps[bb][:, 0:cw], [(l, XT[:, kc, c0:c0 + cw]) for kc, l in enumerate(lb)],
                                 [xtok, f'wbuf{slot}'], f'ps{bb}')
                    if pend is not None:
                        epilogue(*pend)
                    pend = (idx, c0, cw, ba, bb)
            if pend is not None:
                epilogue(*pend)

        with ExitStack() as ms:
            def msb(name, shape, dt=F32):
                return ms.enter_context(nc.sbuf_tensor(name, list(shape), dt))
            memT = msb("memT", [128, 16, 256], BF16)
            mx = msb("mx", [128, D])
            msq = msb("msq", [128, D])
            mhb = msb("mhb", [128, D], BF16)
            mstat = msb("mstat", [128, 2])
            t_sq = msb("m_t_sq", [128, 256], BF16)
            t_rs = msb("m_t_rs", [128, 256])
            for ti in range(2):
                b.op('sp', lambda e, ti=ti: e.dma_start(out=mx[:], in_=memx[ti * 128:(ti + 1) * 128, :]),
                     w=['mx'], dsem='mxl')
                norm_transpose('m', mx[:], 'mx', memT, ti * 128, CV_GMEM, 'memT', msq[:], mhb[:], mstat, (0, 1))
            slot = load_w([(lambda wb: wb[:, :, 0:512], wsrc(w_mem_kv, 0, 512))])
            for h in range(4):
                mm_group(ps[2][:, 0:256], [(wbuf[slot][:, kc, h * 128:(h + 1) * 128], memT[:, kc, :]) for kc in range(KC)],
                         ['memT', f'wbuf{slot}'], 'ps2')
                b.op('act', lambda e: e.activation(out=t_sq[:], in_=ps[2][:, 0:256], func=AF.Square), r=['ps2'], w=['m_sq'])
                mm_group(ps[3][:, 0:256], [(ones, t_sq[:])], ['m_sq', 'cmat'], 'ps3')
                b.op('act', lambda e: e.activation(out=t_rs[:], in_=ps[3][:, 0:256], func=AF.Sqrt, bias=cvec_eps, scale=1.0 / 128),
                     r=['ps3', 'eps'], w=['m_rs'])
                b.op('dve', lambda e: e.reciprocal(out=t_rs[:], in_=t_rs[:]), r=['m_rs'], w=['m_rs'])
                b.op('dve', lambda e, h=h: e.scalar_tensor_tensor(out=mkT[:, h, :], in0=ps[2][:, 0:256],
                                                                  scalar=cvec[:, CV_MKG:CV_MKG + 1], in1=t_rs[:],
                                                                  op0=ALU.mult, op1=ALU.mult),
                     r=['ps2', 'm_rs', 'cvec'], w=['mkT'])
            slot = load_w([(lambda wb: wb[:, :, 0:512], wsrc(w_mem_kv, 512, 512))])
            for ti in range(2):
                mm_group(ps[4][:, :], [(memT[:, kc, ti * 128:(ti + 1) * 128], wbuf[slot][:, kc, :]) for kc in range(KC)],
                         ['memT', f'wbuf{slot}'], 'ps4')
                b.op('act', lambda e, ti=ti: e.activation(out=mvd[:, ti, :], in_=ps[4][:, :], func=AF.Copy), r=['ps4'], w=['mvd'])
            b.barrier()

        for pi in range(NPASS):
            tok0 = pi * TS
            with ExitStack() as ms:
              try:
                  def msb(name, shape, dt=F32):
                      return ms.enter_context(nc.sbuf_tensor(f"{name}_p{pi}", list(shape), dt))
                  hT = msb("hT", [128, 16, TH], BF16)
                  qT = msb("qT", [128, 8, TH], BF16)
                  kTA = msb("kTA", [128, 4, TH], BF16)
                  kTB = msb("kTB", [128, 4, TH], BF16)
                  b.op('dve', lambda e: e.memset(kTA[64:128, :, :], 0.0), w=['kT2'])
                  b.op('dve', lambda e: e.memset(kTB[0:64, :, :], 0.0), w=['kT2'])
                  vdup = msb("vdup", [128, NT + 1, 4, 128], BF16)
                  attnT = msb("attnT", [128, 8, TS], BF16)
                  arena = msb("arena", [128, 8 * TH + 16 * TS], BF16)
                  o1 = 8 * TH
                  o2 = o1 + 8 * TS
                  o3 = o2 + 4 * TS
                  gluT = arena[:, 0:o1].bitcast(F32).rearrange("p (c t) -> p c t", c=4)
                  cT = arena[:, o1:o2].bitcast(F32).rearrange("p (c t) -> p c t", c=4)
                  cbf = arena[:, o2:o3].rearrange("p (c t) -> p c t", c=4)
                  csq = arena[:, o3:o3 + 4 * TS].rearrange("p (c t) -> p c t", c=4)
                  mergedT = arena[:, 0:16 * TS].rearrange("p (c t) -> p c t", c=16)
                  convT = msb("convT", [128, 4, TS], BF16)
                  mqT = msb("mqT", [128, 4, TS], BF16)
                  memoT = msb("memoT", [128, 4, TS], BF16)
                  cosT = msb("cosT", [128, TH])
                  sinS = msb("sinS", [128, TH])
                  posi = msb("posi", [128, TH], I32)
                  xhalo = msb("xhalo", [128, D])
                  tsq = msb("tsq", [128, D], BF16)
                  thb = msb("thb", [128, D], BF16)
                  stat = msb("stat", [128, 2])
                  tA = [msb(f"tA{i}", [128, 512]) for i in range(4)]
                  tB = [msb(f"tB{i}", [128, 512], BF16) for i in range(4)]

                  b.op('sp', lambda e: e.dma_start(out=posi[:], in_=posb[:, tok0:tok0 + TH].partition_broadcast(128)),
                       w=['posi'], dsem='pl')
                  ang = cosT
                  kk = sinS
                  b.op('dve', lambda e: e.tensor_copy(out=ang[:], in_=posi[:]), r=['posi'], w=['cosT'])
                  b.op('dve', lambda e: e.tensor_scalar(out=ang[:], in0=ang[:], scalar1=cvec[:, CV_INVF:CV_INVF + 1], scalar2=None,
                                                        op0=ALU.mult), r=['cosT', 'cvec'], w=['cosT'])
                  MAGIC = 12582912.0
                  b.op('dve', lambda e: e.tensor_scalar(out=kk[:], in0=ang[:], scalar1=1.0 / TWO_PI, scalar2=MAGIC,
                                                        op0=ALU.mult, op1=ALU.add), r=['cosT'], w=['sinS'])
                  b.op('dve', lambda e: e.tensor_scalar(out=kk[:], in0=kk[:], scalar1=MAGIC, scalar2=None,
                                                        op0=ALU.subtract), r=['sinS'], w=['sinS'])
                  C1 = 6.28125
                  C2 = float(np.float32(TWO_PI - 6.28125))
                  C3 = float(TWO_PI - 6.28125 - np.float64(np.float32(TWO_PI - 6.28125)))
                  for cc in (C1, C2, C3):
                      b.op('dve', lambda e, cc=cc: e.scalar_tensor_tensor(out=ang[:], in0=kk[:], scalar=-cc, in1=ang[:],
                                                                          op0=ALU.mult, op1=ALU.add),
                           r=['sinS', 'cosT'], w=['cosT'])
                  PI_LO = 3.1415925
                  b.op('dve', lambda e: e.tensor_scalar(out=ang[:], in0=ang[:], scalar1=PI_LO, scalar2=-PI_LO,
                                                        op0=ALU.min, op1=ALU.max), r=['cosT'], w=['cosT'])
                  b.op('act', lambda e: e.activation(out=sinS[:], in_=ang[:], func=AF.Sin), r=['cosT'], w=['sinS'])
                  b.op('dve', lambda e: e.tensor_scalar(out=sinS[:], in0=sinS[:], scalar1=cvec[:, CV_SGN:CV_SGN + 1], scalar2=None,
                                                        op0=ALU.mult), r=['sinS', 'cvec'], w=['sinS'])
                  wr = tA[0]
                  b.op('dve', lambda e: e.tensor_scalar(out=ang[:], in0=ang[:], scalar1=float(np.pi / 2), scalar2=None,
                                                        op0=ALU.add), r=['cosT'], w=['cosT'])
                  for (c0, cw) in blocks_of(TH):
                      b.op('dve', lambda e, c0=c0, cw=cw: e.tensor_scalar(out=wr[:, 0:cw], in0=ang[:, c0:c0 + cw], scalar1=PI_LO,
                                                                          scalar2=-TWO_PI, op0=ALU.is_gt, op1=ALU.mult),
                           r=['cosT'], w=['tA0'])
                      b.op('dve', lambda e, c0=c0, cw=cw: e.tensor_tensor(out=ang[:, c0:c0 + cw], in0=ang[:, c0:c0 + cw],
                                                                          in1=wr[:, 0:cw], op=ALU.add),
                           r=['tA0', 'cosT'], w=['cosT'])
                  b.op('dve', lambda e: e.tensor_scalar(out=ang[:], in0=ang[:], scalar1=PI_LO, scalar2=-PI_LO,
                                                        op0=ALU.min, op1=ALU.max), r=['cosT'], w=['cosT'])
                  b.op('act', lambda e: e.activation(out=cosT[:], in_=ang[:], func=AF.Sin), r=['cosT'], w=['cosT'])

                  ckpt(1)
                  for ti in range(NT + 1):
                      if ti == 0:
                          dst, tok = xhalo[:], 'xhalo'
                      else:
                          dst, tok = xacc[:, ti - 1, :], f'xacc{ti - 1}'
                      b.op('sp', lambda e, dst=dst, ti=ti: e.dma_start(out=dst, in_=xh[tok0 + ti * 128: tok0 + (ti + 1) * 128, :]),
                           w=[tok], dsem=f'xl{ti}')
                      norm_transpose('x', dst, tok, hT, ti * 128, CV_GMIX, 'hT', tsq[:], thb[:], stat, (0, 1))

                  ckpt(2)
                  def qk_epilogue_factory(dstT, gcol, nblk_items):
                      def epi(idx, c0, cw, ba, bb):
                          sq = tB[0]
                          rs = tA[1]
                          t1 = tA[2]
                          t2 = tA[3]
                          b.op('act', lambda e: e.activation(out=sq[:, 0:cw], in_=ps[ba][:, 0:cw], func=AF.Square),
                               r=[f'ps{ba}'], w=['tB0'])
                          mm_group(ps[6][:, 0:cw], [(bones, sq[:, 0:cw])], ['tB0', 'cmat'], 'ps6')
                          b.op('act', lambda e: e.activation(out=rs[:, 0:cw], in_=ps[6][:, 0:cw], func=AF.Sqrt,
                                                             bias=cvec_eps, scale=1.0 / 64), r=['ps6', 'eps'], w=['tA1'])
                          b.op('dve', lambda e: e.reciprocal(out=rs[:, 0:cw], in_=rs[:, 0:cw]), r=['tA1'], w=['tA1'])
                          b.op('dve', lambda e: e.scalar_tensor_tensor(out=t1[:, 0:cw], in0=ps[ba][:, 0:cw],
                                                                       scalar=cvec[:, gcol:gcol + 1], in1=cosT[:, c0:c0 + cw],
                                                                       op0=ALU.mult, op1=ALU.mult),
                               r=[f'ps{ba}', 'cvec', 'cosT'], w=['tA2'])
                          b.op('dve', lambda e: e.scalar_tensor_tensor(out=t2[:, 0:cw], in0=ps[bb][:, 0:cw],
                                                                       scalar=cvec[:, gcol + 1:gcol + 2], in1=sinS[:, c0:c0 + cw],
                                                                       op0=ALU.mult, op1=ALU.mult),
                               r=[f'ps{bb}', 'cvec', 'sinS'], w=['tA3'])
                          b.op('dve', lambda e: e.tensor_tensor(out=t1[:, 0:cw], in0=t1[:, 0:cw], in1=t2[:, 0:cw], op=ALU.add),
                               r=['tA2', 'tA3'], w=['tA2'])
                          if isinstance(dstT, tuple):
                              for (dd, p0) in zip(dstT, (0, 64)):
                                  b.op('dve', lambda e, dd=dd, p0=p0: e.tensor_tensor(out=dd[p0:p0 + 64, idx, c0:c0 + cw], in0=t1[p0:p0 + 64, 0:cw],
                                                                                      in1=rs[p0:p0 + 64, 0:cw], op=ALU.mult),
                                       r=['tA2', 'tA1'], w=[nblk_items])
                          else:
                              b.op('dve', lambda e: e.tensor_tensor(out=dstT[:, idx, c0:c0 + cw], in0=t1[:, 0:cw], in1=rs[:, 0:cw],
                                                                    op=ALU.mult), r=['tA2', 'tA1'], w=[nblk_items])
                      return epi

                  banks2 = [(2, 3), (4, 5)]
                  for blk in range(4):
                      slot = load_w([(lambda wb: wb[:, :, 0:256], wsrc(w_in, blk * 256, 256)),
                                     (lambda wb: wb[:, :, 256:512], wsrc(wqk_sw, blk * 256, 256))])
                      items = []
                      for j in range(2):
                          items.append((slot, [wbuf[slot][:, kc, j * 128:(j + 1) * 128] for kc in range(KC)],
                                        [wbuf[slot][:, kc, 256 + j * 128:256 + (j + 1) * 128] for kc in range(KC)]))
                      epi = qk_epilogue_factory(qT, CV_GQ, 'qT')
                      fm_pair_gemm(hT, 'hT', TH, items, lambda idx, c0, cw, ba, bb, blk=blk, epi=epi: epi(blk * 2 + idx, c0, cw, ba, bb), banks2)
                  for blk in range(2):
                      slot = load_w([(lambda wb: wb[:, :, 0:256], wsrc(wkdup, blk * 256, 256)),
                                     (lambda wb: wb[:, :, 256:512], wsrc(wqk_sw, 1024 + blk * 256, 256))])
                      items = []
                      for j in range(2):
                          items.append((slot, [wbuf[slot][:, kc, j * 128:(j + 1) * 128] for kc in range(KC)],
                                        [wbuf[slot][:, kc, 256 + j * 128:256 + (j + 1) * 128] for kc in range(KC)]))
                      epi = qk_epilogue_factory((kTA, kTB), CV_GK, 'kT2')
                      fm_pair_gemm(hT, 'hT', TH, items, lambda idx, c0, cw, ba, bb, blk=blk, epi=epi: epi(blk * 2 + idx, c0, cw, ba, bb), banks2)

                  ckpt(3)
                  slot = load_w([(lambda wb: wb[:, :, 0:256], wsrc(w_in, 1280, 256))])
                  for ti in range(NT + 1):
                      bk = 2 + (ti % 2)
                      mm_group(ps[bk][:, 0:256], [(hT[:, kc, ti * 128:(ti + 1) * 128], wbuf[slot][:, kc, 0:256]) for kc in range(KC)],
                               ['hT', f'wbuf{slot}'], f'ps{bk}')
                      for dup in range(2):
                          b.op('act' if dup == 0 else 'dve',
                               (lambda e, ti=ti, bk=bk: e.activation(out=vdup[:, ti, :, 0:64], in_=ps[bk][:, 0:256].rearrange("p (g d) -> p g d", g=4), func=AF.Copy))
                               if dup == 0 else
                               (lambda e, ti=ti, bk=bk: e.tensor_copy(out=vdup[:, ti, :, 64:128], in_=ps[bk][:, 0:256].rearrange("p (g d) -> p g d", g=4))),
                               r=[f'ps{bk}'], w=['vdup'])

                  ckpt(4)
                  def glu_epi(idx, c0, cw, ba, bb):
                      sg = tA[1]
                      b.op('act', lambda e: e.activation(out=sg[:, 0:cw], in_=ps[bb][:, 0:cw], func=AF.Sigmoid), r=[f'ps{bb}'], w=['tA1'])
                      b.op('dve', lambda e: e.tensor_tensor(out=gluT[:, idx, c0:c0 + cw], in0=ps[ba][:, 0:cw], in1=sg[:, 0:cw], op=ALU.mult),
                           r=[f'ps{ba}', 'tA1'], w=['gluT'])
                  for blk in range(2):
                      slot = load_w([(lambda wb: wb[:, :, 0:256], wsrc(w_in, 1536 + blk * 256, 256)),
                                     (lambda wb: wb[:, :, 256:512], wsrc(w_in, 2048 + blk * 256, 256))])
                      items = []
                      for j in range(2):
                          items.append((slot, [wbuf[slot][:, kc, j * 128:(j + 1) * 128] for kc in range(KC)],
                                        [wbuf[slot][:, kc, 256 + j * 128:256 + (j + 1) * 128] for kc in range(KC)]))
                      fm_pair_gemm(hT, 'hT', TH, items, lambda idx, c0, cw, ba, bb, blk=blk: glu_epi(blk * 2 + idx, c0, cw, ba, bb), banks2)
                  for c in range(4):
                      b.op('dve', lambda e, c=c: e.tensor_scalar(out=cT[:, c, :], in0=gluT[:, c, 98:98 + TS],
                                                                 scalar1=cvec[:, CV_DW + c * 31:CV_DW + c * 31 + 1],
                                                                 scalar2=cvec[:, CV_DWB + c:CV_DWB + c + 1], op0=ALU.mult, op1=ALU.add),
                           r=['gluT', 'cvec'], w=[f'cT{c}'])
                  for w_ in range(1, 31):
                      for c in range(4):
                          b.op('dve', lambda e, c=c, w_=w_: e.scalar_tensor_tensor(
                              out=cT[:, c, :], in0=gluT[:, c, 98 + w_:98 + w_ + TS],
                              scalar=cvec[:, CV_DW + c * 31 + w_:CV_DW + c * 31 + w_ + 1], in1=cT[:, c, :],
                              op0=ALU.mult, op1=ALU.add), r=['gluT', 'cvec', f'cT{c}'], w=[f'cT{c}'])
                  for c in range(4):
                      b.op('act', lambda e, c=c: e.activation(out=cbf[:, c, :], in_=cT[:, c, :], func=AF.Copy), r=[f'cT{c}'], w=['cbf'])
                      b.op('act', lambda e, c=c: e.activation(out=csq[:, c, :], in_=cT[:, c, :], func=AF.Square), r=[f'cT{c}'], w=['csq'])
                  for (c0, cw) in blocks_of(TS):
                      mm_group(ps[2][:, 0:cw], [(ones, cbf[:, c, c0:c0 + cw]) for c in range(4)], ['cbf', 'cmat'], 'ps2')
                      mm_group(ps[3][:, 0:cw], [(ones, csq[:, c, c0:c0 + cw]) for c in range(4)], ['csq', 'cmat'], 'ps3')
                      mean, msq_, rstd = tA[0], tA[1], tA[2]
                      b.op('dve', lambda e: e.tensor_scalar(out=mean[:, 0:cw], in0=ps[2][:, 0:cw], scalar1=1.0 / 512, scalar2=None, op0=ALU.mult),
                           r=['ps2'], w=['tA0'])
                      b.op('dve', lambda e: e.tensor_tensor(out=msq_[:, 0:cw], in0=mean[:, 0:cw], in1=mean[:, 0:cw], op=ALU.mult),
                           r=['tA0'], w=['tA1'])
                      b.op('dve', lambda e: e.scalar_tensor_tensor(out=rstd[:, 0:cw], in0=ps[3][:, 0:cw], scalar=1.0 / 512, in1=msq_[:, 0:cw],
                                                                   op0=ALU.mult, op1=ALU.subtract), r=['ps3', 'tA1'], w=['tA2'])
                      b.op('act', lambda e: e.activation(out=rstd[:, 0:cw], in_=rstd[:, 0:cw], func=AF.Sqrt, bias=cvec_eps, scale=1.0),
                           r=['tA2', 'eps'], w=['tA2'])
                      b.op('dve', lambda e: e.reciprocal(out=rstd[:, 0:cw], in_=rstd[:, 0:cw]), r=['tA2'], w=['tA2'])
                      for c in range(4):
                          xc = tA[3]
                          b.op('dve', lambda e, c=c: e.tensor_tensor(out=xc[:, 0:cw], in0=cT[:, c, c0:c0 + cw], in1=mean[:, 0:cw], op=ALU.subtract),
                               r=[f'cT{c}', 'tA0'], w=['tA3'])
                          b.op('dve', lambda e: e.tensor_tensor(out=xc[:, 0:cw], in0=xc[:, 0:cw], in1=rstd[:, 0:cw], op=ALU.mult),
                               r=['tA3', 'tA2'], w=['tA3'])
                          b.op('act', lambda e, c=c: e.activation(out=convT[:, c, c0:c0 + cw], in_=xc[:, 0:cw], func=AF.Silu,
                                                                  bias=cvec[:, CV_LNB + c:CV_LNB + c + 1], scale=cvec[:, CV_LNG + c:CV_LNG + c + 1]),
                               r=['tA3', 'cvec'], w=['convT'])

                  ckpt(5)
                  def mq_epi(idx, c0, cw, ba, bb):
                      sq = tB[0]
                      rs = tA[1]
                      b.op('act', lambda e: e.activation(out=sq[:, 0:cw], in_=ps[ba][:, 0:cw], func=AF.Square), r=[f'ps{ba}'], w=['tB0'])
                      mm_group(ps[6][:, 0:cw], [(ones, sq[:, 0:cw])], ['tB0', 'cmat'], 'ps6')
                      b.op('act', lambda e: e.activation(out=rs[:, 0:cw], in_=ps[6][:, 0:cw], func=AF.Sqrt, bias=cvec_eps, scale=1.0 / 128),
                           r=['ps6', 'eps'], w=['tA1'])
                      b.op('dve', lambda e: e.reciprocal(out=rs[:, 0:cw], in_=rs[:, 0:cw]), r=['tA1'], w=['tA1'])
                      b.op('dve', lambda e: e.scalar_tensor_tensor(out=mqT[:, idx, c0:c0 + cw], in0=ps[ba][:, 0:cw],
                                                                   scalar=cvec[:, CV_MQG:CV_MQG + 1], in1=rs[:, 0:cw],
                                                                   op0=ALU.mult, op1=ALU.mult), r=[f'ps{ba}', 'tA1', 'cvec'], w=['mqT'])
                  slot = load_w([(lambda wb: wb[:, :, 0:512], wsrc(w_in, 2560, 512))])
                  items = [(slot, [wbuf[slot][:, kc, j * 128:(j + 1) * 128] for kc in range(KC)], None) for j in range(4)]
                  hT_own = hT[:, :, 128:TH]
                  fm_pair_gemm(hT_own, 'hT', TS, items, mq_epi, banks2)

                  ckpt(6)
                  def attn_core(st_pairs_fn, nkb, v_lhsT_fn, mask_fn, scale, den_extra, out_fn, rtoks, cw=512):
                      for kb in range(nkb):
                          bk = 2 + kb
                          def fn(e, kb=kb, bk=bk):
                              inst = None
                              for (oc0, ocw, l, r_) in st_pairs_fn(kb):
                                  inst = e.matmul(ps[bk][:, oc0:oc0 + ocw], l, r_, start=True, stop=True)
                              return inst
                          b.op('pe', fn, r=rtoks, w=[f'ps{bk}'])
                          b.op('act', lambda e, kb=kb, bk=bk: e.activation(out=tB[kb][:, 0:cw], in_=ps[bk][:, 0:cw], func=AF.Exp, scale=scale),
                               r=[f'ps{bk}'], w=[f'tB{kb}'])
                          m = mask_fn(kb)
                          if m is not None:
                              b.op('dve', lambda e, kb=kb, m=m: e.tensor_tensor(out=tB[kb][:, 0:cw], in0=tB[kb][:, 0:cw], in1=m, op=ALU.mult),
                                   r=[f'tB{kb}', 'masks'], w=[f'tB{kb}'])
                      mm_group(ps[4][:, 0:cw], [(v_lhsT_fn(kb), tB[kb][:, 0:cw]) for kb in range(nkb)],
                               [f'tB{kb}' for kb in range(nkb)] + rtoks, 'ps4')
                      mm_group(ps[5][:, 0:cw], [(ones, tB[kb][:, 0:cw]) for kb in range(nkb)],
                               [f'tB{kb}' for kb in range(nkb)] + ['cmat'], 'ps5')
                      rden = tA[0]
                      if den_extra is not None:
                          b.op('dve', lambda e: e.tensor_tensor(out=rden[:, 0:cw].rearrange("p (h q) -> p h q", h=4), in0=ps[5][:, 0:cw].rearrange("p (h q) -> p h q", h=4),
                                                                in1=den_extra, op=ALU.add), r=['ps5', 'esink'], w=['tA0'])
                          b.op('dve', lambda e: e.reciprocal(out=rden[:, 0:cw], in_=rden[:, 0:cw]), r=['tA0'], w=['tA0'])
                      else:
                          b.op('dve', lambda e: e.reciprocal(out=rden[:, 0:cw], in_=ps[5][:, 0:cw]), r=['ps5'], w=['tA0'])
                      out_fn(rden)

                  for n in range(NT):
                      qc0 = 128 * (n + 1)
                      for g in range(4):
                          def st_pairs(kb, n=n, g=g, qc0=qc0):
                              kc0 = 128 * (n + kb)
                              return [(0, 256, kTA[:, g, kc0:kc0 + 128], qT[:, 2 * g:2 * g + 2, qc0:qc0 + 128]),
                                      (256, 256, kTB[:, g, kc0:kc0 + 128], qT[:, 2 * g:2 * g + 2, qc0:qc0 + 128])]

                          def mask_fn(kb, n=n):
                              if kb == 1:
                                  return masks[:, 512:1024]
                              if n == 0 and pi == 0:
                                  return masks[:, 1024:1536]
                              return masks[:, 0:512]

                          def out_fn(rden, n=n, g=g):
                              b.op('dve', lambda e: e.tensor_tensor(out=attnT[0:64, 2 * g:2 * g + 2, n * 128:(n + 1) * 128],
                                                                    in0=ps[4][0:64, 0:256].rearrange("p (h q) -> p h q", h=2),
                                                                    in1=rden[0:64, 0:256].rearrange("p (h q) -> p h q", h=2), op=ALU.mult),
                                   r=['ps4', 'tA0'], w=['attnT'])
                              b.op('dve', lambda e: e.tensor_tensor(out=attnT[64:128, 2 * g:2 * g + 2, n * 128:(n + 1) * 128],
                                                                    in0=ps[4][64:128, 256:512].rearrange("p (h q) -> p h q", h=2),
                                                                    in1=rden[64:128, 256:512].rearrange("p (h q) -> p h q", h=2), op=ALU.mult),
                                   r=['ps4', 'tA0'], w=['attnT'])
                          attn_core(st_pairs, 2, lambda kb, n=n, g=g: vdup[:, n + kb, g, :], mask_fn, 0.125,
                                    esink[:, 4 * g:4 * g + 4].unsqueeze(2).to_broadcast([128, 4, 128]), out_fn,
                                    ['qT', 'kT2', 'vdup'])
                  for h in range(4):
                      for (c0, cw) in blocks_of(TS):
                          def st_pairs(kb, h=h, c0=c0, cw=cw):
                              return [(0, cw, mkT[:, h, kb * 128:(kb + 1) * 128], mqT[:, h, c0:c0 + cw])]

                          def out_fn(rden, h=h, c0=c0, cw=cw):
                              b.op('dve', lambda e: e.tensor_tensor(out=memoT[:, h, c0:c0 + cw], in0=ps[4][:, 0:cw], in1=rden[:, 0:cw], op=ALU.mult),
                                   r=['ps4', 'tA0'], w=['memoT'])
                          attn_core(st_pairs, 2, lambda kb, h=h: mvd[:, kb, h * 128:(h + 1) * 128], lambda kb: None,
                                    float(128 ** -0.5), None, out_fn, ['mqT', 'mkT', 'mvd'], cw=cw)

                  ckpt(7)
                  b.barrier()
                  for n in range(16):
                      col = n * 128
                      i = wstate['i'] % 2
                      wv = lambda wb: wb[:].rearrange("p a (b c) -> p (a b) c", c=128)
                      slot = load_w([(lambda wb: wb[:].rearrange("p a n -> p (a n)").rearrange("p (c n) -> p c n", c=4),
                                      wmerge[n].rearrange("p (c n) -> p c n", c=4))])
                      wb = wv(wbuf[slot])
                      for (c0, cw) in blocks_of(TS):
                          wt = [f'wbuf{slot}']
                          mm_group(ps[0][:, 0:cw], [(wb[:, kc, :], hT[:, kc, 128 + c0:128 + c0 + cw]) for kc in range(16)], ['hT'] + wt, 'ps0')
                          mm_group(ps[1][:, 0:cw], [(wb[:, 16 + kc, :], attnT[:, kc, c0:c0 + cw]) for kc in range(8)], ['attnT'] + wt, 'ps1')
                          mm_group(ps[2][:, 0:cw], [(wb[:, 24 + kc, :], hT[:, kc, 128 + c0:128 + c0 + cw]) for kc in range(16)], ['hT'] + wt, 'ps2')
                          mm_group(ps[3][:, 0:cw], [(wb[:, 40 + kc, :], convT[:, kc, c0:c0 + cw]) for kc in range(4)], ['convT'] + wt, 'ps3')
                          mm_group(ps[4][:, 0:cw], [(wb[:, 44 + kc, :], hT[:, kc, 128 + c0:128 + c0 + cw]) for kc in range(16)], ['hT'] + wt, 'ps4')
                          mm_group(ps[5][:, 0:cw], [(wb[:, 60 + kc, :], memoT[:, kc, c0:c0 + cw]) for kc in range(4)], ['memoT'] + wt, 'ps5')
                          sg, m1, m2 = tA[0], tA[1], tA[2]
                          for bi, (gb, ob) in enumerate([(0, 1), (2, 3), (4, 5)]):
                              sgb, sgt = (tA[3], 'tA3') if bi == 1 else (tA[0], 'tA0')
                              b.op('act', lambda e, gb=gb, sgb=sgb: e.activation(out=sgb[:, 0:cw], in_=ps[gb][:, 0:cw], func=AF.Sigmoid), r=[f'ps{gb}'], w=[sgt])
                              dst = m1 if bi == 0 else m2
                              b.op('dve', lambda e, ob=ob, dst=dst, sgb=sgb: e.tensor_tensor(out=dst[:, 0:cw], in0=ps[ob][:, 0:cw], in1=sgb[:, 0:cw], op=ALU.mult),
                                   r=[f'ps{ob}', sgt], w=['tA1' if bi == 0 else 'tA2'])
                              if bi == 1:
                                  b.op('dve', lambda e: e.tensor_tensor(out=m1[:, 0:cw], in0=m1[:, 0:cw], in1=m2[:, 0:cw], op=ALU.add),
                                       r=['tA1', 'tA2'], w=['tA1'])
                              if bi == 2:
                                  b.op('dve', lambda e, n=n: e.tensor_tensor(out=mergedT[:, n, c0:c0 + cw], in0=m1[:, 0:cw], in1=m2[:, 0:cw], op=ALU.add),
                                       r=['tA1', 'tA2'], w=['mergedT'])

                  ckpt(8)
                  for nb in range(4):
                      slot = load_w([(lambda wb: wb[:, :, 0:512], wsrc(w_out, nb * 512, 512))])
                      for ti in range(NT):
                          bk = 6 + (ti % 2)
                          mm_group(ps[bk][:, :], [(mergedT[:, kc, ti * 128:(ti + 1) * 128], wbuf[slot][:, kc, :]) for kc in range(KC)],
                                   ['mergedT', f'wbuf{slot}'], f'ps{bk}')
                          b.op('dve', lambda e, ti=ti, nb=nb, bk=bk: e.tensor_tensor(out=xacc[:, ti, nb * 512:(nb + 1) * 512],
                                                                                     in0=xacc[:, ti, nb * 512:(nb + 1) * 512], in1=ps[bk][:, :], op=ALU.add),
                               r=[f'ps{bk}', f'xacc{ti}'], w=[f'xacc{ti}'])
                  b.barrier()

              except _Stop:
                b.barrier()
            if do_peer:
                peer_phase(nc, b, ps, xacc, cvec, cmat, wbuf, load_w, wsrc, mm_group, norm_transpose, dr, NT, TS, cvec_eps, pi, fm_pair_gemm, wstate)
                b.barrier()

            for ti in range(NT):
                b.op('sp', lambda e, ti=ti: e.dma_start(out=out_d[tok0 + ti * 128: tok0 + (ti + 1) * 128, :], in_=xacc[:, ti, :]),
                     r=[f'xacc{ti}'], dsem=f'st{ti}')
            b.barrier()
        b.barrier()
    return nc


def peer_phase(nc, b, ps, xacc, cvec, cmat, wbuf, load_w, wsrc, mm_group, norm_transpose, dr, NT, TS, cvec_eps, pi, fm_pair_gemm, wstate):
    ident = cmat[:, 0:128]
    NEG = -1.0e30
    w_query, skT_d, uT_d, ev_d, iota_d, cmat_d = dr["w_query"], dr["skT"], dr["uT"], dr["ev"], dr["iota"], dr["cmat"]
    gsc = dr["gsc"]
    with ExitStack() as ms:
        def msb(name, shape, dt=F32):
            return ms.enter_context(nc.sbuf_tensor(f"{name}_q{pi}", list(shape), dt))
        hnT = msb("hnT", [128, 16, TS], BF16)
        iota_f = msb("iota_f", [128, 128])
        ident_f = msb("ident_f", [128, 128])
        b.op('sp', lambda e: e.dma_start(out=iota_f[:], in_=iota_d), w=['iota_f'], dsem='pl')
        b.op('sp', lambda e: e.dma_start(out=ident_f[:], in_=cmat_d[:, 0:128]), w=['ident_f'], dsem='pl')
        fin = ('pl', b.cnt['pl'])
        b.lastw['iota_f'] = fin
        b.lastw['ident_f'] = fin
        with ExitStack() as m1:
            def sb1(name, shape, dt=F32):
                return m1.enter_context(nc.sbuf_tensor(f"{name}_q{pi}", list(shape), dt))
            tsq = sb1("ptsq", [128, D])
            thb = sb1("pthb", [128, D], BF16)
            stat = sb1("pstat", [128, 2])
            qpT = sb1("qpT", [128, 16, TS], BF16)
            skT = sb1("skT", [128, 16, 128], BF16)
            Gst = sb1("Gst", [128, 128, 128], BF16)
            jhot = sb1("jhot", [128, 4, 128], BF16)
            gih = sb1("gih", [128, 4, 128], BF16)
            jsq = sb1("jsq", [128, 4, 128])
            iota_n = sb1("iota_n", [128, 128])
            b.op('dve', lambda e: e.tensor_scalar(out=iota_n[:], in0=iota_f[:], scalar1=-1.0, scalar2=None, op0=ALU.mult), r=['iota_f'], w=['iota_n'])
            one_t = sb1("one_t", [128, 1])
            b.op('dve', lambda e: e.memset(one_t[:], 1.0), w=['one_t'])
            s12x = [sb1(f"s12_{i}", [128, 256]) for i in range(2)]
            s12bx = [sb1(f"s12b_{i}", [128, 256]) for i in range(2)]
            cand2x = [sb1(f"cand2_{i}", [128, 256]) for i in range(2)]
            vals = sb1("vals", [128, 16, 16])
            idxu = sb1("idxu", [128, 16, 16], U32)
            idxf = sb1("idxf", [128, 16, 16])
            cand = sb1("cand", [128, 8, 256])
            cand2 = sb1("cand2", [128, 256])
            cvals = sb1("cvals", [128, 8, 16])
            cposu = sb1("cposu", [128, 8, 16], U32)
            au = sb1("au", [128, 128], U32)
            bu = sb1("bu", [128, 128], U32)
            af = sb1("af", [128, 128])
            bf_ = sb1("bf_", [128, 128])
            eq = sb1("eq", [128, 128, 16])
            negm = sb1("negm", [128, 8])
            gsum = sb1("gsum", [128, 8])
            Itm = sb1("Itm", [128, 128])
            Jtm = sb1("Jtm", [128, 128])
            Gtm = sb1("Gtm", [128, 128])
            iT = sb1("iT", [128, 128])
            jT = sb1("jT", [128, 128])
            gT = sb1("gT", [128, 128])
            b.op('pool', lambda e: e.dma_start(out=skT[:], in_=skT_d.rearrange("p (c k) -> p c k", c=16)), w=['skT'], dsem='cst')
            for ti in range(NT):
                norm_transpose('h', xacc[:, ti, :], f'xacc{ti}', hnT, ti * 128, CV_GFFN, 'hnT', tsq[:], thb[:], stat, (0, 1))
            def q_epi(idx, c0, cw, ba, bb):
                b.op('act', lambda e: e.activation(out=qpT[:, q_epi.base + idx, c0:c0 + cw], in_=ps[ba][:, 0:cw], func=AF.Copy),
                     r=[f'ps{ba}'], w=['qpT'])
            for blk in range(4):
                slot = load_w([(lambda wb: wb[:, :, 0:512], wsrc(w_query, blk * 512, 512))])
                items = [(slot, [wbuf[slot][:, kc, j * 128:(j + 1) * 128] for kc in range(KC)], None) for j in range(4)]
                q_epi.base = blk * 4
                fm_pair_gemm(hnT, 'hnT', TS, items, q_epi, [(2, 3), (4, 5)])
            for ti in range(NT):
                tc = slice(ti * 128, (ti + 1) * 128)
                for h0 in range(0, 8, 2):
                    chains = []
                    for h in (h0, h0 + 1):
                        q_ = h % 2
                        bk = 2 + q_
                        s12h, s12bh = s12x[q_], s12bx[q_]
                        def fn(e, h=h, bk=bk):
                            e.matmul(ps[bk][:, 0:128], qpT[:, 2 * h, tc], skT[:, 2 * h, :], start=True, stop=True)
                            return e.matmul(ps[bk][:, 128:256], qpT[:, 2 * h + 1, tc], skT[:, 2 * h + 1, :], start=True, stop=True)
                        b.op('pe', fn, r=['qpT', 'skT'], w=[f'ps{bk}'])
                        b.op('act', lambda e, bk=bk, s12h=s12h: e.activation(out=s12h[:], in_=ps[bk][:, 0:256], func=AF.Copy), r=[f'ps{bk}'], w=[f's12_{q_}'])
                        for p in range(2):
                            hp = 2 * h + p
                            sv = s12h[:, p * 128:(p + 1) * 128]
                            sv2 = s12bh[:, p * 128:(p + 1) * 128]
                            vt, it_, st, st2 = f'vals{hp}', f'idxu{hp}', f's12_{q_}', f's12b_{q_}_{p}'
                            chains.append([
                                ('dve', lambda e, hp=hp, sv=sv: e.max(out=vals[:, hp, 0:8], in_=sv), [st], [vt + 'a']),
                                ('dve', lambda e, hp=hp, sv=sv: e.max_index(out=idxu[:, hp, 0:8], in_max=vals[:, hp, 0:8], in_values=sv), [st, vt + 'a'], [it_ + 'a']),
                                ('dve', lambda e, hp=hp, sv=sv, sv2=sv2: e.match_replace(out=sv2, in_to_replace=vals[:, hp, 0:8], in_values=sv, imm_value=NEG), [st, vt + 'a'], [st2]),
                                ('dve', lambda e, hp=hp, sv2=sv2: e.max(out=vals[:, hp, 8:16], in_=sv2), [st2], [vt + 'b']),
                                ('dve', lambda e, hp=hp, sv2=sv2: e.max_index(out=idxu[:, hp, 8:16], in_max=vals[:, hp, 8:16], in_values=sv2), [st2, vt + 'b'], [it_ + 'b']),
                            ])
                    for step in range(5):
                        for ch in chains:
                            eng_, f_, r_, w_ = ch[step]
                            b.op(eng_, f_, r=r_, w=w_)
                    chains = []
                    for h in (h0, h0 + 1):
                        cv = cand[:, h, :]
                        c2 = cand2x[h % 2]
                        vr = [f'vals{2 * h}a', f'vals{2 * h}b', f'vals{2 * h + 1}a', f'vals{2 * h + 1}b']
                        chains.append([
                            ('dve', lambda e, h=h, cv=cv: e.tensor_tensor(out=cv.rearrange("p (a c) -> p a c", a=16),
                                                                          in0=vals[:, 2 * h, :].unsqueeze(2).to_broadcast([128, 16, 16]),
                                                                          in1=vals[:, 2 * h + 1, :].unsqueeze(1).to_broadcast([128, 16, 16]), op=ALU.add), vr, [f'cand{h}']),
                            ('dve', lambda e, h=h, cv=cv: e.max(out=cvals[:, h, 0:8], in_=cv), [f'cand{h}'], [f'cvals{h}a']),
                            ('dve', lambda e, h=h, cv=cv: e.max_index(out=cposu[:, h, 0:8], in_max=cvals[:, h, 0:8], in_values=cv), [f'cand{h}', f'cvals{h}a'], [f'cposu{h}a']),
                            ('dve', lambda e, h=h, cv=cv, c2=c2: e.match_replace(out=c2[:], in_to_replace=cvals[:, h, 0:8], in_values=cv, imm_value=NEG),
                             [f'cand{h}', f'cvals{h}a'], [f'cand2_{h % 2}']),
                            ('dve', lambda e, h=h, c2=c2: e.max(out=cvals[:, h, 8:16], in_=c2[:]), [f'cand2_{h % 2}'], [f'cvals{h}b']),
                            ('dve', lambda e, h=h, c2=c2: e.max_index(out=cposu[:, h, 8:16], in_max=cvals[:, h, 8:16], in_values=c2[:]), [f'cand2_{h % 2}', f'cvals{h}b'], [f'cposu{h}b']),
                        ])
                    for step in range(6):
                        for ch in chains:
                            eng_, f_, r_, w_ = ch[step]
                            b.op(eng_, f_, r=r_, w=w_)
                ALLV = [f'vals{i}{x}' for i in range(16) for x in 'ab']
                ALLI = [f'idxu{i}{x}' for i in range(16) for x in 'ab']
                ALLC = [f'cvals{i}{x}' for i in range(8) for x in 'ab']
                ALLP = [f'cposu{i}{x}' for i in range(8) for x in 'ab']
                b.op('dve', lambda e: e.tensor_copy(out=idxf[:], in_=idxu[:]), r=ALLI, w=['idxf'])
                b.op('dve', lambda e: e.tensor_scalar(out=negm[:], in0=cvals[:, :, 0], scalar1=-1.0, scalar2=None, op0=ALU.mult), r=ALLC, w=['negm'])
                for h in range(8):
                    b.op('act', lambda e, h=h: e.activation(out=Gtm[:, h * 16:(h + 1) * 16], in_=cvals[:, h, :], func=AF.Exp, bias=negm[:, h:h + 1],
                                                            scale=1.0, accum_out=gsum[:, h:h + 1]), r=ALLC + ['negm'], w=['Gtm', 'gsum'])
                b.op('dve', lambda e: e.reciprocal(out=gsum[:], in_=gsum[:]), r=['gsum'], w=['gsum'])
                b.op('dve', lambda e: e.tensor_tensor(out=Gtm[:].rearrange("p (h r) -> p h r", h=8), in0=Gtm[:].rearrange("p (h r) -> p h r", h=8),
                                                      in1=gsum[:].unsqueeze(2).to_broadcast([128, 8, 16]), op=ALU.mult), r=['Gtm', 'gsum'], w=['Gtm'])
                cpu_ = cposu[:].rearrange("p h r -> p (h r)")
                b.op('dve', lambda e: e.tensor_single_scalar(out=au[:], in_=cpu_, scalar=4, op=ALU.logical_shift_right), r=ALLP, w=['au'])
                b.op('dve', lambda e: e.tensor_single_scalar(out=bu[:], in_=cpu_, scalar=15, op=ALU.bitwise_and), r=ALLP, w=['bu'])
                b.op('dve', lambda e: e.tensor_copy(out=af[:], in_=au[:]), r=['au'], w=['af'])
                b.op('dve', lambda e: e.tensor_copy(out=bf_[:], in_=bu[:]), r=['bu'], w=['bf_'])
                for (srcf, half, dst, dtok) in ((af, 0, Itm, 'Itm'), (bf_, 1, Jtm, 'Jtm')):
                    b.op('dve', lambda e, srcf=srcf: e.tensor_tensor(out=eq[:], in0=srcf[:].unsqueeze(2).to_broadcast([128, 128, 16]),
                                                                     in1=iota_f[:, 0:16].unsqueeze(1).to_broadcast([128, 128, 16]), op=ALU.is_equal),
                         r=['af', 'bf_', 'iota_f'], w=['eq'])
                    idx_h = idxf[:].rearrange("p (h t) a -> p h t a", t=2)[:, :, half, :]
                    b.op('dve', lambda e, idx_h=idx_h: e.tensor_tensor(out=eq[:].rearrange("p (h r) a -> p h r a", h=8),
                                                                       in0=eq[:].rearrange("p (h r) a -> p h r a", h=8),
                                                                       in1=idx_h.unsqueeze(2).to_broadcast([128, 8, 16, 16]), op=ALU.mult),
                         r=['eq', 'idxf'], w=['eq'])
                    b.op('dve', lambda e, dst=dst: e.tensor_reduce(out=dst[:], in_=eq[:], axis=AX.X, op=ALU.add), r=['eq'], w=[dtok])
                for (srct, stok, dstt, dtok, bk) in ((Itm, 'Itm', iT, 'iT', 4), (Jtm, 'Jtm', jT, 'jT', 5), (Gtm, 'Gtm', gT, 'gT', 6)):
                    b.op('pe', lambda e, srct=srct, bk=bk: e.transpose(ps[bk][:, 0:128], srct[:], ident_f[:]), r=[stok, 'ident_f'], w=[f'ps{bk}'])
                    b.op('act', lambda e, dstt=dstt, bk=bk, dtok=dtok: e.activation(out=dstt[:], in_=ps[bk][:, 0:128], func=AF.Copy, scale=(-1.0 if dtok == 'jT' else 1.0)),
                         r=[f'ps{bk}'], w=[dtok])
                for q4 in range(32):
                    bk = q4 % 2
                    for u in range(3):
                        t = q4 * 4 + u
                        b.op('act', lambda e, u=u, t=t: e.activation(out=jsq[:, u, :], in_=iota_f[:], func=AF.Square, bias=jT[:, t:t + 1], scale=1.0),
                             r=['iota_f', 'jT'], w=[f'jsq{u}'])
                    for u in range(3):
                        t = q4 * 4 + u
                        b.op('act', lambda e, u=u, t=t: e.activation(out=jhot[:, u, :], in_=jsq[:, u, :], func=AF.Relu, bias=one_t[:, 0:1], scale=-1.0),
                             r=[f'jsq{u}', 'one_t'], w=[f'jhot{u}'])
                    b.op('dve', lambda e, q4=q4: e.tensor_scalar(out=jhot[:, 3, :], in0=iota_n[:], scalar1=jT[:, q4 * 4 + 3:q4 * 4 + 4], scalar2=None, op0=ALU.is_equal),
                         r=['iota_n', 'jT'], w=['jhot3'])
                    for u in range(4):
                        t = q4 * 4 + u
                        b.op('dve', lambda e, u=u, t=t: e.tensor_scalar(out=gih[:, u, :], in0=iota_f[:], scalar1=iT[:, t:t + 1], scalar2=gT[:, t:t + 1],
                                                                        op0=ALU.is_equal, op1=ALU.mult), r=['iota_f', 'iT', 'gT'], w=[f'gih{u}'])
                    def fn(e, bk=bk):
                        inst = None
                        for u in range(4):
                            inst = e.matmul(ps[bk][:, u * 128:(u + 1) * 128], jhot[:, u, :], gih[:, u, :], start=True, stop=True)
                        return inst
                    b.op('pe', fn, r=[f'jhot{u}' for u in range(4)] + [f'gih{u}' for u in range(4)], w=[f'ps{bk}'])
                    b.op('act', lambda e, bk=bk, q4=q4: e.activation(out=Gst[:, :, q4 * 4:(q4 + 1) * 4], in_=ps[bk][:, :].rearrange("p (t c) -> p c t", t=4), func=AF.Copy),
                         r=[f'ps{bk}'], w=[f'Gst{q4}'])
                for c8 in range(8):
                    b.op('sp', lambda e, c8=c8, ti=ti: e.dma_start(out=gsc[c8 * 16:(c8 + 1) * 16, :, ti * 128:(ti + 1) * 128].rearrange("c j t -> j c t"),
                                                                  in_=Gst[:, c8 * 16:(c8 + 1) * 16, :]), r=[f'Gst{q}' for q in range(32)], w=['gsc'], dsem='gsp')
            b.barrier()
        with ExitStack() as m2:
            def sb2(name, shape, dt=F32):
                return m2.enter_context(nc.sbuf_tensor(f"{name}_q{pi}", list(shape), dt))
            wb2 = [sb2(f"wbx{i}", [128, 16, 512], BF16) for i in range(2)]
            gTg = [sb2(f"gTg{i}", [128, 4, TS], BF16) for i in range(2)]
            actT = [sb2(f"actT{i}", [128, 4, TS], BF16) for i in range(2)]
            gel = [sb2(f"gel{i}", [128, 512], BF16) for i in range(2)]
            ob = 0
            for gi in range(32):
                par = gi % 2
                b.op('pool', lambda e, gi=gi, par=par: e.dma_start(out=wbuf[par][:], in_=wsrc(uT_d, gi * 512, 512)), w=[f'wbuf{par}'], dsem=f'w{par}')
                b.op('pool', lambda e, gi=gi, par=par: e.dma_start(out=wb2[par][:].rearrange("p a n -> p (a n)").rearrange("p (c n) -> p c n", c=4),
                                                                   in_=ev_d[gi * 512:(gi + 1) * 512, :].rearrange("(c p) n -> p c n", p=128)),
                     w=[f'wbx{par}'], dsem=f'wx{par}')
                b.op('sp', lambda e, gi=gi, par=par: e.dma_start(out=gTg[par][:], in_=gsc[gi * 4:(gi + 1) * 4, :, :].rearrange("c j t -> j c t")),
                     r=['gsc'], w=[f'gTg{par}'], dsem=f'gl{par}')
                vview = wb2[par][:].rearrange("p a n -> p (a n)").rearrange("p (c n) -> p c n", c=4)
                for cc in range(4):
                    for (c0, cw) in blocks_of(TS):
                        ab = cc % 2
                        mm_group(ps[ab][:, 0:cw], [(wbuf[par][:, kc, cc * 128:(cc + 1) * 128], hnT[:, kc, c0:c0 + cw]) for kc in range(KC)],
                                 ['hnT', f'wbuf{par}'], f'ps{ab}')
                        b.op('act', lambda e, ab=ab, cw=cw: e.activation(out=gel[ab][:, 0:cw], in_=ps[ab][:, 0:cw], func=AF.Gelu), r=[f'ps{ab}'], w=[f'gel{ab}'])
                        b.op('dve', lambda e, ab=ab, cc=cc, c0=c0, cw=cw, par=par: e.tensor_tensor(out=actT[par][:, cc, c0:c0 + cw], in0=gel[ab][:, 0:cw],
                                                                                                   in1=gTg[par][:, cc, c0:c0 + cw], op=ALU.mult),
                             r=[f'gel{ab}', f'gTg{par}'], w=[f'actT{par}'])
                for ti in range(NT):
                    for db in range(4):
                        bk = 2 + (ob % 4)
                        ob += 1
                        mm_group(ps[bk][:, :], [(actT[par][:, cc, ti * 128:(ti + 1) * 128], vview[:, cc, db * 512:(db + 1) * 512]) for cc in range(4)],
                                 [f'actT{par}', f'wbx{par}'], f'ps{bk}')
                        b.op('dve', lambda e, ti=ti, db=db, bk=bk: e.tensor_tensor(out=xacc[:, ti, db * 512:(db + 1) * 512], in0=xacc[:, ti, db * 512:(db + 1) * 512],
                                                                                   in1=ps[bk][:, :], op=ALU.add), r=[f'ps{bk}', f'xacc{ti}'], w=[f'xacc{ti}'])
            b.barrier()


def host_prep(inp, NPASS, NT, do_peer=True):
    TS = NT * 128
    TTOT = NPASS * TS
    f = lambda a: np.ascontiguousarray(np.asarray(a, dtype=np.float32))
    x = f(inp["x"])[0]
    pos = np.asarray(inp["positions"])[0].astype(np.int32)
    w_in = f(inp["w_in"][0])
    def swap_cols(w, nh):
        w4 = w.reshape(w.shape[0], nh, 2, 32)
        return np.ascontiguousarray(w4[:, :, ::-1, :].reshape(w.shape[0], nh * 64))
    wq = w_in[:, 0:1024]
    wk = w_in[:, 1024:1280]
    wk_sw = swap_cols(wk, 4)
    def dup(w):
        w3 = w.reshape(w.shape[0], 4, 1, 64)
        return np.ascontiguousarray(np.repeat(w3, 2, axis=2).reshape(w.shape[0], 512))
    wqk_sw = np.ascontiguousarray(np.concatenate([swap_cols(wq, 16), dup(wk_sw)], axis=1))
    wkdup = dup(wk)
    fm = lambda v, c: np.ascontiguousarray(f(v).reshape(c, 128).T)
    cvec = np.zeros((128, NCV), np.float32)
    cvec[:, CV_GMIX:CV_GMIX + 16] = fm(inp["g_mix"][0], 16)
    cvec[:, CV_GFFN:CV_GFFN + 16] = fm(inp["g_ffn"][0], 16)
    cvec[:, CV_GMEM:CV_GMEM + 16] = fm(inp["g_mem"][0], 16)
    p = np.arange(128)
    gq = f(inp["q_norm_g"][0]); gk = f(inp["k_norm_g"][0])
    cvec[:, CV_GQ] = gq[p % 64]; cvec[:, CV_GQ + 1] = gq[(p % 64 + 32) % 64]
    cvec[:, CV_GK] = gk[p % 64]; cvec[:, CV_GK + 1] = gk[(p % 64 + 32) % 64]
    cvec[:, CV_MQG] = f(inp["mq_norm_g"][0]); cvec[:, CV_MKG] = f(inp["mk_norm_g"][0])
    sinks = f(inp["attn_sinks"][0])
    order = []
    for g in range(4):
        order += [4 * g, 4 * g + 2, 4 * g + 1, 4 * g + 3]
    cvec[:, CV_SINK:CV_SINK + 16] = sinks[order][None, :]
    dw = f(inp["conv_dw_w"][0])[:, 0, :]
    for c in range(4):
        cvec[:, CV_DW + c * 31:CV_DW + (c + 1) * 31] = dw[:, c * 128:(c + 1) * 128].T
    cvec[:, CV_DWB:CV_DWB + 4] = fm(inp["conv_dw_b"][0], 4)
    cvec[:, CV_LNG:CV_LNG + 4] = fm(inp["conv_ln_g"][0], 4)
    cvec[:, CV_LNB:CV_LNB + 4] = fm(inp["conv_ln_b"][0], 4)
    inv_freq = (10000.0 ** (-np.arange(0, 64, 2, dtype=np.float32) / 64)).astype(np.float32)
    cvec[:, CV_INVF] = inv_freq[p % 32]
    cvec[:, CV_SGN] = np.where((p % 64) < 32, -1.0, 1.0)
    cmat = np.zeros((128, 384), np.float32)
    cmat[:, 0:128] = np.eye(128)
    cmat[:, 128:256] = 1.0
    cmat[0:64, 256:320] = 1.0
    cmat[64:128, 320:384] = 1.0
    kk = np.arange(128)[:, None]; qq = np.arange(128)[None, :]
    mprev = (kk > qq).astype(np.float32); mcur = (kk <= qq).astype(np.float32)
    def slabs(w, n):
        return w[:, n * 128:(n + 1) * 128].reshape(-1, 128, 128)
    wa, wc, wm = f(inp["w_attn_o"][0]), f(inp["w_conv_o"][0]), f(inp["w_mem_o"][0])
    wmerge = np.empty((16, 128, 64, 128), np.float32)
    for n in range(16):
        parts = [slabs(w_in[:, 3072:5120], n), slabs(wa, n), slabs(w_in[:, 5120:7168], n), slabs(wc, n),
                 slabs(w_in[:, 7168:9216], n), slabs(wm, n)]
        wmerge[n] = np.concatenate(parts, axis=0).transpose(1, 0, 2)
    wmerge = wmerge.reshape(16, 128, 8192)
    common = dict(w_in=np.ascontiguousarray(w_in[:, 0:3072]), wqk_sw=wqk_sw, wkdup=wkdup, wmerge=wmerge,
                  w_out=f(inp["w_out"][0]), w_mem_kv=f(inp["w_mem_kv"][0]),
                  mem=f(inp["mem"][0]), cvec=cvec, cmat=cmat)
    if do_peer:
        common["w_query"] = f(inp["w_query"][0])
        sk = f(inp["sub_keys"][0]).reshape(16, 128, 128)
        common["skT"] = np.ascontiguousarray(sk.transpose(2, 0, 1).reshape(128, 16 * 128))
        common["uT"] = np.ascontiguousarray(f(inp["expert_u"][0]).T)
        common["ev"] = f(inp["expert_v"][0])
        common["iota"] = np.ascontiguousarray(np.tile(np.arange(128, dtype=np.float32)[None, :], (128, 1)))
    in_maps = []
    for c in range(NCORES):
        s0 = c * TTOT
        if c == 0:
            xhc = np.concatenate([np.zeros((128, D), np.float32), x[0:TTOT]], axis=0)
            posc = np.concatenate([np.zeros(128, np.int32), pos[0:TTOT]])
            mfirst = np.zeros_like(mprev)
        else:
            xhc = x[s0 - 128:s0 + TTOT]
            posc = pos[s0 - 128:s0 + TTOT]
            mfirst = mprev
        m = dict(common)
        m["xh"] = np.ascontiguousarray(xhc)
        m["posb"] = np.ascontiguousarray(posc[None, :])
        m["masks"] = np.ascontiguousarray(np.concatenate([np.tile(mprev, (1, 4)), np.tile(mcur, (1, 4)), np.tile(mfirst, (1, 4))], axis=1))
        in_maps.append(m)
    return in_maps


_CACHE = {}


def run(inp, NPASS, NT, do_peer=True, trace=False):
    key = (NPASS, NT, do_peer)
    if key not in _CACHE:
        _CACHE[key] = build_program(NPASS, NT, do_peer)
    nc = _CACHE[key]
    in_maps = host_prep(inp, NPASS, NT, do_peer)
    res = run_bass_kernel_spmd(nc, in_maps, core_ids=list(range(NCORES)), **({"trace": True} if trace else {}))
    out = np.concatenate([r["out"] for r in res.results], axis=0)
    return out[None].astype(np.float32), res


def kernel(**inputs):
    out, _ = run(inputs, 4, 4, True)
    return out
```
